# Optimizing a Trainium2 kernel written in Bass

```python
import math
import jax, jax.numpy as jnp
from jax import lax
import numpy as np

D_MODEL = 1024
BATCH = 16
SEQ = 2048
DEPTH = 1

MEM_LEN = 256
D_S5 = D_MODEL
S5_CH_PER_GROUP = 16
S5_GROUPS = D_S5 // S5_CH_PER_GROUP
S5_STATE = 64
D_M2 = 2 * D_MODEL
M2_HEADDIM = 64
M2_HEADS = D_M2 // M2_HEADDIM
M2_GROUPS = 4
M2_STATE = 128
M2_CONV = 4
M2_CHUNK = 128
D_XBC = D_M2 + 2 * M2_GROUPS * M2_STATE
XA_HEADS = 4
XA_HEADDIM = 128
D_XA = XA_HEADS * XA_HEADDIM
N_BRANCH = 3
D_FF = 2816
FFN_CONV = 3
D_IN = D_S5 + D_M2 + D_XBC + M2_HEADS + D_XA + N_BRANCH * D_MODEL
EPS = 1e-6

kernel_name = "hybrid_s5_ssd_xattn_gated_merge_convffn"


def rmsnorm(x, g):
    xf = x.astype(jnp.float32)
    inv = lax.rsqrt(jnp.mean(xf * xf, axis=-1, keepdims=True) + EPS)
    return (xf * inv).astype(x.dtype) * g


def causal_dwconv(u, w, b):
    K, C = w.shape
    y = lax.conv_general_dilated(
        u, w[:, None, :].astype(u.dtype), window_strides=(1,), padding=[(K - 1, 0)],
        dimension_numbers=('NWC', 'WIO', 'NWC'), feature_group_count=C)
    return y + b


def s5_mixer(u, lam_re, lam_im, log_dt, b_re, b_im, c_re, c_im, d):
    bsz, L, _ = u.shape
    ug = u.reshape(bsz, L, S5_GROUPS, S5_CH_PER_GROUP)
    dt = jnp.exp(log_dt)[:, None]
    mag = jnp.exp(lam_re * dt)
    ar = mag * jnp.cos(lam_im * dt)
    ai = mag * jnp.sin(lam_im * dt)
    den = lam_re * lam_re + lam_im * lam_im
    fr = ((ar - 1.0) * lam_re + ai * lam_im) / den
    fi = (ai * lam_re - (ar - 1.0) * lam_im) / den
    bbr = fr[..., None] * b_re - fi[..., None] * b_im
    bbi = fr[..., None] * b_im + fi[..., None] * b_re
    bu_re = jnp.einsum('blgk,gpk->blgp', ug, bbr)
    bu_im = jnp.einsum('blgk,gpk->blgp', ug, bbi)
    a_re = jnp.broadcast_to(ar, (1, L) + ar.shape)
    a_im = jnp.broadcast_to(ai, (1, L) + ai.shape)

    def combine(e1, e2):
        a1r, a1i, b1r, b1i = e1
        a2r, a2i, b2r, b2i = e2
        return (a2r * a1r - a2i * a1i,
                a2r * a1i + a2i * a1r,
                a2r * b1r - a2i * b1i + b2r,
                a2r * b1i + a2i * b1r + b2i)

    _, _, xr, xi = lax.associative_scan(combine, (a_re, a_im, bu_re, bu_im), axis=1)
    y = jnp.einsum('gkp,blgp->blgk', c_re, xr) - jnp.einsum('gkp,blgp->blgk', c_im, xi)
    return y.reshape(bsz, L, D_S5) + d * u


def ssd_chunked(xh, dt, A, Bm, Cm):
    bsz, L, H, P = xh.shape
    G, N = Bm.shape[-2:]
    R = H // G
    nc = L // M2_CHUNK
    x = xh.reshape(bsz, nc, M2_CHUNK, G, R, P)
    dtc = dt.reshape(bsz, nc, M2_CHUNK, G, R)
    dA = (dtc * A.reshape(G, R)).astype(jnp.float32)
    Bc = Bm.reshape(bsz, nc, M2_CHUNK, G, N)
    Cc = Cm.reshape(bsz, nc, M2_CHUNK, G, N)
    a_cum = jnp.cumsum(dA, axis=2)
    xdt = x * dtc[..., None]
    seg = a_cum[:, :, :, None] - a_cum[:, :, None, :]
    causal = jnp.tril(jnp.ones((M2_CHUNK, M2_CHUNK), dtype=bool))
    lmat = jnp.exp(jnp.where(causal[:, :, None, None], seg, -jnp.inf))
    cb = jnp.einsum('bclgn,bcsgn->bclsg', Cc, Bc)
    y_diag = jnp.einsum('bclsg,bclsgr,bcsgrp->bclgrp', cb, lmat, xdt)
    decay_s = jnp.exp(a_cum[:, :, -1:] - a_cum)
    states = jnp.einsum('bclgn,bclgr,bclgrp->bcgrpn', Bc, decay_s, xdt)
    chunk_decay = jnp.exp(a_cum[:, :, -1])

    def step(h, inp):
        s, dec = inp
        return h * dec[..., None, None] + s, h

    h0 = jnp.zeros((bsz, G, R, P, N), dtype=states.dtype)
    _, states_in = lax.scan(step, h0, (jnp.moveaxis(states, 1, 0), jnp.moveaxis(chunk_decay, 1, 0)))
    states_in = jnp.moveaxis(states_in, 0, 1)
    y_off = jnp.einsum('bclgn,bcgrpn,bclgr->bclgrp', Cc, states_in, jnp.exp(a_cum))
    return (y_diag + y_off).reshape(bsz, L, H * P)


def mamba2_mixer(z, xbc, dt_raw, conv_w, conv_b, dt_bias, a_log, d, norm_w):
    bsz, L, _ = z.shape
    xbc = jax.nn.silu(causal_dwconv(xbc, conv_w, conv_b))
    xs = xbc[..., :D_M2]
    Bm = xbc[..., D_M2:D_M2 + M2_GROUPS * M2_STATE].reshape(bsz, L, M2_GROUPS, M2_STATE)
    Cm = xbc[..., D_M2 + M2_GROUPS * M2_STATE:].reshape(bsz, L, M2_GROUPS, M2_STATE)
    dt = jax.nn.softplus((dt_raw + dt_bias).astype(jnp.float32))
    A = -jnp.exp(a_log.astype(jnp.float32))
    xh = xs.reshape(bsz, L, M2_HEADS, M2_HEADDIM)
    y = ssd_chunked(xh, dt, A, Bm, Cm)
    y = y + (xh * d[:, None]).reshape(bsz, L, D_M2)
    return rmsnorm(y * jax.nn.silu(z), norm_w)


def memory_attention(q_flat, mem, norm_mem, w_kv):
    bsz, L, _ = q_flat.shape
    mem_n = rmsnorm(mem, norm_mem)
    kv = mem_n @ w_kv
    k = kv[..., :D_XA].reshape(bsz, MEM_LEN, XA_HEADS, XA_HEADDIM)
    v = kv[..., D_XA:].reshape(bsz, MEM_LEN, XA_HEADS, XA_HEADDIM)
    q = q_flat.reshape(bsz, L, XA_HEADS, XA_HEADDIM)
    s = jnp.einsum('bqhd,bkhd->bhqk', q, k) * (XA_HEADDIM ** -0.5)
    p = jax.nn.softmax(s.astype(jnp.float32), axis=-1).astype(v.dtype)
    return jnp.einsum('bhqk,bkhd->bqhd', p, v).reshape(bsz, L, D_XA)


def setup_inputs(seed: int = 0) -> dict:
    key = jax.random.key(seed)
    ks = jax.random.split(key, 32)
    f32 = jnp.float32

    def nrm(k, shape, scale):
        return jax.random.normal(k, shape, f32) * scale

    def gain(k, n):
        return 1.0 + 0.02 * jax.random.normal(k, (DEPTH, n), f32)

    log_lo, log_hi = math.log(1e-3), math.log(1e-1)
    m2_dt = jnp.exp(jax.random.uniform(ks[14], (DEPTH, M2_HEADS), f32, log_lo, log_hi))
    return {
        "x": nrm(ks[0], (BATCH, SEQ, D_MODEL), 1.0),
        "mem": nrm(ks[1], (BATCH, MEM_LEN, D_MODEL), 1.0),
        "norm_mix": gain(ks[2], D_MODEL),
        "w_in": nrm(ks[3], (DEPTH, D_MODEL, D_IN), D_MODEL ** -0.5),
        "s5_lambda_re": -0.5 + 0.01 * jax.random.normal(ks[4], (DEPTH, S5_GROUPS, S5_STATE), f32),
        "s5_lambda_im": jnp.broadcast_to(jnp.pi * jnp.arange(S5_STATE, dtype=f32), (DEPTH, S5_GROUPS, S5_STATE)),
        "s5_log_dt": jax.random.uniform(ks[5], (DEPTH, S5_GROUPS), f32, log_lo, log_hi),
        "s5_b_re": nrm(ks[6], (DEPTH, S5_GROUPS, S5_STATE, S5_CH_PER_GROUP), (2 * S5_CH_PER_GROUP) ** -0.5),
        "s5_b_im": nrm(ks[7], (DEPTH, S5_GROUPS, S5_STATE, S5_CH_PER_GROUP), (2 * S5_CH_PER_GROUP) ** -0.5),
        "s5_c_re": nrm(ks[8], (DEPTH, S5_GROUPS, S5_CH_PER_GROUP, S5_STATE), S5_STATE ** -0.5),
        "s5_c_im": nrm(ks[9], (DEPTH, S5_GROUPS, S5_CH_PER_GROUP, S5_STATE), S5_STATE ** -0.5),
        "s5_d": nrm(ks[10], (DEPTH, D_S5), 1.0),
        "w_a_val": nrm(ks[11], (DEPTH, D_S5, D_MODEL), D_S5 ** -0.5),
        "w_a_gate": nrm(ks[12], (DEPTH, D_S5, D_MODEL), D_S5 ** -0.5),
        "m2_conv_w": nrm(ks[13], (DEPTH, M2_CONV, D_XBC), M2_CONV ** -0.5),
        "m2_conv_b": nrm(ks[15], (DEPTH, D_XBC), 0.02),
        "m2_dt_bias": m2_dt + jnp.log(-jnp.expm1(-m2_dt)),
        "m2_a_log": jnp.log(jax.random.uniform(ks[16], (DEPTH, M2_HEADS), f32, 1.0, 16.0)),
        "m2_d": gain(ks[17], M2_HEADS),
        "m2_norm": gain(ks[18], D_M2),
        "w_b": nrm(ks[19], (DEPTH, D_M2, D_MODEL), D_M2 ** -0.5),
        "norm_mem": gain(ks[20], D_MODEL),
        "w_kv": nrm(ks[21], (DEPTH, D_MODEL, 2 * D_XA), D_MODEL ** -0.5),
        "w_c": nrm(ks[22], (DEPTH, D_XA, D_MODEL), D_XA ** -0.5),
        "w_out": nrm(ks[23], (DEPTH, D_MODEL, D_MODEL), D_MODEL ** -0.5),
        "norm_ffn": gain(ks[24], D_MODEL),
        "w_up": nrm(ks[25], (DEPTH, D_MODEL, 2 * D_FF), D_MODEL ** -0.5),
        "ffn_conv_w": nrm(ks[26], (DEPTH, FFN_CONV, 2 * D_FF), FFN_CONV ** -0.5),
        "ffn_conv_b": nrm(ks[27], (DEPTH, 2 * D_FF), 0.02),
        "w_down": nrm(ks[28], (DEPTH, D_FF, D_MODEL), D_FF ** -0.5),
        "norm_final": 1.0 + 0.02 * jax.random.normal(ks[29], (D_MODEL,), f32),
    }


def reference(x, mem, norm_mix, w_in, s5_lambda_re, s5_lambda_im, s5_log_dt, s5_b_re, s5_b_im,
              s5_c_re, s5_c_im, s5_d, w_a_val, w_a_gate, m2_conv_w, m2_conv_b, m2_dt_bias,
              m2_a_log, m2_d, m2_norm, w_b, norm_mem, w_kv, w_c, w_out, norm_ffn, w_up,
              ffn_conv_w, ffn_conv_b, w_down, norm_final):
    p1 = D_S5
    p2 = p1 + D_M2
    p3 = p2 + D_XBC
    p4 = p3 + M2_HEADS
    p5 = p4 + D_XA
    for l in range(DEPTH):
        h = rmsnorm(x, norm_mix[l])
        proj = h @ w_in[l]
        u_s5 = proj[..., :p1]
        z = proj[..., p1:p2]
        xbc = proj[..., p2:p3]
        dt_raw = proj[..., p3:p4]
        q = proj[..., p4:p5]
        gates = jax.nn.sigmoid(proj[..., p5:])

        y_s5 = s5_mixer(u_s5, s5_lambda_re[l], s5_lambda_im[l], s5_log_dt[l], s5_b_re[l],
                        s5_b_im[l], s5_c_re[l], s5_c_im[l], s5_d[l])
        g_s5 = jax.nn.gelu(y_s5)
        y_a = (g_s5 @ w_a_val[l]) * jax.nn.sigmoid(g_s5 @ w_a_gate[l])
        y_b = mamba2_mixer(z, xbc, dt_raw, m2_conv_w[l], m2_conv_b[l], m2_dt_bias[l],
                           m2_a_log[l], m2_d[l], m2_norm[l]) @ w_b[l]
        y_c = memory_attention(q, mem, norm_mem[l], w_kv[l]) @ w_c[l]

        g_a = gates[..., :D_MODEL]
        g_b = gates[..., D_MODEL:2 * D_MODEL]
        g_c = gates[..., 2 * D_MODEL:]
        x = x + (g_a * y_a + g_b * y_b + g_c * y_c) @ w_out[l]

        h = rmsnorm(x, norm_ffn[l])
        up = causal_dwconv(h @ w_up[l], ffn_conv_w[l], ffn_conv_b[l])
        x = x + (jax.nn.silu(up[..., :D_FF]) * up[..., D_FF:]) @ w_down[l]
    return rmsnorm(x, norm_final)
```

```python
import contextlib
import numpy as np
import concourse.bass as bass
import concourse.mybir as mybir
from concourse.bass_utils import run_bass_kernel_spmd

F32 = mybir.dt.float32
BF16 = mybir.dt.bfloat16
I32 = mybir.dt.int32
AF = mybir.ActivationFunctionType
ALU = mybir.AluOpType

D = 1024
NT = 512
MEM = 256
D_FF = 2816
EPS = 1e-6
P1, P2, P3, P4, P5 = 1024, 3072, 6144, 6176, 6688
D_IN = 9760


class TK:
    def __init__(self, nc, es):
        self.nc = nc
        self.es = es
        self.engs = {"pe": nc.tensor, "act": nc.scalar, "dve": nc.vector, "pool": nc.gpsimd, "sp": nc.sync}
        self.sems = {}
        self.cnt = {}
        for e in self.engs:
            self.sems[e] = es.enter_context(nc.semaphore("s_" + e))
            self.cnt[e] = 0
        self.seen = {e: {} for e in self.engs}
        self.recs = {}
        self.dq = {}
        for q, n in (("sp", 12), ("pool", 8), ("act", 4)):
            lst = []
            for i in range(n):
                key = "d_%s%d" % (q, i)
                self.sems[key] = es.enter_context(nc.semaphore(key))
                self.cnt[key] = 0
                lst.append(key)
            self.dq[q] = [lst, 0]
        self.final_waits = []

    @staticmethod
    def _acc(ap):
        name = ap.name
        sp = str(ap.space)
        pairs = ap.ap
        off = ap.offset
        if "DRAM" in sp:
            ext = 0
            for st, c in pairs:
                ext += abs(st) * (c - 1)
            return (name, 0, 1, off, off + ext + 1)
        pst, pc = pairs[0]
        if pst == 0:
            pst = 1 << 40
        p0 = off // pst
        f0 = off % pst
        ext = 0
        for st, c in pairs[1:]:
            ext += abs(st) * (c - 1)
        esz = 4 if ap.dtype in (F32, I32) else 2
        if "PSUM" in sp:
            b0 = (f0 * esz) // 2048
            b1 = ((f0 + ext) * esz) // 2048
            return (name, p0, p0 + pc, b0 * 2048, (b1 + 1) * 2048)
        return (name, p0, p0 + pc, f0 * esz, (f0 + ext + 1) * esz)

    @staticmethod
    def _accs(ap):
        base = TK._acc(ap)
        sp = str(ap.space)
        if "DRAM" in sp or "PSUM" in sp:
            return [base]
        pairs = ap.ap
        if len(pairs) < 3:
            return [base]
        st0, c0 = pairs[1]
        if c0 <= 1 or c0 > 64 or st0 <= 0:
            return [base]
        rest = 0
        for st, c in pairs[2:]:
            if st < 0:
                return [base]
            rest += st * (c - 1)
        rest += 1
        if st0 <= rest:
            return [base]
        name, p0, p1, f0b, _ = base
        esz = 4 if ap.dtype in (F32, I32) else 2
        return [(name, p0, p1, f0b + i * st0 * esz, f0b + (i * st0 + rest) * esz) for i in range(c0)]

    def _deps(self, e, reads, writes):
        need = {}
        for (acc, isw) in [(a, False) for a in reads] + [(a, True) for a in writes]:
            name, p0, p1, f0, f1 = acc
            lst = self.recs.get(name)
            if not lst:
                continue
            psum = name.startswith("ps")
            for r in lst:
                if not (isw or r[6]):
                    if not (psum and r[4] != e):
                        continue
                if r[0] >= p1 or r[1] <= p0 or r[2] >= f1 or r[3] <= f0:
                    continue
                if e == "pe" and r[4] == "pe":
                    continue
                k, v = r[4], r[5]
                if need.get(k, 0) < v:
                    need[k] = v
        return need

    def _emit_waits(self, e, need):
        seen = self.seen[e]
        eng = self.engs[e]
        for k, v in need.items():
            if seen.get(k, 0) < v:
                eng.wait_ge(self.sems[k], v)
                seen[k] = v

    def _record(self, reads, writes, key, val):
        for acc in writes:
            name, p0, p1, f0, f1 = acc
            lst = self.recs.setdefault(name, [])
            lst[:] = [r for r in lst if not (r[0] >= p0 and r[1] <= p1 and r[2] >= f0 and r[3] <= f1)]
            lst.append([p0, p1, f0, f1, key, val, True])
        for acc in reads:
            name, p0, p1, f0, f1 = acc
            lst = self.recs.setdefault(name, [])
            for r in lst:
                if (not r[6]) and r[4] == key and r[0] == p0 and r[1] == p1 and r[2] == f0 and r[3] == f1:
                    r[5] = val
                    break
            else:
                lst.append([p0, p1, f0, f1, key, val, False])

    def op(self, e, fn, **kw):
        reads, writes = [], []
        for k, v in kw.items():
            if hasattr(v, "ap") and hasattr(v, "space"):
                if k in ("out", "accum_out", "ap"):
                    writes.extend(self._accs(v))
                else:
                    reads.extend(self._accs(v))
        need = self._deps(e, reads, writes)
        self._emit_waits(e, need)
        inst = getattr(self.engs[e], fn)(**kw)
        self.cnt[e] += 1
        inst.then_inc(self.sems[e], 1)
        self._record(reads, writes, e, self.cnt[e])
        return inst

    def dma(self, q, out, in_, final=False):
        reads = self._accs(in_)
        writes = self._accs(out)
        need = self._deps(q, reads, writes)
        lst, idx = self.dq[q]
        key = lst[idx % len(lst)]
        self.dq[q][1] = idx + 1
        if self.cnt[key] > 0:
            need[key] = max(need.get(key, 0), self.cnt[key])
        self._emit_waits(q, need)
        inst = self.engs[q].dma_start(out=out, in_=in_)
        self.cnt[key] += 16
        inst.then_inc(self.sems[key], 16)
        self._record(reads, writes, key, self.cnt[key])
        if final:
            self.final_waits.append((q, key, self.cnt[key]))

    def finish(self):
        for q, key, v in self.final_waits:
            self._emit_waits(q, {key: v})
        for e in self.engs:
            need = {f: self.cnt[f] for f in ("pe", "act", "dve", "pool", "sp") if self.cnt[f] > 0}
            self._emit_waits(e, need)


class Cfg:
    def __init__(self, seq=2048, nseq=2, s5=True, ssd=True, xa=True, ffn=True, nt=512):
        self.seq = seq
        self.nseq = nseq
        self.s5 = s5
        self.ssd = ssd
        self.xa = xa
        self.ffn = ffn
        self.nt = nt


def pc_layout():
    off = {}
    o = 0
    for name, n in (("g1", 8), ("g2", 8), ("gf", 8), ("gmem", 8), ("fcw", 132), ("fcb", 44), ("m2cw", 96),
                    ("m2cb", 24), ("m2norm", 16), ("dcol", 16), ("dtb", 32), ("alog", 32), ("eps", 1),
                    ("lre", 64), ("lim", 64), ("ldt", 64), ("s5d", 64), ("ph1", 1), ("ph2", 1), ("psi", 1),
                    ("nv", 32), ("bmul", 64)):
        off[name] = o
        o += n
    off["_n"] = o
    return off


PCO = pc_layout()
NPC = PCO["_n"]
CM_ID, CM_U, CM_NM, CM_PERM, CM_BM, CM_Z = 0, 128, 256, 384, 512, 640
CM_L2 = 640 + 8 * 256
NCM = CM_L2 + 128
SLOT = 4096
NSLOT = 4


def _wblocks(cfg):
    blocks = []

    def add(wname, K, c0, ncols, ncb, tag):
        nk = K // 128
        for i in range(0, ncols, ncb):
            blocks.append((wname, nk, c0 + i, min(ncb, ncols - i), tag))

    if cfg.s5:
        add("w_in", D, 0, 1024, 512, "u")
        for i in range(4):
            blocks.append(("s5st", 32, i, 128, "st"))
    if cfg.xa:
        add("w_in", D, P4, 512, 512, "q")
        add("w_c", 512, 0, 1024, 1024, "wc")
        add("w_in", D, P5 + 2048, 1024, 512, "gc")
    if cfg.s5:
        for i in range(4):
            blocks.append(("s5to", 32, i, 128, "to"))
        add("w_a_val", D, 0, 1024, 512, "av")
        add("w_a_gate", D, 0, 1024, 512, "ag")
        add("w_in", D, P5, 1024, 512, "ga")
    if cfg.ssd:
        add("w_in", D, P3, 32, 32, "dt")
        add("w_in", D, P1, 2048, 512, "z")
        add("w_in", D, P2, 3072, 512, "xbc")
        add("w_in", D, P5 + 1024, 1024, 512, "gb")
        add("w_b", 2048, 0, 1024, 256, "wb")
    if cfg.s5 or cfg.xa or cfg.ssd:
        add("w_out", D, 0, 1024, 512, "wo")
    if cfg.ffn:
        add("w_up", D, 0, 2 * D_FF, 512, "up")
        add("w_down", D_FF, 0, 1024, 128, "dn")
    return blocks


def build(cfg):
    nc = bass.Bass("TRN2", target_bir_lowering=False)
    es = contextlib.ExitStack()
    SEQ, NSEQ, NT = cfg.seq, cfg.nseq, cfg.nt
    TPS = SEQ // NT
    NTILES = TPS * NSEQ
    NCH = NT // 128
    NB = NT // 8
    TWO_PI = float(2 * np.pi)

    def din(name, shape, dt=F32):
        return nc.dram_tensor(name, list(shape), dt, kind="ExternalInput").ap()

    x_d = din("x", [NSEQ, SEQ, D])
    mem_d = din("mem", [NSEQ, MEM, D])
    w_d = {
        "w_in": din("w_in", [D, D_IN]),
        "w_a_val": din("w_a_val", [D, D]),
        "w_a_gate": din("w_a_gate", [D, D]),
        "w_b": din("w_b", [2048, D]),
        "w_kv": din("w_kv", [D, 1024]),
        "w_c": din("w_c", [512, D]),
        "w_out": din("w_out", [D, D]),
        "w_up": din("w_up", [D, 2 * D_FF]),
        "w_down": din("w_down", [D_FF, D]),
    }
    pc_d = din("pcols", [128, NPC])
    cm_d = din("cmat", [128, NCM])
    s5p_d = din("s5p", [128, 4, 1024])
    out_d = nc.dram_tensor("out", [NSEQ, SEQ, D], F32, kind="ExternalOutput").ap()

    blocks = _wblocks(cfg)
    NBK = len(blocks)
    wscr = nc.dram_tensor("wscr", [NBK, 128, SLOT], BF16, kind="Internal").ap()

    with es:
        tk = TK(nc, es)

        def sb(name, shape, dt=F32):
            return es.enter_context(nc.sbuf_tensor("S_" + name, list(shape), dt))

        banks = [es.enter_context(nc.psum_tensor("ps%d" % i, [128, 512], F32)) for i in range(8)]
        bank_i = [0]

        def bank():
            b = banks[bank_i[0] % 8]
            bank_i[0] += 1
            return b

        def V(fn, **kw):
            return tk.op("dve", fn, **kw)

        def G(fn, **kw):
            return tk.op("pool", fn, **kw)

        def A(**kw):
            return tk.op("act", "activation", **kw)

        def MM(**kw):
            return tk.op("pe", "matmul", **kw)

        def TR(**kw):
            return tk.op("pe", "transpose", **kw)

        ARENA = 96 * 1024
        arena = sb("arena", [128, ARENA // 2], BF16)
        ar_off = [0]

        def ar_reset(o=0):
            ar_off[0] = o

        def ar(shape, dt=F32):
            esz = 4 if dt in (F32, I32) else 2
            n = int(np.prod(shape))
            o = (ar_off[0] + 63) // 64 * 64
            assert o + n * esz <= ARENA, ("arena overflow", o, n * esz)
            ar_off[0] = o + n * esz
            v = arena[:, o // 2:o // 2 + n * esz // 2]
            if dt != BF16:
                v = v.bitcast(dt)
            if len(shape) == 2:
                return v.rearrange("p (a b) -> p a b", a=shape[0])
            if len(shape) == 3:
                return v.rearrange("p (a b c) -> p a b c", a=shape[0], b=shape[1])
            return v

        pcols = sb("pcols", [128, NPC])
        ident_f = sb("ident_f", [128, 128])
        bm_f = sb("bm_f", [128, 128])
        cb = sb("cb", [128, 4 * 128 + 512 + 8 * 256 + 128], BF16)
        ident_b, ones_b, U_b, perm_b = cb[:, 0:128], cb[:, 128:256], cb[:, 256:384], cb[:, 384:512]
        nm4_b = cb[:, 512:1024]
        Z_b = cb[:, 1024:1024 + 2048].rearrange("p (a c) -> p a c", a=8)
        L2_b = cb[:, 3072:3200]
        tk.dma("sp", pcols[:], pc_d[:, :])
        ar_reset()
        cm = ar([NCM])
        tk.dma("sp", cm, cm_d[:, :])
        V("tensor_copy", out=ident_f[:], in_=cm[:, CM_ID:CM_ID + 128])
        V("tensor_copy", out=bm_f[:], in_=cm[:, CM_BM:CM_BM + 128])
        V("tensor_copy", out=ident_b, in_=cm[:, CM_ID:CM_ID + 128])
        V("memset", ap=ones_b, constant=1.0)
        V("tensor_copy", out=U_b, in_=cm[:, CM_U:CM_U + 128])
        V("tensor_copy", out=perm_b, in_=cm[:, CM_PERM:CM_PERM + 128])
        for i in range(4):
            V("tensor_copy", out=nm4_b[:, i * 128:(i + 1) * 128], in_=cm[:, CM_NM:CM_NM + 128])
        V("tensor_copy", out=cb[:, 1024:1024 + 2048], in_=cm[:, CM_Z:CM_Z + 2048])
        V("tensor_copy", out=L2_b, in_=cm[:, CM_L2:CM_L2 + 128])

        def pc(name, i=0, n=1):
            o = PCO[name] + i
            return pcols[:, o:o + n]

        if cfg.s5:
            rbar = sb("s5_rbar", [128, 64])
            cosT = sb("s5_cos", [128, 64, NB], BF16)
            sinT = sb("s5_sin", [128, 64, NB], BF16)
            carry = sb("s5_carry", [128, 64])
            ar_reset()
            c_re = ar([64, 16]); c_im = ar([64, 16])
            bbr = ar([64, 16]); bbi = ar([64, 16])
            ctab = {}
            for key in ("L", "S1", "S2", "R", "O"):
                for sh in (0, 1):
                    ctab[(key, sh)] = ar([8, 64])
            dtg = ar([64]); lrd = ar([64]); th = ar([64]); t0_ = ar([64]); t1_ = ar([64]); t2_ = ar([64])
            arr = ar([64]); aii = ar([64]); fr = ar([64]); fi = ar([64])
            save = ar_off[0]
            b_re = ar([64, 16]); b_im = ar([64, 16]); tb = ar([64, 16])
            ANG = ar([32, 64]); MAGN = ar([32, 64])
            tmpa = ar([8, 64])
            ki = ar([1024], I32)
            kf_ = ar([1024]); tt_ = ar([1024])
            s5v = s5p_d.rearrange("p a (g k) -> p a g k", g=64)
            tk.dma("sp", b_re, s5v[:, 0])
            tk.dma("sp", b_im, s5v[:, 1])
            tk.dma("sp", c_re, s5v[:, 2])
            tk.dma("sp", c_im, s5v[:, 3])

            def sincos_reduce(dst, src, n):
                kv = ki[:, 0:n]; kf = kf_[:, 0:n]; tt = tt_[:, 0:n]
                V("tensor_scalar", out=kf, in0=src, scalar1=float(1.0 / TWO_PI), scalar2=64.0, op0=ALU.mult, op1=ALU.add)
                V("tensor_copy", out=kv, in_=kf)
                V("tensor_copy", out=kf, in_=kv)
                V("tensor_scalar", out=tt, in0=src, scalar1=float(64 * TWO_PI), scalar2=None, op0=ALU.add)
                V("scalar_tensor_tensor", out=tt, in0=kf, scalar=-TWO_PI, in1=tt, op0=ALU.mult, op1=ALU.add)
                A(out=dst, in_=tt, func=AF.Sin)

            A(out=dtg, in_=pc("ldt", 0, 64), func=AF.Exp)
            V("tensor_tensor", out=lrd, in0=pc("lre", 0, 64), in1=dtg, op=ALU.mult)
            V("tensor_tensor", out=th, in0=pc("lim", 0, 64), in1=dtg, op=ALU.mult)
            A(out=t0_, in_=lrd, func=AF.Exp)
            V("tensor_scalar", out=t1_, in0=th, scalar1=float(np.pi / 2), scalar2=None, op0=ALU.add)
            sincos_reduce(arr, t1_, 64)
            sincos_reduce(aii, th, 64)
            V("tensor_tensor", out=arr, in0=arr, in1=t0_, op=ALU.mult)
            V("tensor_tensor", out=aii, in0=aii, in1=t0_, op=ALU.mult)
            lr, li = pc("lre", 0, 64), pc("lim", 0, 64)
            V("tensor_tensor", out=t1_, in0=lr, in1=lr, op=ALU.mult)
            V("tensor_tensor", out=t2_, in0=li, in1=li, op=ALU.mult)
            V("tensor_tensor", out=t1_, in0=t1_, in1=t2_, op=ALU.add)
            V("reciprocal", out=t1_, in_=t1_)
            V("tensor_scalar", out=t0_, in0=arr, scalar1=-1.0, scalar2=None, op0=ALU.add)
            V("tensor_tensor", out=fr, in0=t0_, in1=lr, op=ALU.mult)
            V("tensor_tensor", out=t2_, in0=aii, in1=li, op=ALU.mult)
            V("tensor_tensor", out=fr, in0=fr, in1=t2_, op=ALU.add)
            V("tensor_tensor", out=fr, in0=fr, in1=t1_, op=ALU.mult)
            V("tensor_tensor", out=fi, in0=aii, in1=lr, op=ALU.mult)
            V("tensor_tensor", out=t2_, in0=t0_, in1=li, op=ALU.mult)
            V("tensor_tensor", out=fi, in0=fi, in1=t2_, op=ALU.subtract)
            V("tensor_tensor", out=fi, in0=fi, in1=t1_, op=ALU.mult)
            frb = fr.unsqueeze(2).to_broadcast([128, 64, 16])
            fib = fi.unsqueeze(2).to_broadcast([128, 64, 16])
            V("tensor_tensor", out=bbr, in0=b_re, in1=frb, op=ALU.mult)
            V("tensor_tensor", out=tb, in0=b_im, in1=fib, op=ALU.mult)
            V("tensor_tensor", out=bbr, in0=bbr, in1=tb, op=ALU.subtract)
            V("tensor_tensor", out=bbi, in0=b_im, in1=frb, op=ALU.mult)
            V("tensor_tensor", out=tb, in0=b_re, in1=fib, op=ALU.mult)
            V("tensor_tensor", out=bbi, in0=bbi, in1=tb, op=ALU.add)
            nvb = pc("nv", 0, 32).unsqueeze(2).to_broadcast([128, 32, 64])
            V("tensor_tensor", out=ANG, in0=th.unsqueeze(1).to_broadcast([128, 32, 64]), in1=nvb, op=ALU.mult)
            V("tensor_tensor", out=MAGN, in0=lrd.unsqueeze(1).to_broadcast([128, 32, 64]), in1=nvb, op=ALU.mult)
            A(out=MAGN, in_=MAGN, func=AF.Exp)
            for key, st, phn in (("L", 0, "ph1"), ("S1", 1, "ph1"), ("S2", 1, "ph2"), ("R", 2, "psi"), ("O", 3, "psi")):
                for sh in (0, 1):
                    dst = ctab[(key, sh)]
                    V("tensor_scalar", out=tmpa, in0=ANG[:, st * 8:(st + 1) * 8, :], scalar1=pc(phn),
                      scalar2=None, op0=ALU.add)
                    V("tensor_scalar", out=tmpa, in0=tmpa, scalar1=float(np.pi / 2 * (1 + sh)), scalar2=None, op0=ALU.add)
                    sincos_reduce(dst.rearrange("p a b -> p (a b)"), tmpa.rearrange("p a b -> p (a b)"), 512)
                    V("tensor_tensor", out=dst, in0=dst, in1=MAGN[:, st * 8:(st + 1) * 8, :], op=ALU.mult)
            A(out=rbar[:], in_=lrd, func=AF.Exp, scale=8.0)
            NBC = 16
            bang = ANG[:, 0:16, :].rearrange("p a b -> p (a b)")
            stmp = MAGN[:, 0:16, :].rearrange("p a b -> p (a b)")
            bang3 = bang.rearrange("p (g b) -> p g b", g=64)
            for b0 in range(0, NB, NBC):
                V("tensor_tensor", out=bang3, in0=th.unsqueeze(2).to_broadcast([128, 64, NBC]),
                  in1=pc("bmul", b0, NBC).unsqueeze(1).to_broadcast([128, 64, NBC]), op=ALU.mult)
                sincos_reduce(stmp, bang, 64 * NBC)
                V("tensor_copy", out=sinT[:, :, b0:b0 + NBC], in_=stmp.rearrange("p (g b) -> p g b", g=64))
                V("tensor_scalar", out=bang, in0=bang, scalar1=float(np.pi / 2), scalar2=None, op0=ALU.add)
                sincos_reduce(stmp, bang, 64 * NBC)
                V("tensor_copy", out=cosT[:, :, b0:b0 + NBC], in_=stmp.rearrange("p (g b) -> p g b", g=64))
            st_bi = [i for i, bl in enumerate(blocks) if bl[4] == "st"]
            to_bi = [i for i, bl in enumerate(blocks) if bl[4] == "to"]
            ar_reset(save)
            stS = ar([16, 2, 128], BF16)
            stT = ar([16, 2, 128], BF16)
            tq = ar([8, 8, 16])
            tq2 = ar([8, 8, 16])
            tm = ar([4, 128])
            tabs = {key: ar([8, 8, 16]) for key in ("L", "S1", "S2", "R", "O")}
            for bt in range(4):
                for sub in range(2):
                    g0 = bt * 16 + sub * 8
                    for key, P_re, P_im in (("L", bbr, bbi), ("S1", bbr, bbi), ("S2", bbr, bbi), ("R", c_re, c_im), ("O", c_re, c_im)):
                        t4 = tabs[key]
                        c0 = ctab[(key, 0)][:, :, g0:g0 + 8].rearrange("p n g -> p g n").unsqueeze(3).to_broadcast([128, 8, 8, 16])
                        c1 = ctab[(key, 1)][:, :, g0:g0 + 8].rearrange("p n g -> p g n").unsqueeze(3).to_broadcast([128, 8, 8, 16])
                        pr = P_re[:, g0:g0 + 8, :].unsqueeze(2).to_broadcast([128, 8, 8, 16])
                        pi_ = P_im[:, g0:g0 + 8, :].unsqueeze(2).to_broadcast([128, 8, 8, 16])
                        EW = G if key in ("S1", "S2", "O") else V
                        tqq = tq2 if key in ("S1", "S2", "O") else tq
                        EW("tensor_tensor", out=t4, in0=pr, in1=c0, op=ALU.mult)
                        EW("tensor_tensor", out=tqq, in0=pi_, in1=c1, op=ALU.mult)
                        EW("tensor_tensor", out=t4, in0=t4, in1=tqq, op=ALU.add)
                    for q in range(2):
                        b = bank()
                        for gg in range(4):
                            gl = q * 4 + gg
                            MM(out=b[:, gg * 128:(gg + 1) * 128], lhsT=tabs["L"][:, gl].rearrange("p a b -> p (a b)"),
                               rhs=tabs["R"][:, gl].rearrange("p a b -> p (a b)"), start=True, stop=True)
                        V("tensor_tensor", out=tm, in0=b[:].rearrange("p (a c) -> p a c", a=4),
                          in1=bm_f[:].unsqueeze(1).to_broadcast([128, 4, 128]), op=ALU.mult)
                        for gg in range(4):
                            gl = q * 4 + gg
                            V("scalar_tensor_tensor", out=stT[:, sub * 8 + gl, 0, :], in0=ident_f[:], scalar=pc("s5d", g0 + gl),
                              in1=tm[:, gg, :], op0=ALU.mult, op1=ALU.add)
                        for si_, key in ((0, "S1"), (1, "S2")):
                            b2 = bank()
                            for gg in range(4):
                                gl = q * 4 + gg
                                TR(out=b2[:, gg * 128:(gg + 1) * 128], in_=tabs[key][:, gl].rearrange("p a b -> p (a b)"),
                                   identity=ident_f[:])
                            A(out=stS[:, sub * 8 + q * 4:sub * 8 + (q + 1) * 4, si_, :],
                              in_=b2[:].rearrange("p (a c) -> p a c", a=4), func=AF.Copy)
                    V("tensor_copy", out=stT[:, sub * 8:(sub + 1) * 8, 1, :], in_=tabs["O"].rearrange("p g a b -> p g (a b)"))
                tk.dma("sp", wscr[st_bi[bt], :, :], stS.rearrange("p a b c -> p (a b c)"))
                tk.dma("sp", wscr[to_bi[bt], :, :], stT.rearrange("p a b c -> p (a b c)"))

        xT = sb("xT", [128, 8, NT])
        hT = sb("hT", [128, 8, NT], BF16)
        mrg = sb("mrg", [128, 8, NT], BF16)
        slots = [sb("wslot%d" % i, [128, SLOT], BF16) for i in range(NSLOT)]
        if cfg.ffn:
            fhalo = sb("fhalo", [128, 44, 2], BF16)
        if cfg.xa:
            KT = sb("KT", [128, 4, NSEQ * MEM], BF16)
            Vm = sb("Vm", [128, NSEQ * 2, 512], BF16)
        if cfg.ssd:
            hs = sb("ssd_hs", [128, 2048])
            hsb = sb("ssd_hsb", [128, 2048], BF16)
            mhalo = sb("mhalo", [128, 24, 3], BF16)
            Aneg = sb("Aneg", [128, 32])
            A(out=Aneg[:], in_=pc("alog", 0, 32), func=AF.Exp)
            V("tensor_scalar", out=Aneg[:], in0=Aneg[:], scalar1=-1.0, scalar2=None, op0=ALU.mult)

        stream = {"next_issue": 0, "next_use": 0}
        total_blocks = NBK * NTILES

        def issue_upto(n):
            while stream["next_issue"] < min(n, total_blocks):
                i = stream["next_issue"]
                bi = i % NBK
                wname, nk, c0, ncb, tag = blocks[bi]
                sl = slots[i % NSLOT][:, 0:nk * ncb]
                if i < NBK and not wname.startswith("s5"):
                    src = w_d[wname][:, c0:c0 + ncb].rearrange("(k p) c -> p k c", p=128)
                    tk.dma("pool", sl.rearrange("p (k c) -> p k c", k=nk), src)
                    if NTILES > 1:
                        tk.dma("sp", wscr[bi, :, 0:nk * ncb], sl)
                else:
                    tk.dma("sp", sl, wscr[bi, :, 0:nk * ncb])
                stream["next_issue"] += 1

        def next_block(tag):
            i = stream["next_use"]
            bi = i % NBK
            wname, nk, c0, ncb, btag = blocks[bi]
            assert btag == tag, (btag, tag)
            issue_upto(i + NSLOT)
            stream["next_use"] += 1
            return slots[i % NSLOT][:, 0:nk * ncb].rearrange("p (k c) -> p k c", k=nk), nk, ncb

        def proj(tag, ncols, rhs_fn, nk, evac, n=NT):
            m = 0
            done = 0
            while done < ncols:
                wv, wnk, ncb = next_block(tag)
                assert wnk == nk
                for mm in range(ncb // 128):
                    b = bank()
                    for k in range(nk):
                        MM(out=b[:, 0:n], lhsT=wv[:, k, mm * 128:(mm + 1) * 128], rhs=rhs_fn(k),
                           start=(k == 0), stop=(k == nk - 1))
                    evac(m, b)
                    m += 1
                done += ncb

        def rms_stats(src_fn, nct, n, sq, denom):
            for ct in range(nct):
                A(out=sq[:, ct, 0:n], in_=src_fn(ct), func=AF.Square)
            b = bank()
            for ct in range(nct):
                MM(out=b[:, 0:n], lhsT=ones_b, rhs=sq[:, ct, 0:n], start=(ct == 0), stop=(ct == nct - 1))
            rs = sq[:, 0:2, :].rearrange("p a b -> p (a b)").bitcast(F32)[:, 0:n]
            A(out=rs, in_=b[:, 0:n], func=AF.Sqrt, scale=1.0 / denom, bias=pc("eps"))
            V("reciprocal", out=rs, in_=rs)
            return rs

        def rmsnorm_to_hT(gname):
            sq = ar([8, NT], BF16)
            rstd = rms_stats(lambda ct: xT[:, ct, :], 8, NT, sq, D)
            for ct in range(8):
                V("scalar_tensor_tensor", out=hT[:, ct, :], in0=xT[:, ct, :], scalar=pc(gname, ct), in1=rstd,
                  op0=ALU.mult, op1=ALU.mult)

        n_merged = [0]

        def merge(gtag, ysrc):
            first = (n_merged[0] == 0)
            n_merged[0] += 1
            sgs = [ar([NT]) for _ in range(2)]
            tmps = [ar([NT], BF16) for _ in range(2)]

            def ev(m, b):
                sg = sgs[m % 2]
                A(out=sg, in_=b[:], func=AF.Sigmoid)
                if first:
                    G("tensor_tensor", out=mrg[:, m, :], in0=sg, in1=ysrc[:, m, :], op=ALU.mult)
                else:
                    tp = tmps[m % 2]
                    G("tensor_tensor", out=tp, in0=sg, in1=ysrc[:, m, :], op=ALU.mult)
                    G("tensor_tensor", out=mrg[:, m, :], in0=mrg[:, m, :], in1=tp, op=ALU.add)
            proj(gtag, 1024, lambda k: hT[:, k, :], 8, ev)

        if cfg.xa:
            ar_reset()
            NM = NSEQ * MEM
            memin = ar([NSEQ * 2, D])
            memT = ar([8, NM])
            memTn = ar([8, NM], BF16)
            sqm = ar([8, NM], BF16)
            wkv = [ar([8, 512], BF16) for _ in range(2)]
            tk.dma("act", memin, mem_d.rearrange("s (t p) d -> p (s t) d", p=128))
            for i in range(2):
                tk.dma("pool", wkv[i], w_d["w_kv"][:, i * 512:(i + 1) * 512].rearrange("(k p) c -> p k c", p=128))
            for ct in range(8):
                b = bank()
                for s in range(NSEQ * 2):
                    TR(out=b[:, s * 128:(s + 1) * 128], in_=memin[:, s, ct * 128:(ct + 1) * 128], identity=ident_f[:])
                A(out=memT[:, ct, :], in_=b[:, 0:NM], func=AF.Copy)
            rstd = rms_stats(lambda ct: memT[:, ct, :], 8, NM, sqm, D)
            for ct in range(8):
                V("scalar_tensor_tensor", out=memTn[:, ct, :], in0=memT[:, ct, :], scalar=pc("gmem", ct),
                  in1=rstd, op0=ALU.mult, op1=ALU.mult)
            for h in range(4):
                b = bank()
                for k in range(8):
                    MM(out=b[:, 0:NM], lhsT=wkv[0][:, k, h * 128:(h + 1) * 128], rhs=memTn[:, k, :],
                       start=(k == 0), stop=(k == 7))
                A(out=KT[:, h, :], in_=b[:, 0:NM], func=AF.Copy)
            for mt in range(NSEQ * 2):
                b = bank()
                for k in range(8):
                    MM(out=b[:], lhsT=memTn[:, k, mt * 128:(mt + 1) * 128], rhs=wkv[1][:, k, :],
                       start=(k == 0), stop=(k == 7))
                A(out=Vm[:, mt, :], in_=b[:], func=AF.Copy)

        xin_next = [None]
        pending_tail = [None]
        for ti in range(NTILES):
            s_i = ti // TPS
            t0 = (ti % TPS) * NT
            first = (ti % TPS == 0)
            n_merged[0] = 0
            ar_reset()
            if ti == 0:
                xin = ar([NCH, D])
                tk.dma("act", xin, x_d[s_i, t0:t0 + NT, :].rearrange("(s p) d -> p s d", p=128))
            else:
                xin = xin_next[0]
            for ct in range(8):
                b = bank()
                for s in range(NCH):
                    TR(out=b[:, s * 128:(s + 1) * 128], in_=xin[:, s, ct * 128:(ct + 1) * 128], identity=ident_f[:])
                A(out=xT[:, ct, :], in_=b[:, 0:NT], func=AF.Copy)
            rmsnorm_to_hT("g1")
            if pending_tail[0] is not None:
                pending_tail[0]()
                pending_tail[0] = None

            if cfg.s5:
                ar_reset()
                uT = ar([8, NT], BF16)
                U2 = ar([8, 8, NB], BF16)
                o_st = ar_off[0]
                St = ar([64, NB])
                o_after = ar_off[0]
                ar_reset(o_st)
                Yg = ar([8, 8, NB], BF16)
                ar_reset(o_after)
                o_wS = ar_off[0]
                wS = ar([64, NB])
                wSb = ar([64, NB], BF16)
                sprev = ar([64, NB + 1], BF16)
                gT = uT
                tA = [ar([8, NB]) for _ in range(2)]
                tB = [ar([8, NB]) for _ in range(2)]
                o_xa = ar_off[0]

                def ev_u(m, b):
                    A(out=uT[:, m, :].rearrange("p (j b) -> p j b", j=8), in_=b[:].rearrange("p (b j) -> p j b", j=8),
                      func=AF.Copy)
                proj("u", 1024, lambda k: hT[:, k, :], 8, ev_u)
                if first:
                    V("memset", ap=carry[:], constant=0.0)
                uTv = uT.rearrange("p c (j b) -> p c b j", j=8)
                for g8 in range(8):
                    b = bank()
                    bv = b[:, 0:8 * NB].rearrange("p (c b) -> p c b", c=8)
                    for j in range(8):
                        MM(out=bv, lhsT=Z_b[:, g8, 128 - 16 * j:256 - 16 * j], rhs=uTv[:, :, :, j],
                           start=(j == 0), stop=(j == 7))
                    V("tensor_copy", out=U2[:, :, g8, :], in_=bv)
                for blk in range(4):
                    wv, _, _ = next_block("st")
                    wv4 = wv.rearrange("p (g s) c -> p g s c", s=2)
                    for half in range(2):
                        bx, by = bank(), bank()
                        for gi in range(8):
                            gl = half * 8 + gi
                            g = blk * 16 + gl
                            rhs = U2[:, g // 8, g % 8, :]
                            MM(out=bx[:, gi * NB:(gi + 1) * NB], lhsT=wv4[:, gl, 0, :], rhs=rhs, start=True, stop=True)
                            MM(out=by[:, gi * NB:(gi + 1) * NB], lhsT=wv4[:, gl, 1, :], rhs=rhs, start=True, stop=True)
                        gs = blk * 16 + half * 8
                        ta, tb_ = tA[half], tB[half]
                        V("tensor_tensor", out=ta, in0=bx[:, 0:8 * NB].rearrange("p (g b) -> p g b", g=8),
                          in1=cosT[:, gs:gs + 8, :], op=ALU.mult)
                        V("tensor_tensor", out=tb_, in0=by[:, 0:8 * NB].rearrange("p (g b) -> p g b", g=8),
                          in1=sinT[:, gs:gs + 8, :], op=ALU.mult)
                        G("tensor_tensor", out=St[:, gs:gs + 8, :], in0=ta, in1=tb_, op=ALU.add)
            if cfg.xa:
                o_x0 = o_xa if cfg.s5 else 0
                ar_reset(o_x0)
                qT = ar([4, NT], BF16)
                PT = [ar([2, NT], BF16) for _ in range(2)]
                oT = ar([4, NT], BF16)
                rden = [ar([NT]) for _ in range(2)]
                o_x3 = ar_off[0]
                ar_reset(o_x0)
                ytmp = ar([8, NT], BF16)
                ar_reset(o_x3)

                def ev_q(m, b):
                    A(out=qT[:, m, :], in_=b[:], func=AF.Copy)
                proj("q", 512, lambda k: hT[:, k, :], 8, ev_q)
                for h in range(4):
                    pt = PT[h % 2]
                    for mt in range(2):
                        b = bank()
                        MM(out=b[:], lhsT=KT[:, h, s_i * MEM + mt * 128:s_i * MEM + (mt + 1) * 128], rhs=qT[:, h, :],
                           start=True, stop=True)
                        A(out=pt[:, mt, :], in_=b[:], func=AF.Exp, scale=float(128 ** -0.5))
                    bd = bank()
                    for mt in range(2):
                        MM(out=bd[:], lhsT=ones_b, rhs=pt[:, mt, :], start=(mt == 0), stop=(mt == 1))
                    bo = bank()
                    for mt in range(2):
                        MM(out=bo[:], lhsT=Vm[:, s_i * 2 + mt, h * 128:(h + 1) * 128], rhs=pt[:, mt, :],
                           start=(mt == 0), stop=(mt == 1))
                    V("reciprocal", out=rden[h % 2], in_=bd[:])
                    V("tensor_tensor", out=oT[:, h, :], in0=bo[:], in1=rden[h % 2], op=ALU.mult)

                def ev_wc(m, b):
                    A(out=ytmp[:, m, :], in_=b[:], func=AF.Copy)
                proj("wc", 1024, lambda k: oT[:, k, :], 4, ev_wc)
                merge("gc", ytmp)

            if cfg.s5:
                for g in range(64):
                    V("tensor_tensor_scan", out=wS[:, g, :], data0=rbar[:, g:g + 1].to_broadcast([128, NB]),
                      data1=St[:, g, :], initial=carry[:, g:g + 1], op0=ALU.mult, op1=ALU.add)
                A(out=wSb, in_=wS, func=AF.Copy)
                A(out=sprev[:, :, 0], in_=carry[:], func=AF.Copy)
                wSbf = wSb.rearrange("p g b -> p (g b)")
                for q in range(8):
                    b = bank()
                    MM(out=b[:, 0:8 * NB], lhsT=perm_b, rhs=wSbf[:, q * 8 * NB:(q + 1) * 8 * NB], start=True, stop=True)
                    ta, tb_ = tA[q % 2], tB[q % 2]
                    V("tensor_tensor", out=ta, in0=b[:, 0:8 * NB].rearrange("p (g b) -> p g b", g=8),
                      in1=sinT[:, q * 8:(q + 1) * 8, :], op=ALU.mult)
                    G("tensor_tensor", out=tb_, in0=wS[:, q * 8:(q + 1) * 8, :], in1=cosT[:, q * 8:(q + 1) * 8, :], op=ALU.mult)
                    V("tensor_tensor", out=sprev[:, q * 8:(q + 1) * 8, 1:NB + 1], in0=tb_, in1=ta, op=ALU.subtract)
                    V("tensor_tensor", out=carry[:, q * 8:(q + 1) * 8], in0=tb_[:, :, NB - 1], in1=ta[:, :, NB - 1], op=ALU.subtract)
                for blk in range(4):
                    wv, _, _ = next_block("to")
                    wv4 = wv.rearrange("p (g s) c -> p g s c", s=2)
                    for half in range(2):
                        b = bank()
                        for gi in range(8):
                            gl = half * 8 + gi
                            g = blk * 16 + gl
                            MM(out=b[:, gi * NB:(gi + 1) * NB], lhsT=wv4[:, gl, 0, :], rhs=U2[:, g // 8, g % 8, :],
                               start=True, stop=False)
                            MM(out=b[:, gi * NB:(gi + 1) * NB], lhsT=wv4[:, gl, 1, :], rhs=sprev[:, g, 0:NB],
                               start=False, stop=True)
                        ct = (blk * 16 + half * 8) // 8
                        A(out=Yg[:, ct, :, :], in_=b[:, 0:8 * NB].rearrange("p (g b) -> p g b", g=8), func=AF.Gelu)
                gTv = gT.rearrange("p c (j b) -> p c b j", j=8)
                for t in range(8):
                    b = bank()
                    bv = b[:, 0:8 * NB].rearrange("p (c b) -> p c b", c=8)
                    for g8 in range(8):
                        MM(out=bv, lhsT=Z_b[:, t, 128 - 16 * g8:256 - 16 * g8], rhs=Yg[:, :, g8, :],
                           start=(g8 == 0), stop=(g8 == 7))
                    A(out=gTv[:, :, :, t], in_=bv, func=AF.Copy)

                ar_reset(o_wS)
                ytmp = ar([8, NT], BF16)

                def ev_av(m, b):
                    A(out=ytmp[:, m, :].rearrange("p (b j) -> p b j", j=8), in_=b[:].rearrange("p (j b) -> p b j", j=8),
                      func=AF.Copy)
                proj("av", 1024, lambda k: gT[:, k, :], 8, ev_av)
                sg2 = [ar([NT], BF16) for _ in range(2)]

                def ev_ag(m, b):
                    A(out=sg2[m % 2].rearrange("p (b j) -> p b j", j=8), in_=b[:].rearrange("p (j b) -> p b j", j=8),
                      func=AF.Sigmoid)
                    G("tensor_tensor", out=ytmp[:, m, :], in0=ytmp[:, m, :], in1=sg2[m % 2], op=ALU.mult)
                proj("ag", 1024, lambda k: gT[:, k, :], 8, ev_ag)
                merge("ga", ytmp)

            if cfg.ssd:
                ar_reset()
                sz = ar([16, NT], BF16)
                xbcs = ar([24, NT], BF16)
                o_ssd = ar_off[0]
                dtt = ar([NCH, 32]); dA = ar([NCH, 32]); dtd = ar([NCH, 32]); cdb2 = ar([2, 32])
                dAb = ar([NCH, 32], BF16); ndAb = ar([NCH, 32], BF16)
                raw = [ar([NT + 3], BF16) for _ in range(3)]
                ctm = [ar([NT]) for _ in range(3)]
                Gs = ar([4, 128])
                E4 = [ar([4, 128]) for _ in range(2)]
                EA4 = [ar([4, 128]) for _ in range(2)]
                Mall = ar([32, 128], BF16)
                Cdall = ar([32, 128], BF16)
                xdt = ar([2048], BF16)
                xdtd = ar([2048], BF16)
                Btok = ar([512], BF16)
                ytm = [ar([4, 128], BF16) for _ in range(2)]
                ytm2 = [ar([4, 128]) for _ in range(2)]
                wv, _, _ = next_block("dt")
                bdt = bank()
                for c in range(NCH):
                    for k in range(8):
                        MM(out=bdt[:, c * 32:(c + 1) * 32], lhsT=hT[:, k, c * 128:(c + 1) * 128], rhs=wv[:, k, :],
                           start=(k == 0), stop=(k == 7))
                V("tensor_tensor", out=dtt, in0=bdt[:, 0:NCH * 32].rearrange("p (c h) -> p c h", c=NCH),
                  in1=pc("dtb", 0, 32).unsqueeze(1).to_broadcast([128, NCH, 32]), op=ALU.add)
                A(out=dtt, in_=dtt, func=AF.Exp)
                A(out=dtt, in_=dtt, func=AF.Ln, bias=1.0)
                V("tensor_tensor", out=dA, in0=dtt, in1=Aneg[:].unsqueeze(1).to_broadcast([128, NCH, 32]), op=ALU.mult)
                V("tensor_copy", out=dAb, in_=dA)
                V("tensor_scalar", out=ndAb, in0=dA, scalar1=-1.0, scalar2=None, op0=ALU.mult)

                def ev_z(m, b):
                    A(out=sz[:, m, :], in_=b[:], func=AF.Silu)
                proj("z", 2048, lambda k: hT[:, k, :], 8, ev_z)
                if first:
                    G("memset", ap=mhalo[:], constant=0.0)

                pend = []

                def ev_xbc(m, b):
                    r = raw[m % 3]
                    tm = ctm[m % 3]
                    G("tensor_copy", out=r[:, 0:3], in_=mhalo[:, m, :])
                    A(out=r[:, 3:NT + 3], in_=b[:], func=AF.Copy)
                    G("tensor_copy", out=mhalo[:, m, :], in_=r[:, NT:NT + 3])
                    G("tensor_scalar", out=tm, in0=r[:, 0:NT], scalar1=pc("m2cw", m), scalar2=pc("m2cb", m),
                      op0=ALU.mult, op1=ALU.add)
                    for kk in range(1, 4):
                        V("scalar_tensor_tensor", out=tm, in0=r[:, kk:NT + kk], scalar=pc("m2cw", 24 * kk + m), in1=tm,
                          op0=ALU.mult, op1=ALU.add)
                    while pend:
                        pend.pop(0)()
                    pend.append(lambda m=m, tm=tm: A(out=xbcs[:, m, :], in_=tm, func=AF.Silu))
                proj("xbc", 3072, lambda k: hT[:, k, :], 8, ev_xbc)
                while pend:
                    pend.pop(0)()

                for c in range(NCH):
                    cs = slice(c * 128, (c + 1) * 128)
                    firstc = first and c == 0
                    cdb = cdb2[:, c % 2, :]
                    bD = bank()
                    MM(out=bD[:, 0:32], lhsT=L2_b, rhs=dAb[:, c, :], start=True, stop=True)
                    MM(out=bD[:, 32:64], lhsT=ones_b, rhs=dAb[:, c, :], start=True, stop=True)
                    A(out=dtd[:, c, :], in_=bD[:, 0:32], func=AF.Exp)
                    A(out=cdb, in_=bD[:, 32:64], func=AF.Exp)
                    V("tensor_tensor", out=dtd[:, c, :], in0=dtd[:, c, :], in1=dtt[:, c, :], op=ALU.mult)
                    bG = bank()
                    for g in range(4):
                        MM(out=bG[:, g * 128:(g + 1) * 128], lhsT=xbcs[:, 16 + g, cs], rhs=xbcs[:, 20 + g, cs],
                           start=True, stop=True)
                    A(out=Gs, in_=bG[:].rearrange("p (g l) -> p g l", g=4), func=AF.Copy)
                    for q in range(4):
                        bT = bank()
                        bTb = bT[:].bitcast(BF16)
                        for i in range(4):
                            TR(out=bTb[:, i * 128:(i + 1) * 128], in_=xbcs[:, q * 4 + i, cs], identity=ident_b)
                        src = bTb[:, 0:512].rearrange("p (h d) -> p h d", h=8)
                        V("tensor_tensor", out=xdt[:, q * 512:(q + 1) * 512].rearrange("p (h d) -> p h d", h=8), in0=src,
                          in1=dtt[:, c, q * 8:(q + 1) * 8].unsqueeze(2).to_broadcast([128, 8, 64]), op=ALU.mult)
                        V("tensor_tensor", out=xdtd[:, q * 512:(q + 1) * 512].rearrange("p (h d) -> p h d", h=8), in0=src,
                          in1=dtd[:, c, q * 8:(q + 1) * 8].unsqueeze(2).to_broadcast([128, 8, 64]), op=ALU.mult)
                    bT = bank()
                    bTb = bT[:].bitcast(BF16)
                    for g in range(4):
                        TR(out=bTb[:, g * 128:(g + 1) * 128], in_=xbcs[:, 16 + g, cs], identity=ident_b)
                    A(out=Btok, in_=bTb[:, 0:512], func=AF.Copy)

                    ybank = {}

                    def emit_y(hq):
                        q = hq // 2
                        if hq % 2 == 0:
                            ybank[q] = bank()
                        bY = ybank[q]
                        for pp in range(2):
                            pi_ = (hq % 2) * 2 + pp
                            pr = q * 4 + pi_
                            for hh in range(2):
                                h = 2 * pr + hh
                                o = bY[hh * 64:(hh + 1) * 64, pi_ * 128:(pi_ + 1) * 128]
                                MM(out=o, lhsT=xdt[:, h * 64:(h + 1) * 64], rhs=Mall[:, h, :], start=True, stop=firstc)
                                if not firstc:
                                    MM(out=o, lhsT=hsb[:, h * 64:(h + 1) * 64], rhs=Cdall[:, h, :], start=False, stop=True)
                        if hq % 2 == 1:
                            t1, t2 = ytm[q % 2], ytm2[q % 2]
                            for p4 in range(4):
                                A(out=t1[:, p4, :], in_=xbcs[:, q * 4 + p4, cs], func=AF.Copy, scale=pc("dcol", q * 4 + p4))
                            V("tensor_tensor", out=t2, in0=bY[:].rearrange("p (a l) -> p a l", a=4), in1=t1, op=ALU.add)
                            G("tensor_tensor", out=xbcs[:, q * 4:(q + 1) * 4, cs], in0=t2, in1=sz[:, q * 4:(q + 1) * 4, cs],
                              op=ALU.mult)

                    for hq in range(8):
                        h0 = hq * 4
                        g = hq // 2
                        bE, bA = bank(), bank()
                        MM(out=bE[:].rearrange("p (h l) -> p h l", h=4), lhsT=U_b,
                           rhs=ndAb[:, c, h0:h0 + 4].unsqueeze(2).to_broadcast([128, 4, 128]), start=True, stop=False)
                        MM(out=bE[:], lhsT=ident_b, rhs=nm4_b, start=False, stop=False)
                        for hh in range(4):
                            MM(out=bE[:, hh * 128:(hh + 1) * 128], lhsT=dAb[:, c, h0 + hh:h0 + hh + 1].to_broadcast([128, 128]),
                               rhs=U_b, start=False, stop=(hh == 3))
                        for hh in range(4):
                            MM(out=bA[:, hh * 128:(hh + 1) * 128], lhsT=dAb[:, c, h0 + hh:h0 + hh + 1].to_broadcast([128, 128]),
                               rhs=U_b, start=True, stop=True)
                        e4, ea4 = E4[hq % 2], EA4[hq % 2]
                        A(out=e4, in_=bE[:].rearrange("p (h l) -> p h l", h=4), func=AF.Exp)
                        A(out=ea4, in_=bA[:].rearrange("p (h l) -> p h l", h=4), func=AF.Exp)
                        V("tensor_tensor", out=Mall[:, h0:h0 + 4, :], in0=e4,
                          in1=Gs[:, g, :].unsqueeze(1).to_broadcast([128, 4, 128]), op=ALU.mult)
                        G("tensor_tensor", out=Cdall[:, h0:h0 + 4, :], in0=ea4,
                          in1=xbcs[:, 20 + g, cs].unsqueeze(1).to_broadcast([128, 4, 128]), op=ALU.mult)
                        if hq >= 1:
                            emit_y(hq - 1)
                    emit_y(7)

                    for g in range(4):
                        bS = bank()
                        MM(out=bS[:], lhsT=Btok[:, g * 128:(g + 1) * 128], rhs=xdtd[:, g * 512:(g + 1) * 512], start=True, stop=True)
                        hv = hs[:, g * 512:(g + 1) * 512]
                        if firstc:
                            V("tensor_copy", out=hv, in_=bS[:])
                        else:
                            G("tensor_tensor", out=hv.rearrange("p (h d) -> p h d", h=8), in0=hv.rearrange("p (h d) -> p h d", h=8),
                              in1=cdb[:, g * 8:(g + 1) * 8].unsqueeze(2).to_broadcast([128, 8, 64]), op=ALU.mult)
                            V("tensor_tensor", out=hv, in0=bS[:], in1=hv, op=ALU.add)
                        A(out=hsb[:, g * 512:(g + 1) * 512], in_=hv, func=AF.Copy)
                ar_reset(o_ssd)
                ytmp = ar([8, NT], BF16)
                sgb = ar([8, NT], BF16)
                for ct in range(16):
                    A(out=sz[:, ct, :], in_=xbcs[:, ct, :], func=AF.Square)

                def ev_gb(m, b):
                    A(out=sgb[:, m, :], in_=b[:], func=AF.Sigmoid)
                proj("gb", 1024, lambda k: hT[:, k, :], 8, ev_gb)
                bn = bank()
                for ct in range(16):
                    MM(out=bn[:], lhsT=ones_b, rhs=sz[:, ct, :], start=(ct == 0), stop=(ct == 15))
                rstd = sz[:, 0:2, :].rearrange("p a b -> p (a b)").bitcast(F32)
                A(out=rstd, in_=bn[:], func=AF.Sqrt, scale=1.0 / 2048, bias=pc("eps"))
                V("reciprocal", out=rstd, in_=rstd)
                for ct in range(16):
                    V("scalar_tensor_tensor", out=xbcs[:, ct, :], in0=xbcs[:, ct, :], scalar=pc("m2norm", ct), in1=rstd,
                      op0=ALU.mult, op1=ALU.mult)
                first_m = (n_merged[0] == 0)
                n_merged[0] += 1
                mtmp = [ar([NT], BF16) for _ in range(2)]

                def ev_wb(m, b):
                    A(out=ytmp[:, m, :], in_=b[:], func=AF.Copy)
                    if first_m:
                        G("tensor_tensor", out=mrg[:, m, :], in0=sgb[:, m, :], in1=ytmp[:, m, :], op=ALU.mult)
                    else:
                        G("tensor_tensor", out=mtmp[m % 2], in0=sgb[:, m, :], in1=ytmp[:, m, :], op=ALU.mult)
                        G("tensor_tensor", out=mrg[:, m, :], in0=mrg[:, m, :], in1=mtmp[m % 2], op=ALU.add)
                proj("wb", 1024, lambda k: xbcs[:, k, :], 16, ev_wb)

            if n_merged[0] > 0:
                def ev_wo(m, b):
                    V("tensor_tensor", out=xT[:, m, :], in0=b[:], in1=xT[:, m, :], op=ALU.add)
                proj("wo", 1024, lambda k: mrg[:, k, :], 8, ev_wo)

            if ti + 1 < NTILES:
                ar_reset(40 * 1024)
                xin_next[0] = ar([NCH, D])
                ns_i, nt0 = (ti + 1) // TPS, ((ti + 1) % TPS) * NT
                tk.dma("act", xin_next[0], x_d[ns_i, nt0:nt0 + NT, :].rearrange("(s p) d -> p s d", p=128))
            if cfg.ffn:
                ar_reset()
                rmsnorm_to_hT("g2")
                ar_reset()
                gact = ar([22, NT], BF16)
                raw = [ar([NT + 2], BF16) for _ in range(3)]
                ctm = [ar([NT]) for _ in range(3)]
                if first:
                    G("memset", ap=fhalo[:], constant=0.0)

                pend = []

                def ev_up(m, b):
                    r = raw[m % 3]
                    tm = ctm[m % 3]
                    G("tensor_copy", out=r[:, 0:2], in_=fhalo[:, m, :])
                    A(out=r[:, 2:NT + 2], in_=b[:], func=AF.Copy)
                    G("tensor_copy", out=fhalo[:, m, :], in_=r[:, NT:NT + 2])
                    G("tensor_scalar", out=tm, in0=r[:, 0:NT], scalar1=pc("fcw", m), scalar2=pc("fcb", m),
                      op0=ALU.mult, op1=ALU.add)
                    for kk in range(1, 3):
                        V("scalar_tensor_tensor", out=tm, in0=r[:, kk:NT + kk], scalar=pc("fcw", 44 * kk + m), in1=tm,
                          op0=ALU.mult, op1=ALU.add)
                    while pend:
                        pend.pop(0)()
                    if m < 22:
                        pend.append(lambda m=m, tm=tm: A(out=gact[:, m, :], in_=tm, func=AF.Silu))
                    else:
                        pend.append(lambda m=m, tm=tm: V("tensor_tensor", out=gact[:, m - 22, :], in0=gact[:, m - 22, :],
                                                         in1=tm, op=ALU.mult))
                proj("up", 2 * D_FF, lambda k: hT[:, k, :], 8, ev_up)
                while pend:
                    pend.pop(0)()

                def ev_dn(m, b):
                    V("tensor_tensor", out=xT[:, m, :], in0=b[:], in1=xT[:, m, :], op=ALU.add)
                proj("dn", 1024, lambda k: gact[:, k, :], 22, ev_dn)

            ar_reset(56 * 1024)
            sqf = ar([8, NT], BF16)
            xout = ar([NCH, D])
            xfin = ar([8, NT])
            rstd = rms_stats(lambda ct: xT[:, ct, :], 8, NT, sqf, D)
            for ct in range(8):
                V("scalar_tensor_tensor", out=xfin[:, ct, :], in0=xT[:, ct, :], scalar=pc("gf", ct), in1=rstd,
                  op0=ALU.mult, op1=ALU.mult)

            def final_tail(xout=xout, xfin=xfin, s_i=s_i, t0=t0):
                for s in range(NCH):
                    for half in range(2):
                        b = bank()
                        for c4 in range(4):
                            ct = half * 4 + c4
                            TR(out=b[:, c4 * 128:(c4 + 1) * 128], in_=xfin[:, ct, s * 128:(s + 1) * 128], identity=ident_f[:])
                        A(out=xout[:, s, half * 512:(half + 1) * 512], in_=b[:], func=AF.Copy)
                tk.dma("sp", out_d[s_i, t0:t0 + NT, :].rearrange("(s p) d -> p s d", p=128), xout, final=True)
            pending_tail[0] = final_tail

        pending_tail[0]()
        tk.finish()
    return nc


def _cols(v):
    v = np.asarray(v, np.float32).reshape(-1, 128)
    return np.ascontiguousarray(v.T)


def _const_mats():
    cm = np.zeros((128, NCM), np.float32)
    r = np.arange(128)
    cm[:, CM_ID:CM_ID + 128] = np.eye(128)
    cm[:, CM_U:CM_U + 128] = (r[:, None] <= r[None, :])
    cm[:, CM_NM:CM_NM + 128] = np.where(r[None, :] < r[:, None], -30000.0, 0.0)
    pm = np.zeros((128, 128), np.float32)
    for rp in range(64):
        pm[rp + 64, rp] = 1.0
        pm[rp, rp + 64] = -1.0
    cm[:, CM_PERM:CM_PERM + 128] = pm
    cm[:, CM_BM:CM_BM + 128] = ((r[None, :] // 16) >= (r[:, None] // 16))
    cm[:, CM_L2:CM_L2 + 128] = (r[:, None] > r[None, :])
    for a in range(8):
        z = np.zeros((128, 256), np.float32)
        for k in range(16):
            z[16 * a + k, 128 + k] = 1.0
        cm[:, CM_Z + 256 * a:CM_Z + 256 * (a + 1)] = z
    return cm


def make_inputs(cfg, inp, b0):
    pcv = np.zeros((128, NPC), np.float32)

    def put(name, arr, i=0):
        arr = np.asarray(arr, np.float32)
        pcv[:, PCO[name] + i:PCO[name] + i + arr.shape[1]] = arr

    put("g1", _cols(inp["norm_mix"][0]))
    put("g2", _cols(inp["norm_ffn"][0]))
    put("gf", _cols(inp["norm_final"]))
    put("gmem", _cols(inp["norm_mem"][0]))
    for k in range(3):
        put("fcw", _cols(inp["ffn_conv_w"][0][k]), 44 * k)
    put("fcb", _cols(inp["ffn_conv_b"][0]))
    for k in range(4):
        put("m2cw", _cols(inp["m2_conv_w"][0][k]), 24 * k)
    put("m2cb", _cols(inp["m2_conv_b"][0]))
    put("m2norm", _cols(inp["m2_norm"][0]))
    md = np.asarray(inp["m2_d"][0], np.float32)
    put("dcol", np.repeat(md.reshape(16, 2).T, 64, axis=0))
    put("dtb", np.tile(np.asarray(inp["m2_dt_bias"][0], np.float32)[None, :], (128, 1)))
    put("alog", np.tile(np.asarray(inp["m2_a_log"][0], np.float32)[None, :], (128, 1)))
    pcv[:, PCO["eps"]] = EPS
    put("lre", np.tile(np.asarray(inp["s5_lambda_re"][0], np.float32).T, (2, 1)))
    put("lim", np.tile(np.asarray(inp["s5_lambda_im"][0], np.float32).T, (2, 1)))
    put("ldt", np.tile(np.asarray(inp["s5_log_dt"][0], np.float32)[None, :], (128, 1)))
    sd = np.asarray(inp["s5_d"][0], np.float32).reshape(64, 16)
    put("s5d", np.tile(sd.T, (8, 1)))
    half = (np.arange(128) >= 64)
    pcv[:, PCO["ph1"]] = np.where(half, -np.pi / 2, 0.0)
    pcv[:, PCO["ph2"]] = np.where(half, np.pi, -np.pi / 2)
    pcv[:, PCO["psi"]] = np.where(half, np.pi / 2, 0.0)
    nv = [0, -1, -2, -3, -4, -5, -6, -7] + [7, 6, 5, 4, 3, 2, 1, 0] + [0, 1, 2, 3, 4, 5, 6, 7] + [1, 2, 3, 4, 5, 6, 7, 8]
    put("nv", np.tile(np.asarray(nv, np.float32)[None, :], (128, 1)))
    put("bmul", np.tile((8.0 * (np.arange(64, dtype=np.float32) + 1.0))[None, :], (128, 1)))
    s5p = np.zeros((128, 4, 1024), np.float32)
    bre = np.asarray(inp["s5_b_re"][0], np.float32)
    bim = np.asarray(inp["s5_b_im"][0], np.float32)
    cre = np.asarray(inp["s5_c_re"][0], np.float32)
    cim = np.asarray(inp["s5_c_im"][0], np.float32)
    s5p[:, 0] = np.tile(bre.transpose(1, 0, 2).reshape(64, 1024), (2, 1))
    s5p[:, 1] = np.tile(bim.transpose(1, 0, 2).reshape(64, 1024), (2, 1))
    s5p[:, 2] = np.tile(cre.transpose(2, 0, 1).reshape(64, 1024), (2, 1))
    s5p[:, 3] = np.tile(cim.transpose(2, 0, 1).reshape(64, 1024), (2, 1))
    m = {
        "x": np.ascontiguousarray(inp["x"][b0:b0 + cfg.nseq, :cfg.seq]),
        "mem": np.ascontiguousarray(inp["mem"][b0:b0 + cfg.nseq]),
        "pcols": pcv,
        "cmat": _const_mats(),
        "s5p": s5p,
    }
    for k in ("w_in", "w_a_val", "w_a_gate", "w_b", "w_kv", "w_c", "w_out", "w_up", "w_down"):
        m[k] = np.ascontiguousarray(inp[k][0])
    return m


_NC_CACHE = {}


def kernel(**inputs):
    cfg = Cfg()
    inp = {k: np.asarray(v) for k, v in inputs.items()}
    key = "full"
    if key not in _NC_CACHE:
        _NC_CACHE[key] = build(cfg)
    nc = _NC_CACHE[key]
    in_maps = [make_inputs(cfg, inp, 2 * c) for c in range(8)]
    res = run_bass_kernel_spmd(nc, in_maps, core_ids=list(range(8)))
    out = np.concatenate([r["out"] for r in res.results], axis=0)
    return out.astype(np.float32)
```

```python
import contextlib
import numpy as np
import concourse.bass as bass
import concourse.mybir as mybir
from concourse.bass_utils import run_bass_kernel_spmd

F32 = mybir.dt.float32
BF16 = mybir.dt.bfloat16
I32 = mybir.dt.int32
AF = mybir.ActivationFunctionType
ALU = mybir.AluOpType

D = 1024
NT = 512
MEM = 256
D_FF = 2816
EPS = 1e-6
P1, P2, P3, P4, P5 = 1024, 3072, 6144, 6176, 6688
D_IN = 9760


class TK:
    def __init__(self, nc, es):
        self.nc = nc
        self.es = es
        self.engs = {"pe": nc.tensor, "act": nc.scalar, "dve": nc.vector, "pool": nc.gpsimd, "sp": nc.sync}
        self.sems = {}
        self.cnt = {}
        for e in self.engs:
            self.sems[e] = es.enter_context(nc.semaphore("s_" + e))
            self.cnt[e] = 0
        self.seen = {e: {} for e in self.engs}
        self.recs = {}
        self.dq = {}
        for q, n in (("sp", 12), ("pool", 8), ("act", 4)):
            lst = []
            for i in range(n):
                key = "d_%s%d" % (q, i)
                self.sems[key] = es.enter_context(nc.semaphore(key))
                self.cnt[key] = 0
                lst.append(key)
            self.dq[q] = [lst, 0]
        self.final_waits = []

    @staticmethod
    def _acc(ap):
        name = ap.name
        sp = str(ap.space)
        pairs = ap.ap
        off = ap.offset
        if "DRAM" in sp:
            ext = 0
            for st, c in pairs:
                ext += abs(st) * (c - 1)
            return (name, 0, 1, off, off + ext + 1)
        pst, pc = pairs[0]
        if pst == 0:
            pst = 1 << 40
        p0 = off // pst
        f0 = off % pst
        ext = 0
        for st, c in pairs[1:]:
            ext += abs(st) * (c - 1)
        esz = 4 if ap.dtype in (F32, I32) else 2
        if "PSUM" in sp:
            b0 = (f0 * esz) // 2048
            b1 = ((f0 + ext) * esz) // 2048
            return (name, p0, p0 + pc, b0 * 2048, (b1 + 1) * 2048)
        return (name, p0, p0 + pc, f0 * esz, (f0 + ext + 1) * esz)

    @staticmethod
    def _accs(ap):
        base = TK._acc(ap)
        sp = str(ap.space)
        if "DRAM" in sp or "PSUM" in sp:
            return [base]
        pairs = ap.ap
        if len(pairs) < 3:
            return [base]
        st0, c0 = pairs[1]
        if c0 <= 1 or c0 > 64 or st0 <= 0:
            return [base]
        rest = 0
        for st, c in pairs[2:]:
            if st < 0:
                return [base]
            rest += st * (c - 1)
        rest += 1
        if st0 <= rest:
            return [base]
        name, p0, p1, f0b, _ = base
        esz = 4 if ap.dtype in (F32, I32) else 2
        return [(name, p0, p1, f0b + i * st0 * esz, f0b + (i * st0 + rest) * esz) for i in range(c0)]

    def _deps(self, e, reads, writes):
        need = {}
        for (acc, isw) in [(a, False) for a in reads] + [(a, True) for a in writes]:
            name, p0, p1, f0, f1 = acc
            lst = self.recs.get(name)
            if not lst:
                continue
            psum = name.startswith("ps")
            for r in lst:
                if not (isw or r[6]):
                    if not (psum and r[4] != e):
                        continue
                if r[0] >= p1 or r[1] <= p0 or r[2] >= f1 or r[3] <= f0:
                    continue
                if e == "pe" and r[4] == "pe":
                    continue
                k, v = r[4], r[5]
                if need.get(k, 0) < v:
                    need[k] = v
        return need

    def _emit_waits(self, e, need):
        seen = self.seen[e]
        eng = self.engs[e]
        for k, v in need.items():
            if seen.get(k, 0) < v:
                eng.wait_ge(self.sems[k], v)
                seen[k] = v

    def _record(self, reads, writes, key, val):
        for acc in writes:
            name, p0, p1, f0, f1 = acc
            lst = self.recs.setdefault(name, [])
            lst[:] = [r for r in lst if not (r[0] >= p0 and r[1] <= p1 and r[2] >= f0 and r[3] <= f1)]
            lst.append([p0, p1, f0, f1, key, val, True])
        for acc in reads:
            name, p0, p1, f0, f1 = acc
            lst = self.recs.setdefault(name, [])
            for r in lst:
                if (not r[6]) and r[4] == key and r[0] == p0 and r[1] == p1 and r[2] == f0 and r[3] == f1:
                    r[5] = val
                    break
            else:
                lst.append([p0, p1, f0, f1, key, val, False])

    def op(self, e, fn, **kw):
        reads, writes = [], []
        for k, v in kw.items():
            if hasattr(v, "ap") and hasattr(v, "space"):
                if k in ("out", "accum_out", "ap"):
                    writes.extend(self._accs(v))
                else:
                    reads.extend(self._accs(v))
        need = self._deps(e, reads, writes)
        self._emit_waits(e, need)
        inst = getattr(self.engs[e], fn)(**kw)
        self.cnt[e] += 1
        inst.then_inc(self.sems[e], 1)
        self._record(reads, writes, e, self.cnt[e])
        return inst

    def dma(self, q, out, in_, final=False):
        reads = self._accs(in_)
        writes = self._accs(out)
        need = self._deps(q, reads, writes)
        lst, idx = self.dq[q]
        key = lst[idx % len(lst)]
        self.dq[q][1] = idx + 1
        if self.cnt[key] > 0:
            need[key] = max(need.get(key, 0), self.cnt[key])
        self._emit_waits(q, need)
        inst = self.engs[q].dma_start(out=out, in_=in_)
        self.cnt[key] += 16
        inst.then_inc(self.sems[key], 16)
        self._record(reads, writes, key, self.cnt[key])
        if final:
            self.final_waits.append((q, key, self.cnt[key]))

    def finish(self):
        for q, key, v in self.final_waits:
            self._emit_waits(q, {key: v})
        for e in self.engs:
            need = {f: self.cnt[f] for f in ("pe", "act", "dve", "pool", "sp") if self.cnt[f] > 0}
            self._emit_waits(e, need)


class Cfg:
    def __init__(self, seq=2048, nseq=2, s5=True, ssd=True, xa=True, ffn=True, nt=512):
        self.seq = seq
        self.nseq = nseq
        self.s5 = s5
        self.ssd = ssd
        self.xa = xa
        self.ffn = ffn
        self.nt = nt


def pc_layout():
    off = {}
    o = 0
    for name, n in (("g1", 8), ("g2", 8), ("gf", 8), ("gmem", 8), ("fcw", 132), ("fcb", 44), ("m2cw", 96),
                    ("m2cb", 24), ("m2norm", 16), ("dcol", 16), ("dtb", 32), ("alog", 32), ("eps", 1),
                    ("lre", 64), ("lim", 64), ("ldt", 64), ("s5d", 64), ("ph1", 1), ("ph2", 1), ("psi", 1),
                    ("nv", 32), ("bmul", 64)):
        off[name] = o
        o += n
    off["_n"] = o
    return off


PCO = pc_layout()
NPC = PCO["_n"]
CM_ID, CM_U, CM_NM, CM_PERM, CM_BM, CM_Z = 0, 128, 256, 384, 512, 640
CM_L2 = 640 + 8 * 256
NCM = CM_L2 + 128
SLOT = 4096
NSLOT = 4


def _wblocks(cfg):
    blocks = []

    def add(wname, K, c0, ncols, ncb, tag):
        nk = K // 128
        for i in range(0, ncols, ncb):
            blocks.append((wname, nk, c0 + i, min(ncb, ncols - i), tag))

    if cfg.s5:
        add("w_in", D, 0, 1024, 512, "u")
        for i in range(4):
            blocks.append(("s5st", 32, i, 128, "st"))
    if cfg.xa:
        add("w_in", D, P4, 512, 512, "q")
        add("w_c", 512, 0, 1024, 1024, "wc")
        add("w_in", D, P5 + 2048, 1024, 512, "gc")
    if cfg.s5:
        for i in range(4):
            blocks.append(("s5to", 32, i, 128, "to"))
        add("w_a_val", D, 0, 1024, 512, "av")
        add("w_a_gate", D, 0, 1024, 512, "ag")
        add("w_in", D, P5, 1024, 512, "ga")
    if cfg.ssd:
        add("w_in", D, P3, 32, 32, "dt")
        add("w_in", D, P1, 2048, 512, "z")
        add("w_in", D, P2, 3072, 512, "xbc")
        add("w_in", D, P5 + 1024, 1024, 512, "gb")
        add("w_b", 2048, 0, 1024, 256, "wb")
    if cfg.s5 or cfg.xa or cfg.ssd:
        add("w_out", D, 0, 1024, 512, "wo")
    if cfg.ffn:
        add("w_up", D, 0, 2 * D_FF, 512, "up")
        add("w_down", D_FF, 0, 1024, 128, "dn")
    return blocks


def build(cfg):
    nc = bass.Bass("TRN2", target_bir_lowering=False)
    es = contextlib.ExitStack()
    SEQ, NSEQ, NT = cfg.seq, cfg.nseq, cfg.nt
    TPS = SEQ // NT
    NTILES = TPS * NSEQ
    NCH = NT // 128
    NB = NT // 8
    TWO_PI = float(2 * np.pi)

    def din(name, shape, dt=F32):
        return nc.dram_tensor(name, list(shape), dt, kind="ExternalInput").ap()

    x_d = din("x", [NSEQ, SEQ, D])
    mem_d = din("mem", [NSEQ, MEM, D])
    w_d = {
        "w_in": din("w_in", [D, D_IN]),
        "w_a_val": din("w_a_val", [D, D]),
        "w_a_gate": din("w_a_gate", [D, D]),
        "w_b": din("w_b", [2048, D]),
        "w_kv": din("w_kv", [D, 1024]),
        "w_c": din("w_c", [512, D]),
        "w_out": din("w_out", [D, D]),
        "w_up": din("w_up", [D, 2 * D_FF]),
        "w_down": din("w_down", [D_FF, D]),
    }
    pc_d = din("pcols", [128, NPC])
    cm_d = din("cmat", [128, NCM])
    s5p_d = din("s5p", [128, 4, 1024])
    out_d = nc.dram_tensor("out", [NSEQ, SEQ, D], F32, kind="ExternalOutput").ap()

    blocks = _wblocks(cfg)
    NBK = len(blocks)
    wscr = nc.dram_tensor("wscr", [NBK, 128, SLOT], BF16, kind="Internal").ap()

    with es:
        tk = TK(nc, es)

        def sb(name, shape, dt=F32):
            return es.enter_context(nc.sbuf_tensor("S_" + name, list(shape), dt))

        banks = [es.enter_context(nc.psum_tensor("ps%d" % i, [128, 512], F32)) for i in range(8)]
        bank_i = [0]

        def bank():
            b = banks[bank_i[0] % 8]
            bank_i[0] += 1
            return b

        def V(fn, **kw):
            return tk.op("dve", fn, **kw)

        def G(fn, **kw):
            return tk.op("pool", fn, **kw)

        def A(**kw):
            return tk.op("act", "activation", **kw)

        def MM(**kw):
            return tk.op("pe", "matmul", **kw)

        def TR(**kw):
            return tk.op("pe", "transpose", **kw)

        ARENA = 96 * 1024
        arena = sb("arena", [128, ARENA // 2], BF16)
        ar_off = [0]

        def ar_reset(o=0):
            ar_off[0] = o

        def ar(shape, dt=F32):
            esz = 4 if dt in (F32, I32) else 2
            n = int(np.prod(shape))
            o = (ar_off[0] + 63) // 64 * 64
            assert o + n * esz <= ARENA, ("arena overflow", o, n * esz)
            ar_off[0] = o + n * esz
            v = arena[:, o // 2:o // 2 + n * esz // 2]
            if dt != BF16:
                v = v.bitcast(dt)
            if len(shape) == 2:
                return v.rearrange("p (a b) -> p a b", a=shape[0])
            if len(shape) == 3:
                return v.rearrange("p (a b c) -> p a b c", a=shape[0], b=shape[1])
            return v

        pcols = sb("pcols", [128, NPC])
        ident_f = sb("ident_f", [128, 128])
        bm_f = sb("bm_f", [128, 128])
        cb = sb("cb", [128, 4 * 128 + 512 + 8 * 256 + 128], BF16)
        ident_b, ones_b, U_b, perm_b = cb[:, 0:128], cb[:, 128:256], cb[:, 256:384], cb[:, 384:512]
        nm4_b = cb[:, 512:1024]
        Z_b = cb[:, 1024:1024 + 2048].rearrange("p (a c) -> p a c", a=8)
        L2_b = cb[:, 3072:3200]
        tk.dma("sp", pcols[:], pc_d[:, :])
        ar_reset()
        cm = ar([NCM])
        tk.dma("sp", cm, cm_d[:, :])
        V("tensor_copy", out=ident_f[:], in_=cm[:, CM_ID:CM_ID + 128])
        V("tensor_copy", out=bm_f[:], in_=cm[:, CM_BM:CM_BM + 128])
        V("tensor_copy", out=ident_b, in_=cm[:, CM_ID:CM_ID + 128])
        V("memset", ap=ones_b, constant=1.0)
        V("tensor_copy", out=U_b, in_=cm[:, CM_U:CM_U + 128])
        V("tensor_copy", out=perm_b, in_=cm[:, CM_PERM:CM_PERM + 128])
        for i in range(4):
            V("tensor_copy", out=nm4_b[:, i * 128:(i + 1) * 128], in_=cm[:, CM_NM:CM_NM + 128])
        V("tensor_copy", out=cb[:, 1024:1024 + 2048], in_=cm[:, CM_Z:CM_Z + 2048])
        V("tensor_copy", out=L2_b, in_=cm[:, CM_L2:CM_L2 + 128])

        def pc(name, i=0, n=1):
            o = PCO[name] + i
            return pcols[:, o:o + n]

        if cfg.s5:
            rbar = sb("s5_rbar", [128, 64])
            cosT = sb("s5_cos", [128, 64, NB], BF16)
            sinT = sb("s5_sin", [128, 64, NB], BF16)
            carry = sb("s5_carry", [128, 64])
            ar_reset()
            c_re = ar([64, 16]); c_im = ar([64, 16])
            bbr = ar([64, 16]); bbi = ar([64, 16])
            ctab = {}
            for key in ("L", "S1", "S2", "R", "O"):
                for sh in (0, 1):
                    ctab[(key, sh)] = ar([8, 64])
            dtg = ar([64]); lrd = ar([64]); th = ar([64]); t0_ = ar([64]); t1_ = ar([64]); t2_ = ar([64])
            arr = ar([64]); aii = ar([64]); fr = ar([64]); fi = ar([64])
            save = ar_off[0]
            b_re = ar([64, 16]); b_im = ar([64, 16]); tb = ar([64, 16])
            ANG = ar([32, 64]); MAGN = ar([32, 64])
            tmpa = ar([8, 64])
            ki = ar([1024], I32)
            kf_ = ar([1024]); tt_ = ar([1024])
            s5v = s5p_d.rearrange("p a (g k) -> p a g k", g=64)
            tk.dma("sp", b_re, s5v[:, 0])
            tk.dma("sp", b_im, s5v[:, 1])
            tk.dma("sp", c_re, s5v[:, 2])
            tk.dma("sp", c_im, s5v[:, 3])

            def sincos_reduce(dst, src, n):
                kv = ki[:, 0:n]; kf = kf_[:, 0:n]; tt = tt_[:, 0:n]
                V("tensor_scalar", out=kf, in0=src, scalar1=float(1.0 / TWO_PI), scalar2=64.0, op0=ALU.mult, op1=ALU.add)
                V("tensor_copy", out=kv, in_=kf)
                V("tensor_copy", out=kf, in_=kv)
                V("tensor_scalar", out=tt, in0=src, scalar1=float(64 * TWO_PI), scalar2=None, op0=ALU.add)
                V("scalar_tensor_tensor", out=tt, in0=kf, scalar=-TWO_PI, in1=tt, op0=ALU.mult, op1=ALU.add)
                A(out=dst, in_=tt, func=AF.Sin)

            A(out=dtg, in_=pc("ldt", 0, 64), func=AF.Exp)
            V("tensor_tensor", out=lrd, in0=pc("lre", 0, 64), in1=dtg, op=ALU.mult)
            V("tensor_tensor", out=th, in0=pc("lim", 0, 64), in1=dtg, op=ALU.mult)
            A(out=t0_, in_=lrd, func=AF.Exp)
            V("tensor_scalar", out=t1_, in0=th, scalar1=float(np.pi / 2), scalar2=None, op0=ALU.add)
            sincos_reduce(arr, t1_, 64)
            sincos_reduce(aii, th, 64)
            V("tensor_tensor", out=arr, in0=arr, in1=t0_, op=ALU.mult)
            V("tensor_tensor", out=aii, in0=aii, in1=t0_, op=ALU.mult)
            lr, li = pc("lre", 0, 64), pc("lim", 0, 64)
            V("tensor_tensor", out=t1_, in0=lr, in1=lr, op=ALU.mult)
            V("tensor_tensor", out=t2_, in0=li, in1=li, op=ALU.mult)
            V("tensor_tensor", out=t1_, in0=t1_, in1=t2_, op=ALU.add)
            V("reciprocal", out=t1_, in_=t1_)
            V("tensor_scalar", out=t0_, in0=arr, scalar1=-1.0, scalar2=None, op0=ALU.add)
            V("tensor_tensor", out=fr, in0=t0_, in1=lr, op=ALU.mult)
            V("tensor_tensor", out=t2_, in0=aii, in1=li, op=ALU.mult)
            V("tensor_tensor", out=fr, in0=fr, in1=t2_, op=ALU.add)
            V("tensor_tensor", out=fr, in0=fr, in1=t1_, op=ALU.mult)
            V("tensor_tensor", out=fi, in0=aii, in1=lr, op=ALU.mult)
            V("tensor_tensor", out=t2_, in0=t0_, in1=li, op=ALU.mult)
            V("tensor_tensor", out=fi, in0=fi, in1=t2_, op=ALU.subtract)
            V("tensor_tensor", out=fi, in0=fi, in1=t1_, op=ALU.mult)
            frb = fr.unsqueeze(2).to_broadcast([128, 64, 16])
            fib = fi.unsqueeze(2).to_broadcast([128, 64, 16])
            V("tensor_tensor", out=bbr, in0=b_re, in1=frb, op=ALU.mult)
            V("tensor_tensor", out=tb, in0=b_im, in1=fib, op=ALU.mult)
            V("tensor_tensor", out=bbr, in0=bbr, in1=tb, op=ALU.subtract)
            V("tensor_tensor", out=bbi, in0=b_im, in1=frb, op=ALU.mult)
            V("tensor_tensor", out=tb, in0=b_re, in1=fib, op=ALU.mult)
            V("tensor_tensor", out=bbi, in0=bbi, in1=tb, op=ALU.add)
            nvb = pc("nv", 0, 32).unsqueeze(2).to_broadcast([128, 32, 64])
            V("tensor_tensor", out=ANG, in0=th.unsqueeze(1).to_broadcast([128, 32, 64]), in1=nvb, op=ALU.mult)
            V("tensor_tensor", out=MAGN, in0=lrd.unsqueeze(1).to_broadcast([128, 32, 64]), in1=nvb, op=ALU.mult)
            A(out=MAGN, in_=MAGN, func=AF.Exp)
            for key, st, phn in (("L", 0, "ph1"), ("S1", 1, "ph1"), ("S2", 1, "ph2"), ("R", 2, "psi"), ("O", 3, "psi")):
                for sh in (0, 1):
                    dst = ctab[(key, sh)]
                    V("tensor_scalar", out=tmpa, in0=ANG[:, st * 8:(st + 1) * 8, :], scalar1=pc(phn),
                      scalar2=None, op0=ALU.add)
                    V("tensor_scalar", out=tmpa, in0=tmpa, scalar1=float(np.pi / 2 * (1 + sh)), scalar2=None, op0=ALU.add)
                    sincos_reduce(dst.rearrange("p a b -> p (a b)"), tmpa.rearrange("p a b -> p (a b)"), 512)
                    V("tensor_tensor", out=dst, in0=dst, in1=MAGN[:, st * 8:(st + 1) * 8, :], op=ALU.mult)
            A(out=rbar[:], in_=lrd, func=AF.Exp, scale=8.0)
            NBC = 16
            bang = ANG[:, 0:16, :].rearrange("p a b -> p (a b)")
            stmp = MAGN[:, 0:16, :].rearrange("p a b -> p (a b)")
            bang3 = bang.rearrange("p (g b) -> p g b", g=64)
            for b0 in range(0, NB, NBC):
                V("tensor_tensor", out=bang3, in0=th.unsqueeze(2).to_broadcast([128, 64, NBC]),
                  in1=pc("bmul", b0, NBC).unsqueeze(1).to_broadcast([128, 64, NBC]), op=ALU.mult)
                sincos_reduce(stmp, bang, 64 * NBC)
                V("tensor_copy", out=sinT[:, :, b0:b0 + NBC], in_=stmp.rearrange("p (g b) -> p g b", g=64))
                V("tensor_scalar", out=bang, in0=bang, scalar1=float(np.pi / 2), scalar2=None, op0=ALU.add)
                sincos_reduce(stmp, bang, 64 * NBC)
                V("tensor_copy", out=cosT[:, :, b0:b0 + NBC], in_=stmp.rearrange("p (g b) -> p g b", g=64))
            st_bi = [i for i, bl in enumerate(blocks) if bl[4] == "st"]
            to_bi = [i for i, bl in enumerate(blocks) if bl[4] == "to"]
            ar_reset(save)
            stS = ar([16, 2, 128], BF16)
            stT = ar([16, 2, 128], BF16)
            tq = ar([8, 8, 16])
            tq2 = ar([8, 8, 16])
            tm = ar([4, 128])
            tabs = {key: ar([8, 8, 16]) for key in ("L", "S1", "S2", "R", "O")}
            for bt in range(4):
                for sub in range(2):
                    g0 = bt * 16 + sub * 8
                    for key, P_re, P_im in (("L", bbr, bbi), ("S1", bbr, bbi), ("S2", bbr, bbi), ("R", c_re, c_im), ("O", c_re, c_im)):
                        t4 = tabs[key]
                        c0 = ctab[(key, 0)][:, :, g0:g0 + 8].rearrange("p n g -> p g n").unsqueeze(3).to_broadcast([128, 8, 8, 16])
                        c1 = ctab[(key, 1)][:, :, g0:g0 + 8].rearrange("p n g -> p g n").unsqueeze(3).to_broadcast([128, 8, 8, 16])
                        pr = P_re[:, g0:g0 + 8, :].unsqueeze(2).to_broadcast([128, 8, 8, 16])
                        pi_ = P_im[:, g0:g0 + 8, :].unsqueeze(2).to_broadcast([128, 8, 8, 16])
                        EW = G if key in ("S1", "S2", "O") else V
                        tqq = tq2 if key in ("S1", "S2", "O") else tq
                        EW("tensor_tensor", out=t4, in0=pr, in1=c0, op=ALU.mult)
                        EW("tensor_tensor", out=tqq, in0=pi_, in1=c1, op=ALU.mult)
                        EW("tensor_tensor", out=t4, in0=t4, in1=tqq, op=ALU.add)
                    for q in range(2):
                        b = bank()
                        for gg in range(4):
                            gl = q * 4 + gg
                            MM(out=b[:, gg * 128:(gg + 1) * 128], lhsT=tabs["L"][:, gl].rearrange("p a b -> p (a b)"),
                               rhs=tabs["R"][:, gl].rearrange("p a b -> p (a b)"), start=True, stop=True)
                        V("tensor_tensor", out=tm, in0=b[:].rearrange("p (a c) -> p a c", a=4),
                          in1=bm_f[:].unsqueeze(1).to_broadcast([128, 4, 128]), op=ALU.mult)
                        for gg in range(4):
                            gl = q * 4 + gg
                            V("scalar_tensor_tensor", out=stT[:, sub * 8 + gl, 0, :], in0=ident_f[:], scalar=pc("s5d", g0 + gl),
                              in1=tm[:, gg, :], op0=ALU.mult, op1=ALU.add)
                        for si_, key in ((0, "S1"), (1, "S2")):
                            b2 = bank()
                            for gg in range(4):
                                gl = q * 4 + gg
                                TR(out=b2[:, gg * 128:(gg + 1) * 128], in_=tabs[key][:, gl].rearrange("p a b -> p (a b)"),
                                   identity=ident_f[:])
                            A(out=stS[:, sub * 8 + q * 4:sub * 8 + (q + 1) * 4, si_, :],
                              in_=b2[:].rearrange("p (a c) -> p a c", a=4), func=AF.Copy)
                    V("tensor_copy", out=stT[:, sub * 8:(sub + 1) * 8, 1, :], in_=tabs["O"].rearrange("p g a b -> p g (a b)"))
                tk.dma("sp", wscr[st_bi[bt], :, :], stS.rearrange("p a b c -> p (a b c)"))
                tk.dma("sp", wscr[to_bi[bt], :, :], stT.rearrange("p a b c -> p (a b c)"))

        xT = sb("xT", [128, 8, NT])
        hT = sb("hT", [128, 8, NT], BF16)
        mrg = sb("mrg", [128, 8, NT], BF16)
        slots = [sb("wslot%d" % i, [128, SLOT], BF16) for i in range(NSLOT)]
        if cfg.ffn:
            fhalo = sb("fhalo", [128, 44, 2], BF16)
        if cfg.xa:
            KT = sb("KT", [128, 4, NSEQ * MEM], BF16)
            Vm = sb("Vm", [128, NSEQ * 2, 512], BF16)
        if cfg.ssd:
            hs = sb("ssd_hs", [128, 2048])
            hsb = sb("ssd_hsb", [128, 2048], BF16)
            mhalo = sb("mhalo", [128, 24, 3], BF16)
            Aneg = sb("Aneg", [128, 32])
            A(out=Aneg[:], in_=pc("alog", 0, 32), func=AF.Exp)
            V("tensor_scalar", out=Aneg[:], in0=Aneg[:], scalar1=-1.0, scalar2=None, op0=ALU.mult)

        stream = {"next_issue": 0, "next_use": 0}
        total_blocks = NBK * NTILES

        def issue_upto(n):
            while stream["next_issue"] < min(n, total_blocks):
                i = stream["next_issue"]
                bi = i % NBK
                wname, nk, c0, ncb, tag = blocks[bi]
                sl = slots[i % NSLOT][:, 0:nk * ncb]
                if i < NBK and not wname.startswith("s5"):
                    src = w_d[wname][:, c0:c0 + ncb].rearrange("(k p) c -> p k c", p=128)
                    tk.dma("pool", sl.rearrange("p (k c) -> p k c", k=nk), src)
                    if NTILES > 1:
                        tk.dma("sp", wscr[bi, :, 0:nk * ncb], sl)
                else:
                    tk.dma("sp", sl, wscr[bi, :, 0:nk * ncb])
                stream["next_issue"] += 1

        def next_block(tag):
            i = stream["next_use"]
            bi = i % NBK
            wname, nk, c0, ncb, btag = blocks[bi]
            assert btag == tag, (btag, tag)
            issue_upto(i + NSLOT)
            stream["next_use"] += 1
            return slots[i % NSLOT][:, 0:nk * ncb].rearrange("p (k c) -> p k c", k=nk), nk, ncb

        def proj(tag, ncols, rhs_fn, nk, evac, n=NT):
            m = 0
            done = 0
            while done < ncols:
                wv, wnk, ncb = next_block(tag)
                assert wnk == nk
                for mm in range(ncb // 128):
                    b = bank()
                    for k in range(nk):
                        MM(out=b[:, 0:n], lhsT=wv[:, k, mm * 128:(mm + 1) * 128], rhs=rhs_fn(k),
                           start=(k == 0), stop=(k == nk - 1))
                    evac(m, b)
                    m += 1
                done += ncb

        def rms_stats(src_fn, nct, n, sq, denom):
            for ct in range(nct):
                A(out=sq[:, ct, 0:n], in_=src_fn(ct), func=AF.Square)
            b = bank()
            for ct in range(nct):
                MM(out=b[:, 0:n], lhsT=ones_b, rhs=sq[:, ct, 0:n], start=(ct == 0), stop=(ct == nct - 1))
            rs = sq[:, 0:2, :].rearrange("p a b -> p (a b)").bitcast(F32)[:, 0:n]
            A(out=rs, in_=b[:, 0:n], func=AF.Sqrt, scale=1.0 / denom, bias=pc("eps"))
            V("reciprocal", out=rs, in_=rs)
            return rs

        def rmsnorm_to_hT(gname):
            sq = ar([8, NT], BF16)
            rstd = rms_stats(lambda ct: xT[:, ct, :], 8, NT, sq, D)
            for ct in range(8):
                V("scalar_tensor_tensor", out=hT[:, ct, :], in0=xT[:, ct, :], scalar=pc(gname, ct), in1=rstd,
                  op0=ALU.mult, op1=ALU.mult)

        n_merged = [0]

        def merge(gtag, ysrc):
            first = (n_merged[0] == 0)
            n_merged[0] += 1
            sgs = [ar([NT]) for _ in range(2)]
            tmps = [ar([NT], BF16) for _ in range(2)]

            def ev(m, b):
                sg = sgs[m % 2]
                A(out=sg, in_=b[:], func=AF.Sigmoid)
                if first:
                    G("tensor_tensor", out=mrg[:, m, :], in0=sg, in1=ysrc[:, m, :], op=ALU.mult)
                else:
                    tp = tmps[m % 2]
                    G("tensor_tensor", out=tp, in0=sg, in1=ysrc[:, m, :], op=ALU.mult)
                    G("tensor_tensor", out=mrg[:, m, :], in0=mrg[:, m, :], in1=tp, op=ALU.add)
            proj(gtag, 1024, lambda k: hT[:, k, :], 8, ev)

        if cfg.xa:
            ar_reset()
            NM = NSEQ * MEM
            memin = ar([NSEQ * 2, D])
            memT = ar([8, NM])
            memTn = ar([8, NM], BF16)
            sqm = ar([8, NM], BF16)
            wkv = [ar([8, 512], BF16) for _ in range(2)]
            tk.dma("act", memin, mem_d.rearrange("s (t p) d -> p (s t) d", p=128))
            for i in range(2):
                tk.dma("pool", wkv[i], w_d["w_kv"][:, i * 512:(i + 1) * 512].rearrange("(k p) c -> p k c", p=128))
            for ct in range(8):
                b = bank()
                for s in range(NSEQ * 2):
                    TR(out=b[:, s * 128:(s + 1) * 128], in_=memin[:, s, ct * 128:(ct + 1) * 128], identity=ident_f[:])
                A(out=memT[:, ct, :], in_=b[:, 0:NM], func=AF.Copy)
            rstd = rms_stats(lambda ct: memT[:, ct, :], 8, NM, sqm, D)
            for ct in range(8):
                V("scalar_tensor_tensor", out=memTn[:, ct, :], in0=memT[:, ct, :], scalar=pc("gmem", ct),
                  in1=rstd, op0=ALU.mult, op1=ALU.mult)
            for h in range(4):
                b = bank()
                for k in range(8):
                    MM(out=b[:, 0:NM], lhsT=wkv[0][:, k, h * 128:(h + 1) * 128], rhs=memTn[:, k, :],
                       start=(k == 0), stop=(k == 7))
                A(out=KT[:, h, :], in_=b[:, 0:NM], func=AF.Copy)
            for mt in range(NSEQ * 2):
                b = bank()
                for k in range(8):
                    MM(out=b[:], lhsT=memTn[:, k, mt * 128:(mt + 1) * 128], rhs=wkv[1][:, k, :],
                       start=(k == 0), stop=(k == 7))
                A(out=Vm[:, mt, :], in_=b[:], func=AF.Copy)

        xin_next = [None]
        pending_tail = [None]
        for ti in range(NTILES):
            s_i = ti // TPS
            t0 = (ti % TPS) * NT
            first = (ti % TPS == 0)
            n_merged[0] = 0
            ar_reset()
            if ti == 0:
                xin = ar([NCH, D])
                tk.dma("act", xin, x_d[s_i, t0:t0 + NT, :].rearrange("(s p) d -> p s d", p=128))
            else:
                xin = xin_next[0]
            for ct in range(8):
                b = bank()
                for s in range(NCH):
                    TR(out=b[:, s * 128:(s + 1) * 128], in_=xin[:, s, ct * 128:(ct + 1) * 128], identity=ident_f[:])
                A(out=xT[:, ct, :], in_=b[:, 0:NT], func=AF.Copy)
            rmsnorm_to_hT("g1")
            if pending_tail[0] is not None:
                pending_tail[0]()
                pending_tail[0] = None

            if cfg.s5:
                ar_reset()
                uT = ar([8, NT], BF16)
                U2 = ar([8, 8, NB], BF16)
                o_st = ar_off[0]
                St = ar([64, NB])
                o_after = ar_off[0]
                ar_reset(o_st)
                Yg = ar([8, 8, NB], BF16)
                ar_reset(o_after)
                o_wS = ar_off[0]
                wS = ar([64, NB])
                wSb = ar([64, NB], BF16)
                sprev = ar([64, NB + 1], BF16)
                gT = uT
                tA = [ar([8, NB]) for _ in range(2)]
                tB = [ar([8, NB]) for _ in range(2)]
                o_xa = ar_off[0]

                def ev_u(m, b):
                    A(out=uT[:, m, :].rearrange("p (j b) -> p j b", j=8), in_=b[:].rearrange("p (b j) -> p j b", j=8),
                      func=AF.Copy)
                proj("u", 1024, lambda k: hT[:, k, :], 8, ev_u)
                if first:
                    V("memset", ap=carry[:], constant=0.0)
                uTv = uT.rearrange("p c (j b) -> p c b j", j=8)
                for g8 in range(8):
                    b = bank()
                    bv = b[:, 0:8 * NB].rearrange("p (c b) -> p c b", c=8)
                    for j in range(8):
                        MM(out=bv, lhsT=Z_b[:, g8, 128 - 16 * j:256 - 16 * j], rhs=uTv[:, :, :, j],
                           start=(j == 0), stop=(j == 7))
                    V("tensor_copy", out=U2[:, :, g8, :], in_=bv)
                for blk in range(4):
                    wv, _, _ = next_block("st")
                    wv4 = wv.rearrange("p (g s) c -> p g s c", s=2)
                    for half in range(2):
                        bx, by = bank(), bank()
                        for gi in range(8):
                            gl = half * 8 + gi
                            g = blk * 16 + gl
                            rhs = U2[:, g // 8, g % 8, :]
                            MM(out=bx[:, gi * NB:(gi + 1) * NB], lhsT=wv4[:, gl, 0, :], rhs=rhs, start=True, stop=True)
                            MM(out=by[:, gi * NB:(gi + 1) * NB], lhsT=wv4[:, gl, 1, :], rhs=rhs, start=True, stop=True)
                        gs = blk * 16 + half * 8
                        ta, tb_ = tA[half], tB[half]
                        V("tensor_tensor", out=ta, in0=bx[:, 0:8 * NB].rearrange("p (g b) -> p g b", g=8),
                          in1=cosT[:, gs:gs + 8, :], op=ALU.mult)
                        V("tensor_tensor", out=tb_, in0=by[:, 0:8 * NB].rearrange("p (g b) -> p g b", g=8),
                          in1=sinT[:, gs:gs + 8, :], op=ALU.mult)
                        G("tensor_tensor", out=St[:, gs:gs + 8, :], in0=ta, in1=tb_, op=ALU.add)
            if cfg.xa:
                o_x0 = o_xa if cfg.s5 else 0
                ar_reset(o_x0)
                qT = ar([4, NT], BF16)
                PT = [ar([2, NT], BF16) for _ in range(2)]
                oT = ar([4, NT], BF16)
                rden = [ar([NT]) for _ in range(2)]
                o_x3 = ar_off[0]
                ar_reset(o_x0)
                ytmp = ar([8, NT], BF16)
                ar_reset(o_x3)

                def ev_q(m, b):
                    A(out=qT[:, m, :], in_=b[:], func=AF.Copy)
                proj("q", 512, lambda k: hT[:, k, :], 8, ev_q)
                for h in range(4):
                    pt = PT[h % 2]
                    for mt in range(2):
                        b = bank()
                        MM(out=b[:], lhsT=KT[:, h, s_i * MEM + mt * 128:s_i * MEM + (mt + 1) * 128], rhs=qT[:, h, :],
                           start=True, stop=True)
                        A(out=pt[:, mt, :], in_=b[:], func=AF.Exp, scale=float(128 ** -0.5))
                    bd = bank()
                    for mt in range(2):
                        MM(out=bd[:], lhsT=ones_b, rhs=pt[:, mt, :], start=(mt == 0), stop=(mt == 1))
                    bo = bank()
                    for mt in range(2):
                        MM(out=bo[:], lhsT=Vm[:, s_i * 2 + mt, h * 128:(h + 1) * 128], rhs=pt[:, mt, :],
                           start=(mt == 0), stop=(mt == 1))
                    V("reciprocal", out=rden[h % 2], in_=bd[:])
                    V("tensor_tensor", out=oT[:, h, :], in0=bo[:], in1=rden[h % 2], op=ALU.mult)

                def ev_wc(m, b):
                    A(out=ytmp[:, m, :], in_=b[:], func=AF.Copy)
                proj("wc", 1024, lambda k: oT[:, k, :], 4, ev_wc)
                merge("gc", ytmp)

            if cfg.s5:
                for g in range(64):
                    V("tensor_tensor_scan", out=wS[:, g, :], data0=rbar[:, g:g + 1].to_broadcast([128, NB]),
                      data1=St[:, g, :], initial=carry[:, g:g + 1], op0=ALU.mult, op1=ALU.add)
                A(out=wSb, in_=wS, func=AF.Copy)
                A(out=sprev[:, :, 0], in_=carry[:], func=AF.Copy)
                wSbf = wSb.rearrange("p g b -> p (g b)")
                for q in range(8):
                    b = bank()
                    MM(out=b[:, 0:8 * NB], lhsT=perm_b, rhs=wSbf[:, q * 8 * NB:(q + 1) * 8 * NB], start=True, stop=True)
                    ta, tb_ = tA[q % 2], tB[q % 2]
                    V("tensor_tensor", out=ta, in0=b[:, 0:8 * NB].rearrange("p (g b) -> p g b", g=8),
                      in1=sinT[:, q * 8:(q + 1) * 8, :], op=ALU.mult)
                    G("tensor_tensor", out=tb_, in0=wS[:, q * 8:(q + 1) * 8, :], in1=cosT[:, q * 8:(q + 1) * 8, :], op=ALU.mult)
                    V("tensor_tensor", out=sprev[:, q * 8:(q + 1) * 8, 1:NB + 1], in0=tb_, in1=ta, op=ALU.subtract)
                    V("tensor_tensor", out=carry[:, q * 8:(q + 1) * 8], in0=tb_[:, :, NB - 1], in1=ta[:, :, NB - 1], op=ALU.subtract)
                for blk in range(4):
                    wv, _, _ = next_block("to")
                    wv4 = wv.rearrange("p (g s) c -> p g s c", s=2)
                    for half in range(2):
                        b = bank()
                        for gi in range(8):
                            gl = half * 8 + gi
                            g = blk * 16 + gl
                            MM(out=b[:, gi * NB:(gi + 1) * NB], lhsT=wv4[:, gl, 0, :], rhs=U2[:, g // 8, g % 8, :],
                               start=True, stop=False)
                            MM(out=b[:, gi * NB:(gi + 1) * NB], lhsT=wv4[:, gl, 1, :], rhs=sprev[:, g, 0:NB],
                               start=False, stop=True)
                        ct = (blk * 16 + half * 8) // 8
                        A(out=Yg[:, ct, :, :], in_=b[:, 0:8 * NB].rearrange("p (g b) -> p g b", g=8), func=AF.Gelu)
                gTv = gT.rearrange("p c (j b) -> p c b j", j=8)
                for t in range(8):
                    b = bank()
                    bv = b[:, 0:8 * NB].rearrange("p (c b) -> p c b", c=8)
                    for g8 in range(8):
                        MM(out=bv, lhsT=Z_b[:, t, 128 - 16 * g8:256 - 16 * g8], rhs=Yg[:, :, g8, :],
                           start=(g8 == 0), stop=(g8 == 7))
                    A(out=gTv[:, :, :, t], in_=bv, func=AF.Copy)

                ar_reset(o_wS)
                ytmp = ar([8, NT], BF16)

                def ev_av(m, b):
                    A(out=ytmp[:, m, :].rearrange("p (b j) -> p b j", j=8), in_=b[:].rearrange("p (j b) -> p b j", j=8),
                      func=AF.Copy)
                proj("av", 1024, lambda k: gT[:, k, :], 8, ev_av)
                sg2 = [ar([NT], BF16) for _ in range(2)]

                def ev_ag(m, b):
                    A(out=sg2[m % 2].rearrange("p (b j) -> p b j", j=8), in_=b[:].rearrange("p (j b) -> p b j", j=8),
                      func=AF.Sigmoid)
                    G("tensor_tensor", out=ytmp[:, m, :], in0=ytmp[:, m, :], in1=sg2[m % 2], op=ALU.mult)
                proj("ag", 1024, lambda k: gT[:, k, :], 8, ev_ag)
                merge("ga", ytmp)

            if cfg.ssd:
                ar_reset()
                sz = ar([16, NT], BF16)
                xbcs = ar([24, NT], BF16)
                o_ssd = ar_off[0]
                dtt = ar([NCH, 32]); dA = ar([NCH, 32]); dtd = ar([NCH, 32]); cdb2 = ar([2, 32])
                dAb = ar([NCH, 32], BF16); ndAb = ar([NCH, 32], BF16)
                raw = [ar([NT + 3], BF16) for _ in range(3)]
                ctm = [ar([NT]) for _ in range(3)]
                Gs = ar([4, 128])
                E4 = [ar([4, 128]) for _ in range(2)]
                EA4 = [ar([4, 128]) for _ in range(2)]
                Mall = ar([32, 128], BF16)
                Cdall = ar([32, 128], BF16)
                xdt = ar([2048], BF16)
                xdtd = ar([2048], BF16)
                Btok = ar([512], BF16)
                ytm = [ar([4, 128], BF16) for _ in range(2)]
                ytm2 = [ar([4, 128]) for _ in range(2)]
                wv, _, _ = next_block("dt")
                bdt = bank()
                for c in range(NCH):
                    for k in range(8):
                        MM(out=bdt[:, c * 32:(c + 1) * 32], lhsT=hT[:, k, c * 128:(c + 1) * 128], rhs=wv[:, k, :],
                           start=(k == 0), stop=(k == 7))
                V("tensor_tensor", out=dtt, in0=bdt[:, 0:NCH * 32].rearrange("p (c h) -> p c h", c=NCH),
                  in1=pc("dtb", 0, 32).unsqueeze(1).to_broadcast([128, NCH, 32]), op=ALU.add)
                A(out=dtt, in_=dtt, func=AF.Exp)
                A(out=dtt, in_=dtt, func=AF.Ln, bias=1.0)
                V("tensor_tensor", out=dA, in0=dtt, in1=Aneg[:].unsqueeze(1).to_broadcast([128, NCH, 32]), op=ALU.mult)
                V("tensor_copy", out=dAb, in_=dA)
                V("tensor_scalar", out=ndAb, in0=dA, scalar1=-1.0, scalar2=None, op0=ALU.mult)

                def ev_z(m, b):
                    A(out=sz[:, m, :], in_=b[:], func=AF.Silu)
                proj("z", 2048, lambda k: hT[:, k, :], 8, ev_z)
                if first:
                    G("memset", ap=mhalo[:], constant=0.0)

                pend = []

                def ev_xbc(m, b):
                    r = raw[m % 3]
                    tm = ctm[m % 3]
                    G("tensor_copy", out=r[:, 0:3], in_=mhalo[:, m, :])
                    A(out=r[:, 3:NT + 3], in_=b[:], func=AF.Copy)
                    G("tensor_copy", out=mhalo[:, m, :], in_=r[:, NT:NT + 3])
                    G("tensor_scalar", out=tm, in0=r[:, 0:NT], scalar1=pc("m2cw", m), scalar2=pc("m2cb", m),
                      op0=ALU.mult, op1=ALU.add)
                    for kk in range(1, 4):
                        V("scalar_tensor_tensor", out=tm, in0=r[:, kk:NT + kk], scalar=pc("m2cw", 24 * kk + m), in1=tm,
                          op0=ALU.mult, op1=ALU.add)
                    while pend:
                        pend.pop(0)()
                    pend.append(lambda m=m, tm=tm: A(out=xbcs[:, m, :], in_=tm, func=AF.Silu))
                proj("xbc", 3072, lambda k: hT[:, k, :], 8, ev_xbc)
                while pend:
                    pend.pop(0)()

                for c in range(NCH):
                    cs = slice(c * 128, (c + 1) * 128)
                    firstc = first and c == 0
                    cdb = cdb2[:, c % 2, :]
                    bD = bank()
                    MM(out=bD[:, 0:32], lhsT=L2_b, rhs=dAb[:, c, :], start=True, stop=True)
                    MM(out=bD[:, 32:64], lhsT=ones_b, rhs=dAb[:, c, :], start=True, stop=True)
                    A(out=dtd[:, c, :], in_=bD[:, 0:32], func=AF.Exp)
                    A(out=cdb, in_=bD[:, 32:64], func=AF.Exp)
                    V("tensor_tensor", out=dtd[:, c, :], in0=dtd[:, c, :], in1=dtt[:, c, :], op=ALU.mult)
                    bG = bank()
                    for g in range(4):
                        MM(out=bG[:, g * 128:(g + 1) * 128], lhsT=xbcs[:, 16 + g, cs], rhs=xbcs[:, 20 + g, cs],
                           start=True, stop=True)
                    A(out=Gs, in_=bG[:].rearrange("p (g l) -> p g l", g=4), func=AF.Copy)
                    for q in range(4):
                        bT = bank()
                        bTb = bT[:].bitcast(BF16)
                        for i in range(4):
                            TR(out=bTb[:, i * 128:(i + 1) * 128], in_=xbcs[:, q * 4 + i, cs], identity=ident_b)
                        src = bTb[:, 0:512].rearrange("p (h d) -> p h d", h=8)
                        V("tensor_tensor", out=xdt[:, q * 512:(q + 1) * 512].rearrange("p (h d) -> p h d", h=8), in0=src,
                          in1=dtt[:, c, q * 8:(q + 1) * 8].unsqueeze(2).to_broadcast([128, 8, 64]), op=ALU.mult)
                        V("tensor_tensor", out=xdtd[:, q * 512:(q + 1) * 512].rearrange("p (h d) -> p h d", h=8), in0=src,
                          in1=dtd[:, c, q * 8:(q + 1) * 8].unsqueeze(2).to_broadcast([128, 8, 64]), op=ALU.mult)
                    bT = bank()
                    bTb = bT[:].bitcast(BF16)
                    for g in range(4):
                        TR(out=bTb[:, g * 128:(g + 1) * 128], in_=xbcs[:, 16 + g, cs], identity=ident_b)
                    A(out=Btok, in_=bTb[:, 0:512], func=AF.Copy)

                    ybank = {}

                    def emit_y(hq):
                        q = hq // 2
                        if hq % 2 == 0:
                            ybank[q] = bank()
                        bY = ybank[q]
                        for pp in range(2):
                            pi_ = (hq % 2) * 2 + pp
                            pr = q * 4 + pi_
                            for hh in range(2):
                                h = 2 * pr + hh
                                o = bY[hh * 64:(hh + 1) * 64, pi_ * 128:(pi_ + 1) * 128]
                                MM(out=o, lhsT=xdt[:, h * 64:(h + 1) * 64], rhs=Mall[:, h, :], start=True, stop=firstc)
                                if not firstc:
                                    MM(out=o, lhsT=hsb[:, h * 64:(h + 1) * 64], rhs=Cdall[:, h, :], start=False, stop=True)
                        if hq % 2 == 1:
                            t1, t2 = ytm[q % 2], ytm2[q % 2]
                            for p4 in range(4):
                                A(out=t1[:, p4, :], in_=xbcs[:, q * 4 + p4, cs], func=AF.Copy, scale=pc("dcol", q * 4 + p4))
                            V("tensor_tensor", out=t2, in0=bY[:].rearrange("p (a l) -> p a l", a=4), in1=t1, op=ALU.add)
                            G("tensor_tensor", out=xbcs[:, q * 4:(q + 1) * 4, cs], in0=t2, in1=sz[:, q * 4:(q + 1) * 4, cs],
                              op=ALU.mult)

                    for hq in range(8):
                        h0 = hq * 4
                        g = hq // 2
                        bE, bA = bank(), bank()
                        MM(out=bE[:].rearrange("p (h l) -> p h l", h=4), lhsT=U_b,
                           rhs=ndAb[:, c, h0:h0 + 4].unsqueeze(2).to_broadcast([128, 4, 128]), start=True, stop=False)
                        MM(out=bE[:], lhsT=ident_b, rhs=nm4_b, start=False, stop=False)
                        for hh in range(4):
                            MM(out=bE[:, hh * 128:(hh + 1) * 128], lhsT=dAb[:, c, h0 + hh:h0 + hh + 1].to_broadcast([128, 128]),
                               rhs=U_b, start=False, stop=(hh == 3))
                        for hh in range(4):
                            MM(out=bA[:, hh * 128:(hh + 1) * 128], lhsT=dAb[:, c, h0 + hh:h0 + hh + 1].to_broadcast([128, 128]),
                               rhs=U_b, start=True, stop=True)
                        e4, ea4 = E4[hq % 2], EA4[hq % 2]
                        A(out=e4, in_=bE[:].rearrange("p (h l) -> p h l", h=4), func=AF.Exp)
                        A(out=ea4, in_=bA[:].rearrange("p (h l) -> p h l", h=4), func=AF.Exp)
                        V("tensor_tensor", out=Mall[:, h0:h0 + 4, :], in0=e4,
                          in1=Gs[:, g, :].unsqueeze(1).to_broadcast([128, 4, 128]), op=ALU.mult)
                        (V if hq % 2 == 0 else G)("tensor_tensor", out=Cdall[:, h0:h0 + 4, :], in0=ea4,
                          in1=xbcs[:, 20 + g, cs].unsqueeze(1).to_broadcast([128, 4, 128]), op=ALU.mult)
                        if hq >= 1:
                            emit_y(hq - 1)
                    emit_y(7)

                    for g in range(4):
                        bS = bank()
                        MM(out=bS[:], lhsT=Btok[:, g * 128:(g + 1) * 128], rhs=xdtd[:, g * 512:(g + 1) * 512], start=True, stop=True)
                        hv = hs[:, g * 512:(g + 1) * 512]
                        if firstc:
                            V("tensor_copy", out=hv, in_=bS[:])
                        else:
                            G("tensor_tensor", out=hv.rearrange("p (h d) -> p h d", h=8), in0=hv.rearrange("p (h d) -> p h d", h=8),
                              in1=cdb[:, g * 8:(g + 1) * 8].unsqueeze(2).to_broadcast([128, 8, 64]), op=ALU.mult)
                            V("tensor_tensor", out=hv, in0=bS[:], in1=hv, op=ALU.add)
                        A(out=hsb[:, g * 512:(g + 1) * 512], in_=hv, func=AF.Copy)
                ar_reset(o_ssd)
                ytmp = ar([8, NT], BF16)
                sgb = ar([8, NT], BF16)
                for ct in range(16):
                    A(out=sz[:, ct, :], in_=xbcs[:, ct, :], func=AF.Square)

                def ev_gb(m, b):
                    A(out=sgb[:, m, :], in_=b[:], func=AF.Sigmoid)
                proj("gb", 1024, lambda k: hT[:, k, :], 8, ev_gb)
                bn = bank()
                for ct in range(16):
                    MM(out=bn[:], lhsT=ones_b, rhs=sz[:, ct, :], start=(ct == 0), stop=(ct == 15))
                rstd = sz[:, 0:2, :].rearrange("p a b -> p (a b)").bitcast(F32)
                A(out=rstd, in_=bn[:], func=AF.Sqrt, scale=1.0 / 2048, bias=pc("eps"))
                V("reciprocal", out=rstd, in_=rstd)
                for ct in range(16):
                    V("scalar_tensor_tensor", out=xbcs[:, ct, :], in0=xbcs[:, ct, :], scalar=pc("m2norm", ct), in1=rstd,
                      op0=ALU.mult, op1=ALU.mult)
                first_m = (n_merged[0] == 0)
                n_merged[0] += 1
                mtmp = [ar([NT], BF16) for _ in range(2)]

                def ev_wb(m, b):
                    A(out=ytmp[:, m, :], in_=b[:], func=AF.Copy)
                    if first_m:
                        G("tensor_tensor", out=mrg[:, m, :], in0=sgb[:, m, :], in1=ytmp[:, m, :], op=ALU.mult)
                    else:
                        G("tensor_tensor", out=mtmp[m % 2], in0=sgb[:, m, :], in1=ytmp[:, m, :], op=ALU.mult)
                        G("tensor_tensor", out=mrg[:, m, :], in0=mrg[:, m, :], in1=mtmp[m % 2], op=ALU.add)
                proj("wb", 1024, lambda k: xbcs[:, k, :], 16, ev_wb)

            if n_merged[0] > 0:
                def ev_wo(m, b):
                    V("tensor_tensor", out=xT[:, m, :], in0=b[:], in1=xT[:, m, :], op=ALU.add)
                proj("wo", 1024, lambda k: mrg[:, k, :], 8, ev_wo)

            if ti + 1 < NTILES:
                ar_reset(40 * 1024)
                xin_next[0] = ar([NCH, D])
                ns_i, nt0 = (ti + 1) // TPS, ((ti + 1) % TPS) * NT
                tk.dma("act", xin_next[0], x_d[ns_i, nt0:nt0 + NT, :].rearrange("(s p) d -> p s d", p=128))
            if cfg.ffn:
                ar_reset()
                rmsnorm_to_hT("g2")
                ar_reset()
                gact = ar([22, NT], BF16)
                raw = [ar([NT + 2], BF16) for _ in range(3)]
                ctm = [ar([NT]) for _ in range(3)]
                if first:
                    G("memset", ap=fhalo[:], constant=0.0)

                pend = []

                def ev_up(m, b):
                    r = raw[m % 3]
                    tm = ctm[m % 3]
                    G("tensor_copy", out=r[:, 0:2], in_=fhalo[:, m, :])
                    A(out=r[:, 2:NT + 2], in_=b[:], func=AF.Copy)
                    G("tensor_copy", out=fhalo[:, m, :], in_=r[:, NT:NT + 2])
                    G("tensor_scalar", out=tm, in0=r[:, 0:NT], scalar1=pc("fcw", m), scalar2=pc("fcb", m),
                      op0=ALU.mult, op1=ALU.add)
                    for kk in range(1, 3):
                        V("scalar_tensor_tensor", out=tm, in0=r[:, kk:NT + kk], scalar=pc("fcw", 44 * kk + m), in1=tm,
                          op0=ALU.mult, op1=ALU.add)
                    while pend:
                        pend.pop(0)()
                    if m < 22:
                        pend.append(lambda m=m, tm=tm: A(out=gact[:, m, :], in_=tm, func=AF.Silu))
                    else:
                        pend.append(lambda m=m, tm=tm: V("tensor_tensor", out=gact[:, m - 22, :], in0=gact[:, m - 22, :],
                                                         in1=tm, op=ALU.mult))
                proj("up", 2 * D_FF, lambda k: hT[:, k, :], 8, ev_up)
                while pend:
                    pend.pop(0)()

                def ev_dn(m, b):
                    V("tensor_tensor", out=xT[:, m, :], in0=b[:], in1=xT[:, m, :], op=ALU.add)
                proj("dn", 1024, lambda k: gact[:, k, :], 22, ev_dn)

            ar_reset(56 * 1024)
            sqf = ar([8, NT], BF16)
            xout = ar([NCH, D])
            xfin = ar([8, NT])
            rstd = rms_stats(lambda ct: xT[:, ct, :], 8, NT, sqf, D)
            for ct in range(8):
                V("scalar_tensor_tensor", out=xfin[:, ct, :], in0=xT[:, ct, :], scalar=pc("gf", ct), in1=rstd,
                  op0=ALU.mult, op1=ALU.mult)

            def final_tail(xout=xout, xfin=xfin, s_i=s_i, t0=t0):
                for s in range(NCH):
                    for half in range(2):
                        b = bank()
                        for c4 in range(4):
                            ct = half * 4 + c4
                            TR(out=b[:, c4 * 128:(c4 + 1) * 128], in_=xfin[:, ct, s * 128:(s + 1) * 128], identity=ident_f[:])
                        A(out=xout[:, s, half * 512:(half + 1) * 512], in_=b[:], func=AF.Copy)
                tk.dma("sp", out_d[s_i, t0:t0 + NT, :].rearrange("(s p) d -> p s d", p=128), xout, final=True)
            pending_tail[0] = final_tail

        pending_tail[0]()
        tk.finish()
    return nc


def _cols(v):
    v = np.asarray(v, np.float32).reshape(-1, 128)
    return np.ascontiguousarray(v.T)


def _const_mats():
    cm = np.zeros((128, NCM), np.float32)
    r = np.arange(128)
    cm[:, CM_ID:CM_ID + 128] = np.eye(128)
    cm[:, CM_U:CM_U + 128] = (r[:, None] <= r[None, :])
    cm[:, CM_NM:CM_NM + 128] = np.where(r[None, :] < r[:, None], -30000.0, 0.0)
    pm = np.zeros((128, 128), np.float32)
    for rp in range(64):
        pm[rp + 64, rp] = 1.0
        pm[rp, rp + 64] = -1.0
    cm[:, CM_PERM:CM_PERM + 128] = pm
    cm[:, CM_BM:CM_BM + 128] = ((r[None, :] // 16) >= (r[:, None] // 16))
    cm[:, CM_L2:CM_L2 + 128] = (r[:, None] > r[None, :])
    for a in range(8):
        z = np.zeros((128, 256), np.float32)
        for k in range(16):
            z[16 * a + k, 128 + k] = 1.0
        cm[:, CM_Z + 256 * a:CM_Z + 256 * (a + 1)] = z
    return cm


def make_inputs(cfg, inp, b0):
    pcv = np.zeros((128, NPC), np.float32)

    def put(name, arr, i=0):
        arr = np.asarray(arr, np.float32)
        pcv[:, PCO[name] + i:PCO[name] + i + arr.shape[1]] = arr

    put("g1", _cols(inp["norm_mix"][0]))
    put("g2", _cols(inp["norm_ffn"][0]))
    put("gf", _cols(inp["norm_final"]))
    put("gmem", _cols(inp["norm_mem"][0]))
    for k in range(3):
        put("fcw", _cols(inp["ffn_conv_w"][0][k]), 44 * k)
    put("fcb", _cols(inp["ffn_conv_b"][0]))
    for k in range(4):
        put("m2cw", _cols(inp["m2_conv_w"][0][k]), 24 * k)
    put("m2cb", _cols(inp["m2_conv_b"][0]))
    put("m2norm", _cols(inp["m2_norm"][0]))
    md = np.asarray(inp["m2_d"][0], np.float32)
    put("dcol", np.repeat(md.reshape(16, 2).T, 64, axis=0))
    put("dtb", np.tile(np.asarray(inp["m2_dt_bias"][0], np.float32)[None, :], (128, 1)))
    put("alog", np.tile(np.asarray(inp["m2_a_log"][0], np.float32)[None, :], (128, 1)))
    pcv[:, PCO["eps"]] = EPS
    put("lre", np.tile(np.asarray(inp["s5_lambda_re"][0], np.float32).T, (2, 1)))
    put("lim", np.tile(np.asarray(inp["s5_lambda_im"][0], np.float32).T, (2, 1)))
    put("ldt", np.tile(np.asarray(inp["s5_log_dt"][0], np.float32)[None, :], (128, 1)))
    sd = np.asarray(inp["s5_d"][0], np.float32).reshape(64, 16)
    put("s5d", np.tile(sd.T, (8, 1)))
    half = (np.arange(128) >= 64)
    pcv[:, PCO["ph1"]] = np.where(half, -np.pi / 2, 0.0)
    pcv[:, PCO["ph2"]] = np.where(half, np.pi, -np.pi / 2)
    pcv[:, PCO["psi"]] = np.where(half, np.pi / 2, 0.0)
    nv = [0, -1, -2, -3, -4, -5, -6, -7] + [7, 6, 5, 4, 3, 2, 1, 0] + [0, 1, 2, 3, 4, 5, 6, 7] + [1, 2, 3, 4, 5, 6, 7, 8]
    put("nv", np.tile(np.asarray(nv, np.float32)[None, :], (128, 1)))
    put("bmul", np.tile((8.0 * (np.arange(64, dtype=np.float32) + 1.0))[None, :], (128, 1)))
    s5p = np.zeros((128, 4, 1024), np.float32)
    bre = np.asarray(inp["s5_b_re"][0], np.float32)
    bim = np.asarray(inp["s5_b_im"][0], np.float32)
    cre = np.asarray(inp["s5_c_re"][0], np.float32)
    cim = np.asarray(inp["s5_c_im"][0], np.float32)
    s5p[:, 0] = np.tile(bre.transpose(1, 0, 2).reshape(64, 1024), (2, 1))
    s5p[:, 1] = np.tile(bim.transpose(1, 0, 2).reshape(64, 1024), (2, 1))
    s5p[:, 2] = np.tile(cre.transpose(2, 0, 1).reshape(64, 1024), (2, 1))
    s5p[:, 3] = np.tile(cim.transpose(2, 0, 1).reshape(64, 1024), (2, 1))
    m = {
        "x": np.ascontiguousarray(inp["x"][b0:b0 + cfg.nseq, :cfg.seq]),
        "mem": np.ascontiguousarray(inp["mem"][b0:b0 + cfg.nseq]),
        "pcols": pcv,
        "cmat": _const_mats(),
        "s5p": s5p,
    }
    for k in ("w_in", "w_a_val", "w_a_gate", "w_b", "w_kv", "w_c", "w_out", "w_up", "w_down"):
        m[k] = np.ascontiguousarray(inp[k][0])
    return m


_NC_CACHE = {}


def kernel(**inputs):
    cfg = Cfg()
    inp = {k: np.asarray(v) for k, v in inputs.items()}
    key = "full"
    if key not in _NC_CACHE:
        _NC_CACHE[key] = build(cfg)
    nc = _NC_CACHE[key]
    in_maps = [make_inputs(cfg, inp, 2 * c) for c in range(8)]
    res = run_bass_kernel_spmd(nc, in_maps, core_ids=list(range(8)))
    out = np.concatenate([r["out"] for r in res.results], axis=0)
    return out.astype(np.float32)
```

```python
import contextlib
import numpy as np
import concourse.bass as bass
import concourse.mybir as mybir
from concourse.bass_utils import run_bass_kernel_spmd

F32 = mybir.dt.float32
BF16 = mybir.dt.bfloat16
I32 = mybir.dt.int32
AF = mybir.ActivationFunctionType
ALU = mybir.AluOpType

D = 1024
NT = 512
MEM = 256
D_FF = 2816
EPS = 1e-6
P1, P2, P3, P4, P5 = 1024, 3072, 6144, 6176, 6688
D_IN = 9760


class TK:
    def __init__(self, nc, es):
        self.nc = nc
        self.es = es
        self.engs = {"pe": nc.tensor, "act": nc.scalar, "dve": nc.vector, "pool": nc.gpsimd, "sp": nc.sync}
        self.sems = {}
        self.cnt = {}
        for e in self.engs:
            self.sems[e] = es.enter_context(nc.semaphore("s_" + e))
            self.cnt[e] = 0
        self.seen = {e: {} for e in self.engs}
        self.recs = {}
        self.dq = {}
        for q, n in (("sp", 12), ("pool", 8), ("act", 4)):
            lst = []
            for i in range(n):
                key = "d_%s%d" % (q, i)
                self.sems[key] = es.enter_context(nc.semaphore(key))
                self.cnt[key] = 0
                lst.append(key)
            self.dq[q] = [lst, 0]
        self.final_waits = []

    @staticmethod
    def _acc(ap):
        name = ap.name
        sp = str(ap.space)
        pairs = ap.ap
        off = ap.offset
        if "DRAM" in sp:
            ext = 0
            for st, c in pairs:
                ext += abs(st) * (c - 1)
            return (name, 0, 1, off, off + ext + 1)
        pst, pc = pairs[0]
        if pst == 0:
            pst = 1 << 40
        p0 = off // pst
        f0 = off % pst
        ext = 0
        for st, c in pairs[1:]:
            ext += abs(st) * (c - 1)
        esz = 4 if ap.dtype in (F32, I32) else 2
        if "PSUM" in sp:
            b0 = (f0 * esz) // 2048
            b1 = ((f0 + ext) * esz) // 2048
            return (name, p0, p0 + pc, b0 * 2048, (b1 + 1) * 2048)
        return (name, p0, p0 + pc, f0 * esz, (f0 + ext + 1) * esz)

    @staticmethod
    def _accs(ap):
        base = TK._acc(ap)
        sp = str(ap.space)
        if "DRAM" in sp or "PSUM" in sp:
            return [base]
        pairs = ap.ap
        if len(pairs) < 3:
            return [base]
        st0, c0 = pairs[1]
        if c0 <= 1 or c0 > 64 or st0 <= 0:
            return [base]
        rest = 0
        for st, c in pairs[2:]:
            if st < 0:
                return [base]
            rest += st * (c - 1)
        rest += 1
        if st0 <= rest:
            return [base]
        name, p0, p1, f0b, _ = base
        esz = 4 if ap.dtype in (F32, I32) else 2
        return [(name, p0, p1, f0b + i * st0 * esz, f0b + (i * st0 + rest) * esz) for i in range(c0)]

    def _deps(self, e, reads, writes):
        need = {}
        for (acc, isw) in [(a, False) for a in reads] + [(a, True) for a in writes]:
            name, p0, p1, f0, f1 = acc
            lst = self.recs.get(name)
            if not lst:
                continue
            psum = name.startswith("ps")
            for r in lst:
                if not (isw or r[6]):
                    if not (psum and r[4] != e):
                        continue
                if r[0] >= p1 or r[1] <= p0 or r[2] >= f1 or r[3] <= f0:
                    continue
                if e == "pe" and r[4] == "pe":
                    continue
                k, v = r[4], r[5]
                if need.get(k, 0) < v:
                    need[k] = v
        return need

    def _emit_waits(self, e, need):
        seen = self.seen[e]
        eng = self.engs[e]
        for k, v in need.items():
            if seen.get(k, 0) < v:
                eng.wait_ge(self.sems[k], v)
                seen[k] = v

    def _record(self, reads, writes, key, val):
        for acc in writes:
            name, p0, p1, f0, f1 = acc
            lst = self.recs.setdefault(name, [])
            lst[:] = [r for r in lst if not (r[0] >= p0 and r[1] <= p1 and r[2] >= f0 and r[3] <= f1)]
            lst.append([p0, p1, f0, f1, key, val, True])
        for acc in reads:
            name, p0, p1, f0, f1 = acc
            lst = self.recs.setdefault(name, [])
            for r in lst:
                if (not r[6]) and r[4] == key and r[0] == p0 and r[1] == p1 and r[2] == f0 and r[3] == f1:
                    r[5] = val
                    break
            else:
                lst.append([p0, p1, f0, f1, key, val, False])

    def op(self, e, fn, **kw):
        reads, writes = [], []
        for k, v in kw.items():
            if hasattr(v, "ap") and hasattr(v, "space"):
                if k in ("out", "accum_out", "ap"):
                    writes.extend(self._accs(v))
                else:
                    reads.extend(self._accs(v))
        need = self._deps(e, reads, writes)
        self._emit_waits(e, need)
        inst = getattr(self.engs[e], fn)(**kw)
        self.cnt[e] += 1
        inst.then_inc(self.sems[e], 1)
        self._record(reads, writes, e, self.cnt[e])
        return inst

    def dma(self, q, out, in_, final=False):
        reads = self._accs(in_)
        writes = self._accs(out)
        need = self._deps(q, reads, writes)
        lst, idx = self.dq[q]
        key = lst[idx % len(lst)]
        self.dq[q][1] = idx + 1
        if self.cnt[key] > 0:
            need[key] = max(need.get(key, 0), self.cnt[key])
        self._emit_waits(q, need)
        inst = self.engs[q].dma_start(out=out, in_=in_)
        self.cnt[key] += 16
        inst.then_inc(self.sems[key], 16)
        self._record(reads, writes, key, self.cnt[key])
        if final:
            self.final_waits.append((q, key, self.cnt[key]))

    def finish(self):
        for q, key, v in self.final_waits:
            self._emit_waits(q, {key: v})
        for e in self.engs:
            need = {f: self.cnt[f] for f in ("pe", "act", "dve", "pool", "sp") if self.cnt[f] > 0}
            self._emit_waits(e, need)


class Cfg:
    def __init__(self, seq=2048, nseq=2, s5=True, ssd=True, xa=True, ffn=True, nt=512):
        self.seq = seq
        self.nseq = nseq
        self.s5 = s5
        self.ssd = ssd
        self.xa = xa
        self.ffn = ffn
        self.nt = nt


def pc_layout():
    off = {}
    o = 0
    for name, n in (("g1", 8), ("g2", 8), ("gf", 8), ("gmem", 8), ("fcw", 132), ("fcb", 44), ("m2cw", 96),
                    ("m2cb", 24), ("m2norm", 16), ("dcol", 16), ("dtb", 32), ("alog", 32), ("eps", 1),
                    ("lre", 64), ("lim", 64), ("ldt", 64), ("s5d", 64), ("ph1", 1), ("ph2", 1), ("psi", 1),
                    ("nv", 32), ("bmul", 64)):
        off[name] = o
        o += n
    off["_n"] = o
    return off


PCO = pc_layout()
NPC = PCO["_n"]
CM_ID, CM_U, CM_NM, CM_PERM, CM_BM, CM_Z = 0, 128, 256, 384, 512, 640
CM_L2 = 640 + 8 * 256
NCM = CM_L2 + 128
SLOT = 4096
NSLOT = 4


def _wblocks(cfg):
    blocks = []

    def add(wname, K, c0, ncols, ncb, tag):
        nk = K // 128
        for i in range(0, ncols, ncb):
            blocks.append((wname, nk, c0 + i, min(ncb, ncols - i), tag))

    if cfg.s5:
        add("w_in", D, 0, 1024, 512, "u")
        for i in range(4):
            blocks.append(("s5st", 32, i, 128, "st"))
    if cfg.xa:
        add("w_in", D, P4, 512, 512, "q")
        add("w_c", 512, 0, 1024, 1024, "wc")
        add("w_in", D, P5 + 2048, 1024, 512, "gc")
    if cfg.s5:
        for i in range(4):
            blocks.append(("s5to", 32, i, 128, "to"))
        add("w_a_val", D, 0, 1024, 512, "av")
        add("w_a_gate", D, 0, 1024, 512, "ag")
        add("w_in", D, P5, 1024, 512, "ga")
    if cfg.ssd:
        add("w_in", D, P3, 32, 32, "dt")
        add("w_in", D, P1, 2048, 512, "z")
        add("w_in", D, P2, 3072, 512, "xbc")
        add("w_in", D, P5 + 1024, 1024, 512, "gb")
        add("w_b", 2048, 0, 1024, 256, "wb")
    if cfg.s5 or cfg.xa or cfg.ssd:
        add("w_out", D, 0, 1024, 512, "wo")
    if cfg.ffn:
        add("w_up", D, 0, 2 * D_FF, 512, "up")
        add("w_down", D_FF, 0, 1024, 128, "dn")
    return blocks


def build(cfg):
    nc = bass.Bass("TRN2", target_bir_lowering=False)
    es = contextlib.ExitStack()
    SEQ, NSEQ, NT = cfg.seq, cfg.nseq, cfg.nt
    TPS = SEQ // NT
    NTILES = TPS * NSEQ
    NCH = NT // 128
    NB = NT // 8
    TWO_PI = float(2 * np.pi)

    def din(name, shape, dt=F32):
        return nc.dram_tensor(name, list(shape), dt, kind="ExternalInput").ap()

    x_d = din("x", [NSEQ, SEQ, D])
    mem_d = din("mem", [NSEQ, MEM, D])
    w_d = {
        "w_in": din("w_in", [D, D_IN]),
        "w_a_val": din("w_a_val", [D, D]),
        "w_a_gate": din("w_a_gate", [D, D]),
        "w_b": din("w_b", [2048, D]),
        "w_kv": din("w_kv", [D, 1024]),
        "w_c": din("w_c", [512, D]),
        "w_out": din("w_out", [D, D]),
        "w_up": din("w_up", [D, 2 * D_FF]),
        "w_down": din("w_down", [D_FF, D]),
    }
    pc_d = din("pcols", [128, NPC])
    cm_d = din("cmat", [128, NCM])
    s5p_d = din("s5p", [128, 4, 1024])
    out_d = nc.dram_tensor("out", [NSEQ, SEQ, D], F32, kind="ExternalOutput").ap()

    blocks = _wblocks(cfg)
    NBK = len(blocks)
    wscr = nc.dram_tensor("wscr", [NBK, 128, SLOT], BF16, kind="Internal").ap()

    with es:
        tk = TK(nc, es)

        def sb(name, shape, dt=F32):
            return es.enter_context(nc.sbuf_tensor("S_" + name, list(shape), dt))

        banks = [es.enter_context(nc.psum_tensor("ps%d" % i, [128, 512], F32)) for i in range(8)]
        bank_i = [0]

        def bank():
            b = banks[bank_i[0] % 8]
            bank_i[0] += 1
            return b

        def V(fn, **kw):
            return tk.op("dve", fn, **kw)

        def G(fn, **kw):
            return tk.op("pool", fn, **kw)

        def A(**kw):
            return tk.op("act", "activation", **kw)

        def MM(**kw):
            return tk.op("pe", "matmul", **kw)

        def TR(**kw):
            return tk.op("pe", "transpose", **kw)

        ARENA = 96 * 1024
        arena = sb("arena", [128, ARENA // 2], BF16)
        ar_off = [0]

        def ar_reset(o=0):
            ar_off[0] = o

        def ar(shape, dt=F32):
            esz = 4 if dt in (F32, I32) else 2
            n = int(np.prod(shape))
            o = (ar_off[0] + 63) // 64 * 64
            assert o + n * esz <= ARENA, ("arena overflow", o, n * esz)
            ar_off[0] = o + n * esz
            v = arena[:, o // 2:o // 2 + n * esz // 2]
            if dt != BF16:
                v = v.bitcast(dt)
            if len(shape) == 2:
                return v.rearrange("p (a b) -> p a b", a=shape[0])
            if len(shape) == 3:
                return v.rearrange("p (a b c) -> p a b c", a=shape[0], b=shape[1])
            return v

        pcols = sb("pcols", [128, NPC])
        ident_f = sb("ident_f", [128, 128])
        bm_f = sb("bm_f", [128, 128])
        cb = sb("cb", [128, 4 * 128 + 512 + 8 * 256 + 128], BF16)
        ident_b, ones_b, U_b, perm_b = cb[:, 0:128], cb[:, 128:256], cb[:, 256:384], cb[:, 384:512]
        nm4_b = cb[:, 512:1024]
        Z_b = cb[:, 1024:1024 + 2048].rearrange("p (a c) -> p a c", a=8)
        L2_b = cb[:, 3072:3200]
        tk.dma("sp", pcols[:], pc_d[:, :])
        ar_reset()
        cm = ar([NCM])
        tk.dma("sp", cm, cm_d[:, :])
        V("tensor_copy", out=ident_f[:], in_=cm[:, CM_ID:CM_ID + 128])
        V("tensor_copy", out=bm_f[:], in_=cm[:, CM_BM:CM_BM + 128])
        V("tensor_copy", out=ident_b, in_=cm[:, CM_ID:CM_ID + 128])
        V("memset", ap=ones_b, constant=1.0)
        V("tensor_copy", out=U_b, in_=cm[:, CM_U:CM_U + 128])
        V("tensor_copy", out=perm_b, in_=cm[:, CM_PERM:CM_PERM + 128])
        for i in range(4):
            V("tensor_copy", out=nm4_b[:, i * 128:(i + 1) * 128], in_=cm[:, CM_NM:CM_NM + 128])
        V("tensor_copy", out=cb[:, 1024:1024 + 2048], in_=cm[:, CM_Z:CM_Z + 2048])
        V("tensor_copy", out=L2_b, in_=cm[:, CM_L2:CM_L2 + 128])

        def pc(name, i=0, n=1):
            o = PCO[name] + i
            return pcols[:, o:o + n]

        if cfg.s5:
            rbar = sb("s5_rbar", [128, 64])
            cosT = sb("s5_cos", [128, 64, NB], BF16)
            sinT = sb("s5_sin", [128, 64, NB], BF16)
            carry = sb("s5_carry", [128, 64])
            ar_reset()
            c_re = ar([64, 16]); c_im = ar([64, 16])
            bbr = ar([64, 16]); bbi = ar([64, 16])
            ctab = {}
            for key in ("L", "S1", "S2", "R", "O"):
                for sh in (0, 1):
                    ctab[(key, sh)] = ar([8, 64])
            dtg = ar([64]); lrd = ar([64]); th = ar([64]); t0_ = ar([64]); t1_ = ar([64]); t2_ = ar([64])
            arr = ar([64]); aii = ar([64]); fr = ar([64]); fi = ar([64])
            save = ar_off[0]
            b_re = ar([64, 16]); b_im = ar([64, 16]); tb = ar([64, 16])
            ANG = ar([32, 64]); MAGN = ar([32, 64])
            tmpa = ar([8, 64])
            ki = ar([1024], I32)
            kf_ = ar([1024]); tt_ = ar([1024])
            s5v = s5p_d.rearrange("p a (g k) -> p a g k", g=64)
            tk.dma("sp", b_re, s5v[:, 0])
            tk.dma("sp", b_im, s5v[:, 1])
            tk.dma("sp", c_re, s5v[:, 2])
            tk.dma("sp", c_im, s5v[:, 3])

            def sincos_reduce(dst, src, n):
                kv = ki[:, 0:n]; kf = kf_[:, 0:n]; tt = tt_[:, 0:n]
                V("tensor_scalar", out=kf, in0=src, scalar1=float(1.0 / TWO_PI), scalar2=64.0, op0=ALU.mult, op1=ALU.add)
                V("tensor_copy", out=kv, in_=kf)
                V("tensor_copy", out=kf, in_=kv)
                V("tensor_scalar", out=tt, in0=src, scalar1=float(64 * TWO_PI), scalar2=None, op0=ALU.add)
                V("scalar_tensor_tensor", out=tt, in0=kf, scalar=-TWO_PI, in1=tt, op0=ALU.mult, op1=ALU.add)
                A(out=dst, in_=tt, func=AF.Sin)

            A(out=dtg, in_=pc("ldt", 0, 64), func=AF.Exp)
            V("tensor_tensor", out=lrd, in0=pc("lre", 0, 64), in1=dtg, op=ALU.mult)
            V("tensor_tensor", out=th, in0=pc("lim", 0, 64), in1=dtg, op=ALU.mult)
            A(out=t0_, in_=lrd, func=AF.Exp)
            V("tensor_scalar", out=t1_, in0=th, scalar1=float(np.pi / 2), scalar2=None, op0=ALU.add)
            sincos_reduce(arr, t1_, 64)
            sincos_reduce(aii, th, 64)
            V("tensor_tensor", out=arr, in0=arr, in1=t0_, op=ALU.mult)
            V("tensor_tensor", out=aii, in0=aii, in1=t0_, op=ALU.mult)
            lr, li = pc("lre", 0, 64), pc("lim", 0, 64)
            V("tensor_tensor", out=t1_, in0=lr, in1=lr, op=ALU.mult)
            V("tensor_tensor", out=t2_, in0=li, in1=li, op=ALU.mult)
            V("tensor_tensor", out=t1_, in0=t1_, in1=t2_, op=ALU.add)
            V("reciprocal", out=t1_, in_=t1_)
            V("tensor_scalar", out=t0_, in0=arr, scalar1=-1.0, scalar2=None, op0=ALU.add)
            V("tensor_tensor", out=fr, in0=t0_, in1=lr, op=ALU.mult)
            V("tensor_tensor", out=t2_, in0=aii, in1=li, op=ALU.mult)
            V("tensor_tensor", out=fr, in0=fr, in1=t2_, op=ALU.add)
            V("tensor_tensor", out=fr, in0=fr, in1=t1_, op=ALU.mult)
            V("tensor_tensor", out=fi, in0=aii, in1=lr, op=ALU.mult)
            V("tensor_tensor", out=t2_, in0=t0_, in1=li, op=ALU.mult)
            V("tensor_tensor", out=fi, in0=fi, in1=t2_, op=ALU.subtract)
            V("tensor_tensor", out=fi, in0=fi, in1=t1_, op=ALU.mult)
            frb = fr.unsqueeze(2).to_broadcast([128, 64, 16])
            fib = fi.unsqueeze(2).to_broadcast([128, 64, 16])
            V("tensor_tensor", out=bbr, in0=b_re, in1=frb, op=ALU.mult)
            V("tensor_tensor", out=tb, in0=b_im, in1=fib, op=ALU.mult)
            V("tensor_tensor", out=bbr, in0=bbr, in1=tb, op=ALU.subtract)
            V("tensor_tensor", out=bbi, in0=b_im, in1=frb, op=ALU.mult)
            V("tensor_tensor", out=tb, in0=b_re, in1=fib, op=ALU.mult)
            V("tensor_tensor", out=bbi, in0=bbi, in1=tb, op=ALU.add)
            nvb = pc("nv", 0, 32).unsqueeze(2).to_broadcast([128, 32, 64])
            V("tensor_tensor", out=ANG, in0=th.unsqueeze(1).to_broadcast([128, 32, 64]), in1=nvb, op=ALU.mult)
            V("tensor_tensor", out=MAGN, in0=lrd.unsqueeze(1).to_broadcast([128, 32, 64]), in1=nvb, op=ALU.mult)
            A(out=MAGN, in_=MAGN, func=AF.Exp)
            for key, st, phn in (("L", 0, "ph1"), ("S1", 1, "ph1"), ("S2", 1, "ph2"), ("R", 2, "psi"), ("O", 3, "psi")):
                for sh in (0, 1):
                    dst = ctab[(key, sh)]
                    V("tensor_scalar", out=tmpa, in0=ANG[:, st * 8:(st + 1) * 8, :], scalar1=pc(phn),
                      scalar2=None, op0=ALU.add)
                    V("tensor_scalar", out=tmpa, in0=tmpa, scalar1=float(np.pi / 2 * (1 + sh)), scalar2=None, op0=ALU.add)
                    sincos_reduce(dst.rearrange("p a b -> p (a b)"), tmpa.rearrange("p a b -> p (a b)"), 512)
                    V("tensor_tensor", out=dst, in0=dst, in1=MAGN[:, st * 8:(st + 1) * 8, :], op=ALU.mult)
            A(out=rbar[:], in_=lrd, func=AF.Exp, scale=8.0)
            NBC = 16
            bang = ANG[:, 0:16, :].rearrange("p a b -> p (a b)")
            stmp = MAGN[:, 0:16, :].rearrange("p a b -> p (a b)")
            bang3 = bang.rearrange("p (g b) -> p g b", g=64)
            for b0 in range(0, NB, NBC):
                V("tensor_tensor", out=bang3, in0=th.unsqueeze(2).to_broadcast([128, 64, NBC]),
                  in1=pc("bmul", b0, NBC).unsqueeze(1).to_broadcast([128, 64, NBC]), op=ALU.mult)
                sincos_reduce(stmp, bang, 64 * NBC)
                V("tensor_copy", out=sinT[:, :, b0:b0 + NBC], in_=stmp.rearrange("p (g b) -> p g b", g=64))
                V("tensor_scalar", out=bang, in0=bang, scalar1=float(np.pi / 2), scalar2=None, op0=ALU.add)
                sincos_reduce(stmp, bang, 64 * NBC)
                V("tensor_copy", out=cosT[:, :, b0:b0 + NBC], in_=stmp.rearrange("p (g b) -> p g b", g=64))
            st_bi = [i for i, bl in enumerate(blocks) if bl[4] == "st"]
            to_bi = [i for i, bl in enumerate(blocks) if bl[4] == "to"]
            ar_reset(save)
            stS = ar([16, 2, 128], BF16)
            stT = ar([16, 2, 128], BF16)
            tq = ar([8, 8, 16])
            tq2 = ar([8, 8, 16])
            tm = ar([4, 128])
            tabs = {key: ar([8, 8, 16]) for key in ("L", "S1", "S2", "R", "O")}
            for bt in range(4):
                for sub in range(2):
                    g0 = bt * 16 + sub * 8
                    for key, P_re, P_im in (("L", bbr, bbi), ("S1", bbr, bbi), ("S2", bbr, bbi), ("R", c_re, c_im), ("O", c_re, c_im)):
                        t4 = tabs[key]
                        c0 = ctab[(key, 0)][:, :, g0:g0 + 8].rearrange("p n g -> p g n").unsqueeze(3).to_broadcast([128, 8, 8, 16])
                        c1 = ctab[(key, 1)][:, :, g0:g0 + 8].rearrange("p n g -> p g n").unsqueeze(3).to_broadcast([128, 8, 8, 16])
                        pr = P_re[:, g0:g0 + 8, :].unsqueeze(2).to_broadcast([128, 8, 8, 16])
                        pi_ = P_im[:, g0:g0 + 8, :].unsqueeze(2).to_broadcast([128, 8, 8, 16])
                        EW = G if key in ("S1", "S2", "O") else V
                        tqq = tq2 if key in ("S1", "S2", "O") else tq
                        EW("tensor_tensor", out=t4, in0=pr, in1=c0, op=ALU.mult)
                        EW("tensor_tensor", out=tqq, in0=pi_, in1=c1, op=ALU.mult)
                        EW("tensor_tensor", out=t4, in0=t4, in1=tqq, op=ALU.add)
                    for q in range(2):
                        b = bank()
                        for gg in range(4):
                            gl = q * 4 + gg
                            MM(out=b[:, gg * 128:(gg + 1) * 128], lhsT=tabs["L"][:, gl].rearrange("p a b -> p (a b)"),
                               rhs=tabs["R"][:, gl].rearrange("p a b -> p (a b)"), start=True, stop=True)
                        V("tensor_tensor", out=tm, in0=b[:].rearrange("p (a c) -> p a c", a=4),
                          in1=bm_f[:].unsqueeze(1).to_broadcast([128, 4, 128]), op=ALU.mult)
                        for gg in range(4):
                            gl = q * 4 + gg
                            V("scalar_tensor_tensor", out=stT[:, sub * 8 + gl, 0, :], in0=ident_f[:], scalar=pc("s5d", g0 + gl),
                              in1=tm[:, gg, :], op0=ALU.mult, op1=ALU.add)
                        for si_, key in ((0, "S1"), (1, "S2")):
                            b2 = bank()
                            for gg in range(4):
                                gl = q * 4 + gg
                                TR(out=b2[:, gg * 128:(gg + 1) * 128], in_=tabs[key][:, gl].rearrange("p a b -> p (a b)"),
                                   identity=ident_f[:])
                            A(out=stS[:, sub * 8 + q * 4:sub * 8 + (q + 1) * 4, si_, :],
                              in_=b2[:].rearrange("p (a c) -> p a c", a=4), func=AF.Copy)
                    V("tensor_copy", out=stT[:, sub * 8:(sub + 1) * 8, 1, :], in_=tabs["O"].rearrange("p g a b -> p g (a b)"))
                tk.dma("sp", wscr[st_bi[bt], :, :], stS.rearrange("p a b c -> p (a b c)"))
                tk.dma("sp", wscr[to_bi[bt], :, :], stT.rearrange("p a b c -> p (a b c)"))

        xT = sb("xT", [128, 8, NT])
        hT = sb("hT", [128, 8, NT], BF16)
        mrg = sb("mrg", [128, 8, NT], BF16)
        slots = [sb("wslot%d" % i, [128, SLOT], BF16) for i in range(NSLOT)]
        if cfg.ffn:
            fhalo = sb("fhalo", [128, 44, 2], BF16)
        if cfg.xa:
            KT = sb("KT", [128, 4, NSEQ * MEM], BF16)
            Vm = sb("Vm", [128, NSEQ * 2, 512], BF16)
        if cfg.ssd:
            hs = sb("ssd_hs", [128, 2048])
            hsb = sb("ssd_hsb", [128, 2048], BF16)
            mhalo = sb("mhalo", [128, 24, 3], BF16)
            Aneg = sb("Aneg", [128, 32])
            A(out=Aneg[:], in_=pc("alog", 0, 32), func=AF.Exp)
            V("tensor_scalar", out=Aneg[:], in0=Aneg[:], scalar1=-1.0, scalar2=None, op0=ALU.mult)

        stream = {"next_issue": 0, "next_use": 0}
        total_blocks = NBK * NTILES

        def issue_upto(n):
            while stream["next_issue"] < min(n, total_blocks):
                i = stream["next_issue"]
                bi = i % NBK
                wname, nk, c0, ncb, tag = blocks[bi]
                sl = slots[i % NSLOT][:, 0:nk * ncb]
                if i < NBK and not wname.startswith("s5"):
                    src = w_d[wname][:, c0:c0 + ncb].rearrange("(k p) c -> p k c", p=128)
                    tk.dma("pool", sl.rearrange("p (k c) -> p k c", k=nk), src)
                    if NTILES > 1:
                        tk.dma("sp", wscr[bi, :, 0:nk * ncb], sl)
                else:
                    tk.dma("sp", sl, wscr[bi, :, 0:nk * ncb])
                stream["next_issue"] += 1

        def next_block(tag):
            i = stream["next_use"]
            bi = i % NBK
            wname, nk, c0, ncb, btag = blocks[bi]
            assert btag == tag, (btag, tag)
            issue_upto(i + NSLOT)
            stream["next_use"] += 1
            return slots[i % NSLOT][:, 0:nk * ncb].rearrange("p (k c) -> p k c", k=nk), nk, ncb

        def proj(tag, ncols, rhs_fn, nk, evac, n=NT):
            m = 0
            done = 0
            while done < ncols:
                wv, wnk, ncb = next_block(tag)
                assert wnk == nk
                for mm in range(ncb // 128):
                    b = bank()
                    for k in range(nk):
                        MM(out=b[:, 0:n], lhsT=wv[:, k, mm * 128:(mm + 1) * 128], rhs=rhs_fn(k),
                           start=(k == 0), stop=(k == nk - 1))
                    evac(m, b)
                    m += 1
                done += ncb

        def rms_stats(src_fn, nct, n, sq, denom):
            for ct in range(nct):
                A(out=sq[:, ct, 0:n], in_=src_fn(ct), func=AF.Square)
            b = bank()
            for ct in range(nct):
                MM(out=b[:, 0:n], lhsT=ones_b, rhs=sq[:, ct, 0:n], start=(ct == 0), stop=(ct == nct - 1))
            rs = sq[:, 0:2, :].rearrange("p a b -> p (a b)").bitcast(F32)[:, 0:n]
            A(out=rs, in_=b[:, 0:n], func=AF.Sqrt, scale=1.0 / denom, bias=pc("eps"))
            V("reciprocal", out=rs, in_=rs)
            return rs

        def rmsnorm_to_hT(gname):
            sq = ar([8, NT], BF16)
            rstd = rms_stats(lambda ct: xT[:, ct, :], 8, NT, sq, D)
            for ct in range(8):
                V("scalar_tensor_tensor", out=hT[:, ct, :], in0=xT[:, ct, :], scalar=pc(gname, ct), in1=rstd,
                  op0=ALU.mult, op1=ALU.mult)

        n_merged = [0]

        def merge(gtag, ysrc):
            first = (n_merged[0] == 0)
            n_merged[0] += 1
            sgs = [ar([NT]) for _ in range(2)]
            tmps = [ar([NT], BF16) for _ in range(2)]

            def ev(m, b):
                sg = sgs[m % 2]
                A(out=sg, in_=b[:], func=AF.Sigmoid)
                if first:
                    G("tensor_tensor", out=mrg[:, m, :], in0=sg, in1=ysrc[:, m, :], op=ALU.mult)
                else:
                    tp = tmps[m % 2]
                    G("tensor_tensor", out=tp, in0=sg, in1=ysrc[:, m, :], op=ALU.mult)
                    G("tensor_tensor", out=mrg[:, m, :], in0=mrg[:, m, :], in1=tp, op=ALU.add)
            proj(gtag, 1024, lambda k: hT[:, k, :], 8, ev)

        if cfg.xa:
            ar_reset()
            NM = NSEQ * MEM
            memin = ar([NSEQ * 2, D])
            memT = ar([8, NM])
            memTn = ar([8, NM], BF16)
            sqm = ar([8, NM], BF16)
            wkv = [ar([8, 512], BF16) for _ in range(2)]
            tk.dma("act", memin, mem_d.rearrange("s (t p) d -> p (s t) d", p=128))
            for i in range(2):
                tk.dma("pool", wkv[i], w_d["w_kv"][:, i * 512:(i + 1) * 512].rearrange("(k p) c -> p k c", p=128))
            for ct in range(8):
                b = bank()
                for s in range(NSEQ * 2):
                    TR(out=b[:, s * 128:(s + 1) * 128], in_=memin[:, s, ct * 128:(ct + 1) * 128], identity=ident_f[:])
                A(out=memT[:, ct, :], in_=b[:, 0:NM], func=AF.Copy)
            rstd = rms_stats(lambda ct: memT[:, ct, :], 8, NM, sqm, D)
            for ct in range(8):
                V("scalar_tensor_tensor", out=memTn[:, ct, :], in0=memT[:, ct, :], scalar=pc("gmem", ct),
                  in1=rstd, op0=ALU.mult, op1=ALU.mult)
            for h in range(4):
                b = bank()
                for k in range(8):
                    MM(out=b[:, 0:NM], lhsT=wkv[0][:, k, h * 128:(h + 1) * 128], rhs=memTn[:, k, :],
                       start=(k == 0), stop=(k == 7))
                A(out=KT[:, h, :], in_=b[:, 0:NM], func=AF.Copy)
            for mt in range(NSEQ * 2):
                b = bank()
                for k in range(8):
                    MM(out=b[:], lhsT=memTn[:, k, mt * 128:(mt + 1) * 128], rhs=wkv[1][:, k, :],
                       start=(k == 0), stop=(k == 7))
                A(out=Vm[:, mt, :], in_=b[:], func=AF.Copy)

        xin_next = [None]
        pending_tail = [None]
        for ti in range(NTILES):
            s_i = ti // TPS
            t0 = (ti % TPS) * NT
            first = (ti % TPS == 0)
            n_merged[0] = 0
            ar_reset()
            if ti == 0:
                xin = ar([NCH, D])
                tk.dma("act", xin, x_d[s_i, t0:t0 + NT, :].rearrange("(s p) d -> p s d", p=128))
            else:
                xin = xin_next[0]
            for ct in range(8):
                b = bank()
                for s in range(NCH):
                    TR(out=b[:, s * 128:(s + 1) * 128], in_=xin[:, s, ct * 128:(ct + 1) * 128], identity=ident_f[:])
                A(out=xT[:, ct, :], in_=b[:, 0:NT], func=AF.Copy)
            rmsnorm_to_hT("g1")
            if pending_tail[0] is not None:
                pending_tail[0]()
                pending_tail[0] = None

            if cfg.s5:
                ar_reset()
                uT = ar([8, NT], BF16)
                U2 = ar([8, 8, NB], BF16)
                o_st = ar_off[0]
                St = ar([64, NB])
                o_after = ar_off[0]
                ar_reset(o_st)
                Yg = ar([8, 8, NB], BF16)
                ar_reset(o_after)
                o_wS = ar_off[0]
                wS = ar([64, NB])
                wSb = ar([64, NB], BF16)
                sprev = ar([64, NB + 1], BF16)
                gT = uT
                tA = [ar([8, NB]) for _ in range(2)]
                tB = [ar([8, NB]) for _ in range(2)]
                o_xa = ar_off[0]

                def ev_u(m, b):
                    A(out=uT[:, m, :].rearrange("p (j b) -> p j b", j=8), in_=b[:].rearrange("p (b j) -> p j b", j=8),
                      func=AF.Copy)
                proj("u", 1024, lambda k: hT[:, k, :], 8, ev_u)
                if first:
                    V("memset", ap=carry[:], constant=0.0)
                uTv = uT.rearrange("p c (j b) -> p c b j", j=8)
                for g8 in range(8):
                    b = bank()
                    bv = b[:, 0:8 * NB].rearrange("p (c b) -> p c b", c=8)
                    for j in range(8):
                        MM(out=bv, lhsT=Z_b[:, g8, 128 - 16 * j:256 - 16 * j], rhs=uTv[:, :, :, j],
                           start=(j == 0), stop=(j == 7))
                    V("tensor_copy", out=U2[:, :, g8, :], in_=bv)
                for blk in range(4):
                    wv, _, _ = next_block("st")
                    wv4 = wv.rearrange("p (g s) c -> p g s c", s=2)
                    for half in range(2):
                        bx, by = bank(), bank()
                        for gi in range(8):
                            gl = half * 8 + gi
                            g = blk * 16 + gl
                            rhs = U2[:, g // 8, g % 8, :]
                            MM(out=bx[:, gi * NB:(gi + 1) * NB], lhsT=wv4[:, gl, 0, :], rhs=rhs, start=True, stop=True)
                            MM(out=by[:, gi * NB:(gi + 1) * NB], lhsT=wv4[:, gl, 1, :], rhs=rhs, start=True, stop=True)
                        gs = blk * 16 + half * 8
                        ta, tb_ = tA[half], tB[half]
                        V("tensor_tensor", out=ta, in0=bx[:, 0:8 * NB].rearrange("p (g b) -> p g b", g=8),
                          in1=cosT[:, gs:gs + 8, :], op=ALU.mult)
                        V("tensor_tensor", out=tb_, in0=by[:, 0:8 * NB].rearrange("p (g b) -> p g b", g=8),
                          in1=sinT[:, gs:gs + 8, :], op=ALU.mult)
                        G("tensor_tensor", out=St[:, gs:gs + 8, :], in0=ta, in1=tb_, op=ALU.add)
            if cfg.xa:
                o_x0 = o_xa if cfg.s5 else 0
                ar_reset(o_x0)
                qT = ar([4, NT], BF16)
                PT = [ar([2, NT], BF16) for _ in range(2)]
                oT = ar([4, NT], BF16)
                rden = [ar([NT]) for _ in range(2)]
                o_x3 = ar_off[0]
                ar_reset(o_x0)
                ytmp = ar([8, NT], BF16)
                ar_reset(o_x3)

                def ev_q(m, b):
                    A(out=qT[:, m, :], in_=b[:], func=AF.Copy)
                proj("q", 512, lambda k: hT[:, k, :], 8, ev_q)
                for h in range(4):
                    pt = PT[h % 2]
                    for mt in range(2):
                        b = bank()
                        MM(out=b[:], lhsT=KT[:, h, s_i * MEM + mt * 128:s_i * MEM + (mt + 1) * 128], rhs=qT[:, h, :],
                           start=True, stop=True)
                        A(out=pt[:, mt, :], in_=b[:], func=AF.Exp, scale=float(128 ** -0.5))
                    bd = bank()
                    for mt in range(2):
                        MM(out=bd[:], lhsT=ones_b, rhs=pt[:, mt, :], start=(mt == 0), stop=(mt == 1))
                    bo = bank()
                    for mt in range(2):
                        MM(out=bo[:], lhsT=Vm[:, s_i * 2 + mt, h * 128:(h + 1) * 128], rhs=pt[:, mt, :],
                           start=(mt == 0), stop=(mt == 1))
                    V("reciprocal", out=rden[h % 2], in_=bd[:])
                    V("tensor_tensor", out=oT[:, h, :], in0=bo[:], in1=rden[h % 2], op=ALU.mult)

                def ev_wc(m, b):
                    A(out=ytmp[:, m, :], in_=b[:], func=AF.Copy)
                proj("wc", 1024, lambda k: oT[:, k, :], 4, ev_wc)
                merge("gc", ytmp)

            if cfg.s5:
                for g in range(64):
                    V("tensor_tensor_scan", out=wS[:, g, :], data0=rbar[:, g:g + 1].to_broadcast([128, NB]),
                      data1=St[:, g, :], initial=carry[:, g:g + 1], op0=ALU.mult, op1=ALU.add)
                A(out=wSb, in_=wS, func=AF.Copy)
                A(out=sprev[:, :, 0], in_=carry[:], func=AF.Copy)
                wSbf = wSb.rearrange("p g b -> p (g b)")
                for q in range(8):
                    b = bank()
                    MM(out=b[:, 0:8 * NB], lhsT=perm_b, rhs=wSbf[:, q * 8 * NB:(q + 1) * 8 * NB], start=True, stop=True)
                    ta, tb_ = tA[q % 2], tB[q % 2]
                    V("tensor_tensor", out=ta, in0=b[:, 0:8 * NB].rearrange("p (g b) -> p g b", g=8),
                      in1=sinT[:, q * 8:(q + 1) * 8, :], op=ALU.mult)
                    G("tensor_tensor", out=tb_, in0=wS[:, q * 8:(q + 1) * 8, :], in1=cosT[:, q * 8:(q + 1) * 8, :], op=ALU.mult)
                    V("tensor_tensor", out=sprev[:, q * 8:(q + 1) * 8, 1:NB + 1], in0=tb_, in1=ta, op=ALU.subtract)
                    V("tensor_tensor", out=carry[:, q * 8:(q + 1) * 8], in0=tb_[:, :, NB - 1], in1=ta[:, :, NB - 1], op=ALU.subtract)
                for blk in range(4):
                    wv, _, _ = next_block("to")
                    wv4 = wv.rearrange("p (g s) c -> p g s c", s=2)
                    for half in range(2):
                        b = bank()
                        for gi in range(8):
                            gl = half * 8 + gi
                            g = blk * 16 + gl
                            MM(out=b[:, gi * NB:(gi + 1) * NB], lhsT=wv4[:, gl, 0, :], rhs=U2[:, g // 8, g % 8, :],
                               start=True, stop=False)
                            MM(out=b[:, gi * NB:(gi + 1) * NB], lhsT=wv4[:, gl, 1, :], rhs=sprev[:, g, 0:NB],
                               start=False, stop=True)
                        ct = (blk * 16 + half * 8) // 8
                        A(out=Yg[:, ct, :, :], in_=b[:, 0:8 * NB].rearrange("p (g b) -> p g b", g=8), func=AF.Gelu)
                gTv = gT.rearrange("p c (j b) -> p c b j", j=8)
                for t in range(8):
                    b = bank()
                    bv = b[:, 0:8 * NB].rearrange("p (c b) -> p c b", c=8)
                    for g8 in range(8):
                        MM(out=bv, lhsT=Z_b[:, t, 128 - 16 * g8:256 - 16 * g8], rhs=Yg[:, :, g8, :],
                           start=(g8 == 0), stop=(g8 == 7))
                    A(out=gTv[:, :, :, t], in_=bv, func=AF.Copy)

                ar_reset(o_wS)
                ytmp = ar([8, NT], BF16)

                def ev_av(m, b):
                    A(out=ytmp[:, m, :].rearrange("p (b j) -> p b j", j=8), in_=b[:].rearrange("p (j b) -> p b j", j=8),
                      func=AF.Copy)
                proj("av", 1024, lambda k: gT[:, k, :], 8, ev_av)
                sg2 = [ar([NT], BF16) for _ in range(2)]

                def ev_ag(m, b):
                    A(out=sg2[m % 2].rearrange("p (b j) -> p b j", j=8), in_=b[:].rearrange("p (j b) -> p b j", j=8),
                      func=AF.Sigmoid)
                    G("tensor_tensor", out=ytmp[:, m, :], in0=ytmp[:, m, :], in1=sg2[m % 2], op=ALU.mult)
                proj("ag", 1024, lambda k: gT[:, k, :], 8, ev_ag)
                merge("ga", ytmp)

            if cfg.ssd:
                ar_reset()
                sz = ar([16, NT], BF16)
                xbcs = ar([24, NT], BF16)
                o_ssd = ar_off[0]
                dtt = ar([NCH, 32]); dA = ar([NCH, 32]); dtd = ar([NCH, 32]); cdb2 = ar([2, 32])
                dAb = ar([NCH, 32], BF16); ndAb = ar([NCH, 32], BF16)
                raw = [ar([NT + 3], BF16) for _ in range(3)]
                ctm = [ar([NT]) for _ in range(3)]
                Gs = ar([4, 128])
                E4 = [ar([4, 128]) for _ in range(2)]
                EA4 = [ar([4, 128]) for _ in range(2)]
                Mall = ar([32, 128], BF16)
                Cdall = ar([32, 128], BF16)
                xdt = ar([2048], BF16)
                xdtd = ar([2048], BF16)
                Btok = ar([512], BF16)
                ytm = [ar([4, 128], BF16) for _ in range(2)]
                ytm2 = [ar([4, 128]) for _ in range(2)]
                wv, _, _ = next_block("dt")
                bdt = bank()
                for c in range(NCH):
                    for k in range(8):
                        MM(out=bdt[:, c * 32:(c + 1) * 32], lhsT=hT[:, k, c * 128:(c + 1) * 128], rhs=wv[:, k, :],
                           start=(k == 0), stop=(k == 7))
                V("tensor_tensor", out=dtt, in0=bdt[:, 0:NCH * 32].rearrange("p (c h) -> p c h", c=NCH),
                  in1=pc("dtb", 0, 32).unsqueeze(1).to_broadcast([128, NCH, 32]), op=ALU.add)
                A(out=dtt, in_=dtt, func=AF.Exp)
                A(out=dtt, in_=dtt, func=AF.Ln, bias=1.0)
                V("tensor_tensor", out=dA, in0=dtt, in1=Aneg[:].unsqueeze(1).to_broadcast([128, NCH, 32]), op=ALU.mult)
                V("tensor_copy", out=dAb, in_=dA)
                V("tensor_scalar", out=ndAb, in0=dA, scalar1=-1.0, scalar2=None, op0=ALU.mult)

                def ev_z(m, b):
                    A(out=sz[:, m, :], in_=b[:], func=AF.Silu)
                proj("z", 2048, lambda k: hT[:, k, :], 8, ev_z)
                if first:
                    G("memset", ap=mhalo[:], constant=0.0)

                pend = []

                def ev_xbc(m, b):
                    r = raw[m % 3]
                    tm = ctm[m % 3]
                    G("tensor_copy", out=r[:, 0:3], in_=mhalo[:, m, :])
                    A(out=r[:, 3:NT + 3], in_=b[:], func=AF.Copy)
                    G("tensor_copy", out=mhalo[:, m, :], in_=r[:, NT:NT + 3])
                    G("tensor_scalar", out=tm, in0=r[:, 0:NT], scalar1=pc("m2cw", m), scalar2=pc("m2cb", m),
                      op0=ALU.mult, op1=ALU.add)
                    for kk in range(1, 4):
                        V("scalar_tensor_tensor", out=tm, in0=r[:, kk:NT + kk], scalar=pc("m2cw", 24 * kk + m), in1=tm,
                          op0=ALU.mult, op1=ALU.add)
                    while pend:
                        pend.pop(0)()
                    pend.append(lambda m=m, tm=tm: A(out=xbcs[:, m, :], in_=tm, func=AF.Silu))
                proj("xbc", 3072, lambda k: hT[:, k, :], 8, ev_xbc)
                while pend:
                    pend.pop(0)()

                for c in range(NCH):
                    cs = slice(c * 128, (c + 1) * 128)
                    firstc = first and c == 0
                    cdb = cdb2[:, c % 2, :]
                    bD = bank()
                    MM(out=bD[:, 0:32], lhsT=L2_b, rhs=dAb[:, c, :], start=True, stop=True)
                    MM(out=bD[:, 32:64], lhsT=ones_b, rhs=dAb[:, c, :], start=True, stop=True)
                    A(out=dtd[:, c, :], in_=bD[:, 0:32], func=AF.Exp)
                    A(out=cdb, in_=bD[:, 32:64], func=AF.Exp)
                    V("tensor_tensor", out=dtd[:, c, :], in0=dtd[:, c, :], in1=dtt[:, c, :], op=ALU.mult)
                    bG = bank()
                    for g in range(4):
                        MM(out=bG[:, g * 128:(g + 1) * 128], lhsT=xbcs[:, 16 + g, cs], rhs=xbcs[:, 20 + g, cs],
                           start=True, stop=True)
                    A(out=Gs, in_=bG[:].rearrange("p (g l) -> p g l", g=4), func=AF.Copy)
                    for q in range(4):
                        bT = bank()
                        bTb = bT[:].bitcast(BF16)
                        for i in range(4):
                            TR(out=bTb[:, i * 128:(i + 1) * 128], in_=xbcs[:, q * 4 + i, cs], identity=ident_b)
                        src = bTb[:, 0:512].rearrange("p (h d) -> p h d", h=8)
                        V("tensor_tensor", out=xdt[:, q * 512:(q + 1) * 512].rearrange("p (h d) -> p h d", h=8), in0=src,
                          in1=dtt[:, c, q * 8:(q + 1) * 8].unsqueeze(2).to_broadcast([128, 8, 64]), op=ALU.mult)
                        V("tensor_tensor", out=xdtd[:, q * 512:(q + 1) * 512].rearrange("p (h d) -> p h d", h=8), in0=src,
                          in1=dtd[:, c, q * 8:(q + 1) * 8].unsqueeze(2).to_broadcast([128, 8, 64]), op=ALU.mult)
                    bT = bank()
                    bTb = bT[:].bitcast(BF16)
                    for g in range(4):
                        TR(out=bTb[:, g * 128:(g + 1) * 128], in_=xbcs[:, 16 + g, cs], identity=ident_b)
                    A(out=Btok, in_=bTb[:, 0:512], func=AF.Copy)

                    ybank = {}

                    def emit_y(hq):
                        q = hq // 2
                        if hq % 2 == 0:
                            ybank[q] = bank()
                        bY = ybank[q]
                        for pp in range(2):
                            pi_ = (hq % 2) * 2 + pp
                            pr = q * 4 + pi_
                            for hh in range(2):
                                h = 2 * pr + hh
                                o = bY[hh * 64:(hh + 1) * 64, pi_ * 128:(pi_ + 1) * 128]
                                MM(out=o, lhsT=xdt[:, h * 64:(h + 1) * 64], rhs=Mall[:, h, :], start=True, stop=firstc)
                                if not firstc:
                                    MM(out=o, lhsT=hsb[:, h * 64:(h + 1) * 64], rhs=Cdall[:, h, :], start=False, stop=True)
                        if hq % 2 == 1:
                            t1, t2 = ytm[q % 2], ytm2[q % 2]
                            for p4 in range(4):
                                A(out=t1[:, p4, :], in_=xbcs[:, q * 4 + p4, cs], func=AF.Copy, scale=pc("dcol", q * 4 + p4))
                            V("tensor_tensor", out=t2, in0=bY[:].rearrange("p (a l) -> p a l", a=4), in1=t1, op=ALU.add)
                            G("tensor_tensor", out=xbcs[:, q * 4:(q + 1) * 4, cs], in0=t2, in1=sz[:, q * 4:(q + 1) * 4, cs],
                              op=ALU.mult)

                    for hq in range(8):
                        h0 = hq * 4
                        g = hq // 2
                        bE, bA = bank(), bank()
                        MM(out=bE[:].rearrange("p (h l) -> p h l", h=4), lhsT=U_b,
                           rhs=ndAb[:, c, h0:h0 + 4].unsqueeze(2).to_broadcast([128, 4, 128]), start=True, stop=False)
                        MM(out=bE[:], lhsT=ident_b, rhs=nm4_b, start=False, stop=False)
                        for hh in range(4):
                            MM(out=bE[:, hh * 128:(hh + 1) * 128], lhsT=dAb[:, c, h0 + hh:h0 + hh + 1].to_broadcast([128, 128]),
                               rhs=U_b, start=False, stop=(hh == 3))
                        for hh in range(4):
                            MM(out=bA[:, hh * 128:(hh + 1) * 128], lhsT=dAb[:, c, h0 + hh:h0 + hh + 1].to_broadcast([128, 128]),
                               rhs=U_b, start=True, stop=True)
                        e4, ea4 = E4[hq % 2], EA4[hq % 2]
                        A(out=e4, in_=bE[:].rearrange("p (h l) -> p h l", h=4), func=AF.Exp)
                        A(out=ea4, in_=bA[:].rearrange("p (h l) -> p h l", h=4), func=AF.Exp)
                        V("tensor_tensor", out=Mall[:, h0:h0 + 4, :], in0=e4,
                          in1=Gs[:, g, :].unsqueeze(1).to_broadcast([128, 4, 128]), op=ALU.mult)
                        G("tensor_tensor", out=Cdall[:, h0:h0 + 4, :], in0=ea4,
                          in1=xbcs[:, 20 + g, cs].unsqueeze(1).to_broadcast([128, 4, 128]), op=ALU.mult)
                        if hq >= 2:
                            emit_y(hq - 2)
                    emit_y(6)
                    emit_y(7)

                    for g in range(4):
                        bS = bank()
                        MM(out=bS[:], lhsT=Btok[:, g * 128:(g + 1) * 128], rhs=xdtd[:, g * 512:(g + 1) * 512], start=True, stop=True)
                        hv = hs[:, g * 512:(g + 1) * 512]
                        if firstc:
                            V("tensor_copy", out=hv, in_=bS[:])
                        else:
                            G("tensor_tensor", out=hv.rearrange("p (h d) -> p h d", h=8), in0=hv.rearrange("p (h d) -> p h d", h=8),
                              in1=cdb[:, g * 8:(g + 1) * 8].unsqueeze(2).to_broadcast([128, 8, 64]), op=ALU.mult)
                            V("tensor_tensor", out=hv, in0=bS[:], in1=hv, op=ALU.add)
                        A(out=hsb[:, g * 512:(g + 1) * 512], in_=hv, func=AF.Copy)
                ar_reset(o_ssd)
                ytmp = ar([8, NT], BF16)
                sgb = ar([8, NT], BF16)
                for ct in range(16):
                    A(out=sz[:, ct, :], in_=xbcs[:, ct, :], func=AF.Square)

                def ev_gb(m, b):
                    A(out=sgb[:, m, :], in_=b[:], func=AF.Sigmoid)
                proj("gb", 1024, lambda k: hT[:, k, :], 8, ev_gb)
                bn = bank()
                for ct in range(16):
                    MM(out=bn[:], lhsT=ones_b, rhs=sz[:, ct, :], start=(ct == 0), stop=(ct == 15))
                rstd = sz[:, 0:2, :].rearrange("p a b -> p (a b)").bitcast(F32)
                A(out=rstd, in_=bn[:], func=AF.Sqrt, scale=1.0 / 2048, bias=pc("eps"))
                V("reciprocal", out=rstd, in_=rstd)
                for ct in range(16):
                    V("scalar_tensor_tensor", out=xbcs[:, ct, :], in0=xbcs[:, ct, :], scalar=pc("m2norm", ct), in1=rstd,
                      op0=ALU.mult, op1=ALU.mult)
                first_m = (n_merged[0] == 0)
                n_merged[0] += 1
                mtmp = [ar([NT], BF16) for _ in range(2)]

                def ev_wb(m, b):
                    A(out=ytmp[:, m, :], in_=b[:], func=AF.Copy)
                    if first_m:
                        G("tensor_tensor", out=mrg[:, m, :], in0=sgb[:, m, :], in1=ytmp[:, m, :], op=ALU.mult)
                    else:
                        G("tensor_tensor", out=mtmp[m % 2], in0=sgb[:, m, :], in1=ytmp[:, m, :], op=ALU.mult)
                        G("tensor_tensor", out=mrg[:, m, :], in0=mrg[:, m, :], in1=mtmp[m % 2], op=ALU.add)
                proj("wb", 1024, lambda k: xbcs[:, k, :], 16, ev_wb)

            if n_merged[0] > 0:
                def ev_wo(m, b):
                    V("tensor_tensor", out=xT[:, m, :], in0=b[:], in1=xT[:, m, :], op=ALU.add)
                proj("wo", 1024, lambda k: mrg[:, k, :], 8, ev_wo)

            if ti + 1 < NTILES:
                ar_reset(40 * 1024)
                xin_next[0] = ar([NCH, D])
                ns_i, nt0 = (ti + 1) // TPS, ((ti + 1) % TPS) * NT
                tk.dma("act", xin_next[0], x_d[ns_i, nt0:nt0 + NT, :].rearrange("(s p) d -> p s d", p=128))
            if cfg.ffn:
                ar_reset()
                rmsnorm_to_hT("g2")
                ar_reset()
                gact = ar([22, NT], BF16)
                raw = [ar([NT + 2], BF16) for _ in range(3)]
                ctm = [ar([NT]) for _ in range(3)]
                if first:
                    G("memset", ap=fhalo[:], constant=0.0)

                pend = []

                def ev_up(m, b):
                    r = raw[m % 3]
                    tm = ctm[m % 3]
                    G("tensor_copy", out=r[:, 0:2], in_=fhalo[:, m, :])
                    A(out=r[:, 2:NT + 2], in_=b[:], func=AF.Copy)
                    G("tensor_copy", out=fhalo[:, m, :], in_=r[:, NT:NT + 2])
                    G("tensor_scalar", out=tm, in0=r[:, 0:NT], scalar1=pc("fcw", m), scalar2=pc("fcb", m),
                      op0=ALU.mult, op1=ALU.add)
                    for kk in range(1, 3):
                        V("scalar_tensor_tensor", out=tm, in0=r[:, kk:NT + kk], scalar=pc("fcw", 44 * kk + m), in1=tm,
                          op0=ALU.mult, op1=ALU.add)
                    while pend:
                        pend.pop(0)()
                    if m < 22:
                        pend.append(lambda m=m, tm=tm: A(out=gact[:, m, :], in_=tm, func=AF.Silu))
                    else:
                        pend.append(lambda m=m, tm=tm: V("tensor_tensor", out=gact[:, m - 22, :], in0=gact[:, m - 22, :],
                                                         in1=tm, op=ALU.mult))
                proj("up", 2 * D_FF, lambda k: hT[:, k, :], 8, ev_up)
                while pend:
                    pend.pop(0)()

                def ev_dn(m, b):
                    V("tensor_tensor", out=xT[:, m, :], in0=b[:], in1=xT[:, m, :], op=ALU.add)
                proj("dn", 1024, lambda k: gact[:, k, :], 22, ev_dn)

            ar_reset(56 * 1024)
            sqf = ar([8, NT], BF16)
            xout = ar([NCH, D])
            xfin = ar([8, NT])
            rstd = rms_stats(lambda ct: xT[:, ct, :], 8, NT, sqf, D)
            for ct in range(8):
                V("scalar_tensor_tensor", out=xfin[:, ct, :], in0=xT[:, ct, :], scalar=pc("gf", ct), in1=rstd,
                  op0=ALU.mult, op1=ALU.mult)

            def final_tail(xout=xout, xfin=xfin, s_i=s_i, t0=t0):
                for s in range(NCH):
                    for half in range(2):
                        b = bank()
                        for c4 in range(4):
                            ct = half * 4 + c4
                            TR(out=b[:, c4 * 128:(c4 + 1) * 128], in_=xfin[:, ct, s * 128:(s + 1) * 128], identity=ident_f[:])
                        A(out=xout[:, s, half * 512:(half + 1) * 512], in_=b[:], func=AF.Copy)
                tk.dma("sp", out_d[s_i, t0:t0 + NT, :].rearrange("(s p) d -> p s d", p=128), xout, final=True)
            pending_tail[0] = final_tail

        pending_tail[0]()
        tk.finish()
    return nc


def _cols(v):
    v = np.asarray(v, np.float32).reshape(-1, 128)
    return np.ascontiguousarray(v.T)


def _const_mats():
    cm = np.zeros((128, NCM), np.float32)
    r = np.arange(128)
    cm[:, CM_ID:CM_ID + 128] = np.eye(128)
    cm[:, CM_U:CM_U + 128] = (r[:, None] <= r[None, :])
    cm[:, CM_NM:CM_NM + 128] = np.where(r[None, :] < r[:, None], -30000.0, 0.0)
    pm = np.zeros((128, 128), np.float32)
    for rp in range(64):
        pm[rp + 64, rp] = 1.0
        pm[rp, rp + 64] = -1.0
    cm[:, CM_PERM:CM_PERM + 128] = pm
    cm[:, CM_BM:CM_BM + 128] = ((r[None, :] // 16) >= (r[:, None] // 16))
    cm[:, CM_L2:CM_L2 + 128] = (r[:, None] > r[None, :])
    for a in range(8):
        z = np.zeros((128, 256), np.float32)
        for k in range(16):
            z[16 * a + k, 128 + k] = 1.0
        cm[:, CM_Z + 256 * a:CM_Z + 256 * (a + 1)] = z
    return cm


def make_inputs(cfg, inp, b0):
    pcv = np.zeros((128, NPC), np.float32)

    def put(name, arr, i=0):
        arr = np.asarray(arr, np.float32)
        pcv[:, PCO[name] + i:PCO[name] + i + arr.shape[1]] = arr

    put("g1", _cols(inp["norm_mix"][0]))
    put("g2", _cols(inp["norm_ffn"][0]))
    put("gf", _cols(inp["norm_final"]))
    put("gmem", _cols(inp["norm_mem"][0]))
    for k in range(3):
        put("fcw", _cols(inp["ffn_conv_w"][0][k]), 44 * k)
    put("fcb", _cols(inp["ffn_conv_b"][0]))
    for k in range(4):
        put("m2cw", _cols(inp["m2_conv_w"][0][k]), 24 * k)
    put("m2cb", _cols(inp["m2_conv_b"][0]))
    put("m2norm", _cols(inp["m2_norm"][0]))
    md = np.asarray(inp["m2_d"][0], np.float32)
    put("dcol", np.repeat(md.reshape(16, 2).T, 64, axis=0))
    put("dtb", np.tile(np.asarray(inp["m2_dt_bias"][0], np.float32)[None, :], (128, 1)))
    put("alog", np.tile(np.asarray(inp["m2_a_log"][0], np.float32)[None, :], (128, 1)))
    pcv[:, PCO["eps"]] = EPS
    put("lre", np.tile(np.asarray(inp["s5_lambda_re"][0], np.float32).T, (2, 1)))
    put("lim", np.tile(np.asarray(inp["s5_lambda_im"][0], np.float32).T, (2, 1)))
    put("ldt", np.tile(np.asarray(inp["s5_log_dt"][0], np.float32)[None, :], (128, 1)))
    sd = np.asarray(inp["s5_d"][0], np.float32).reshape(64, 16)
    put("s5d", np.tile(sd.T, (8, 1)))
    half = (np.arange(128) >= 64)
    pcv[:, PCO["ph1"]] = np.where(half, -np.pi / 2, 0.0)
    pcv[:, PCO["ph2"]] = np.where(half, np.pi, -np.pi / 2)
    pcv[:, PCO["psi"]] = np.where(half, np.pi / 2, 0.0)
    nv = [0, -1, -2, -3, -4, -5, -6, -7] + [7, 6, 5, 4, 3, 2, 1, 0] + [0, 1, 2, 3, 4, 5, 6, 7] + [1, 2, 3, 4, 5, 6, 7, 8]
    put("nv", np.tile(np.asarray(nv, np.float32)[None, :], (128, 1)))
    put("bmul", np.tile((8.0 * (np.arange(64, dtype=np.float32) + 1.0))[None, :], (128, 1)))
    s5p = np.zeros((128, 4, 1024), np.float32)
    bre = np.asarray(inp["s5_b_re"][0], np.float32)
    bim = np.asarray(inp["s5_b_im"][0], np.float32)
    cre = np.asarray(inp["s5_c_re"][0], np.float32)
    cim = np.asarray(inp["s5_c_im"][0], np.float32)
    s5p[:, 0] = np.tile(bre.transpose(1, 0, 2).reshape(64, 1024), (2, 1))
    s5p[:, 1] = np.tile(bim.transpose(1, 0, 2).reshape(64, 1024), (2, 1))
    s5p[:, 2] = np.tile(cre.transpose(2, 0, 1).reshape(64, 1024), (2, 1))
    s5p[:, 3] = np.tile(cim.transpose(2, 0, 1).reshape(64, 1024), (2, 1))
    m = {
        "x": np.ascontiguousarray(inp["x"][b0:b0 + cfg.nseq, :cfg.seq]),
        "mem": np.ascontiguousarray(inp["mem"][b0:b0 + cfg.nseq]),
        "pcols": pcv,
        "cmat": _const_mats(),
        "s5p": s5p,
    }
    for k in ("w_in", "w_a_val", "w_a_gate", "w_b", "w_kv", "w_c", "w_out", "w_up", "w_down"):
        m[k] = np.ascontiguousarray(inp[k][0])
    return m


_NC_CACHE = {}


def kernel(**inputs):
    cfg = Cfg()
    inp = {k: np.asarray(v) for k, v in inputs.items()}
    key = "full"
    if key not in _NC_CACHE:
        _NC_CACHE[key] = build(cfg)
    nc = _NC_CACHE[key]
    in_maps = [make_inputs(cfg, inp, 2 * c) for c in range(8)]
    res = run_bass_kernel_spmd(nc, in_maps, core_ids=list(range(8)))
    out = np.concatenate([r["out"] for r in res.results], axis=0)
    return out.astype(np.float32)
```

```python
import contextlib
import numpy as np
import concourse.bass as bass
import concourse.mybir as mybir
from concourse.bass_utils import run_bass_kernel_spmd

F32 = mybir.dt.float32
BF16 = mybir.dt.bfloat16
I32 = mybir.dt.int32
AF = mybir.ActivationFunctionType
ALU = mybir.AluOpType

D = 1024
NT = 512
MEM = 256
D_FF = 2816
EPS = 1e-6
P1, P2, P3, P4, P5 = 1024, 3072, 6144, 6176, 6688
D_IN = 9760


class TK:
    def __init__(self, nc, es):
        self.nc = nc
        self.es = es
        self.engs = {"pe": nc.tensor, "act": nc.scalar, "dve": nc.vector, "pool": nc.gpsimd, "sp": nc.sync}
        self.sems = {}
        self.cnt = {}
        for e in self.engs:
            self.sems[e] = es.enter_context(nc.semaphore("s_" + e))
            self.cnt[e] = 0
        self.seen = {e: {} for e in self.engs}
        self.recs = {}
        self.dq = {}
        for q, n in (("sp", 12), ("pool", 8), ("act", 4)):
            lst = []
            for i in range(n):
                key = "d_%s%d" % (q, i)
                self.sems[key] = es.enter_context(nc.semaphore(key))
                self.cnt[key] = 0
                lst.append(key)
            self.dq[q] = [lst, 0]
        self.final_waits = []

    @staticmethod
    def _acc(ap):
        name = ap.name
        sp = str(ap.space)
        pairs = ap.ap
        off = ap.offset
        if "DRAM" in sp:
            ext = 0
            for st, c in pairs:
                ext += abs(st) * (c - 1)
            return (name, 0, 1, off, off + ext + 1)
        pst, pc = pairs[0]
        if pst == 0:
            pst = 1 << 40
        p0 = off // pst
        f0 = off % pst
        ext = 0
        for st, c in pairs[1:]:
            ext += abs(st) * (c - 1)
        esz = 4 if ap.dtype in (F32, I32) else 2
        if "PSUM" in sp:
            b0 = (f0 * esz) // 2048
            b1 = ((f0 + ext) * esz) // 2048
            return (name, p0, p0 + pc, b0 * 2048, (b1 + 1) * 2048)
        return (name, p0, p0 + pc, f0 * esz, (f0 + ext + 1) * esz)

    @staticmethod
    def _accs(ap):
        base = TK._acc(ap)
        sp = str(ap.space)
        if "DRAM" in sp or "PSUM" in sp:
            return [base]
        pairs = ap.ap
        if len(pairs) < 3:
            return [base]
        st0, c0 = pairs[1]
        if c0 <= 1 or c0 > 64 or st0 <= 0:
            return [base]
        rest = 0
        for st, c in pairs[2:]:
            if st < 0:
                return [base]
            rest += st * (c - 1)
        rest += 1
        if st0 <= rest:
            return [base]
        name, p0, p1, f0b, _ = base
        esz = 4 if ap.dtype in (F32, I32) else 2
        return [(name, p0, p1, f0b + i * st0 * esz, f0b + (i * st0 + rest) * esz) for i in range(c0)]

    def _deps(self, e, reads, writes):
        need = {}
        for (acc, isw) in [(a, False) for a in reads] + [(a, True) for a in writes]:
            name, p0, p1, f0, f1 = acc
            lst = self.recs.get(name)
            if not lst:
                continue
            psum = name.startswith("ps")
            for r in lst:
                if not (isw or r[6]):
                    if not (psum and r[4] != e):
                        continue
                if r[0] >= p1 or r[1] <= p0 or r[2] >= f1 or r[3] <= f0:
                    continue
                if e == "pe" and r[4] == "pe":
                    continue
                k, v = r[4], r[5]
                if need.get(k, 0) < v:
                    need[k] = v
        return need

    def _emit_waits(self, e, need):
        seen = self.seen[e]
        eng = self.engs[e]
        for k, v in need.items():
            if seen.get(k, 0) < v:
                eng.wait_ge(self.sems[k], v)
                seen[k] = v

    def _record(self, reads, writes, key, val):
        for acc in writes:
            name, p0, p1, f0, f1 = acc
            lst = self.recs.setdefault(name, [])
            lst[:] = [r for r in lst if not (r[0] >= p0 and r[1] <= p1 and r[2] >= f0 and r[3] <= f1)]
            lst.append([p0, p1, f0, f1, key, val, True])
        for acc in reads:
            name, p0, p1, f0, f1 = acc
            lst = self.recs.setdefault(name, [])
            for r in lst:
                if (not r[6]) and r[4] == key and r[0] == p0 and r[1] == p1 and r[2] == f0 and r[3] == f1:
                    r[5] = val
                    break
            else:
                lst.append([p0, p1, f0, f1, key, val, False])

    def op(self, e, fn, **kw):
        reads, writes = [], []
        for k, v in kw.items():
            if hasattr(v, "ap") and hasattr(v, "space"):
                if k in ("out", "accum_out", "ap"):
                    writes.extend(self._accs(v))
                else:
                    reads.extend(self._accs(v))
        need = self._deps(e, reads, writes)
        self._emit_waits(e, need)
        inst = getattr(self.engs[e], fn)(**kw)
        self.cnt[e] += 1
        inst.then_inc(self.sems[e], 1)
        self._record(reads, writes, e, self.cnt[e])
        return inst

    def dma(self, q, out, in_, final=False):
        reads = self._accs(in_)
        writes = self._accs(out)
        need = self._deps(q, reads, writes)
        lst, idx = self.dq[q]
        key = lst[idx % len(lst)]
        self.dq[q][1] = idx + 1
        if self.cnt[key] > 0:
            need[key] = max(need.get(key, 0), self.cnt[key])
        self._emit_waits(q, need)
        inst = self.engs[q].dma_start(out=out, in_=in_)
        self.cnt[key] += 16
        inst.then_inc(self.sems[key], 16)
        self._record(reads, writes, key, self.cnt[key])
        if final:
            self.final_waits.append((q, key, self.cnt[key]))

    def finish(self):
        for q, key, v in self.final_waits:
            self._emit_waits(q, {key: v})
        for e in self.engs:
            need = {f: self.cnt[f] for f in ("pe", "act", "dve", "pool", "sp") if self.cnt[f] > 0}
            self._emit_waits(e, need)


class Cfg:
    def __init__(self, seq=2048, nseq=2, s5=True, ssd=True, xa=True, ffn=True, nt=512):
        self.seq = seq
        self.nseq = nseq
        self.s5 = s5
        self.ssd = ssd
        self.xa = xa
        self.ffn = ffn
        self.nt = nt


def pc_layout():
    off = {}
    o = 0
    for name, n in (("g1", 8), ("g2", 8), ("gf", 8), ("gmem", 8), ("fcw", 132), ("fcb", 44), ("m2cw", 96),
                    ("m2cb", 24), ("m2norm", 16), ("dcol", 16), ("dtb", 32), ("alog", 32), ("eps", 1),
                    ("lre", 64), ("lim", 64), ("ldt", 64), ("s5d", 64), ("ph1", 1), ("ph2", 1), ("psi", 1),
                    ("nv", 32), ("bmul", 64)):
        off[name] = o
        o += n
    off["_n"] = o
    return off


PCO = pc_layout()
NPC = PCO["_n"]
CM_ID, CM_U, CM_NM, CM_PERM, CM_BM, CM_Z = 0, 128, 256, 384, 512, 640
CM_L2 = 640 + 8 * 256
NCM = CM_L2 + 128
SLOT = 4096
NSLOT = 4


def _wblocks(cfg):
    blocks = []

    def add(wname, K, c0, ncols, ncb, tag):
        nk = K // 128
        for i in range(0, ncols, ncb):
            blocks.append((wname, nk, c0 + i, min(ncb, ncols - i), tag))

    if cfg.s5:
        add("w_in", D, 0, 1024, 512, "u")
        for i in range(4):
            blocks.append(("s5st", 32, i, 128, "st"))
    if cfg.xa:
        add("w_in", D, P4, 512, 512, "q")
        add("w_c", 512, 0, 1024, 1024, "wc")
        add("w_in", D, P5 + 2048, 1024, 512, "gc")
    if cfg.s5:
        for i in range(4):
            blocks.append(("s5to", 32, i, 128, "to"))
        add("w_a_val", D, 0, 1024, 512, "av")
        add("w_a_gate", D, 0, 1024, 512, "ag")
        add("w_in", D, P5, 1024, 512, "ga")
    if cfg.ssd:
        add("w_in", D, P3, 32, 32, "dt")
        add("w_in", D, P1, 2048, 512, "z")
        add("w_in", D, P2, 3072, 512, "xbc")
        add("w_in", D, P5 + 1024, 1024, 512, "gb")
        add("w_b", 2048, 0, 1024, 256, "wb")
    if cfg.s5 or cfg.xa or cfg.ssd:
        add("w_out", D, 0, 1024, 512, "wo")
    if cfg.ffn:
        add("w_up", D, 0, 2 * D_FF, 512, "up")
        add("w_down", D_FF, 0, 1024, 128, "dn")
    return blocks


def build(cfg):
    nc = bass.Bass("TRN2", target_bir_lowering=False)
    es = contextlib.ExitStack()
    SEQ, NSEQ, NT = cfg.seq, cfg.nseq, cfg.nt
    TPS = SEQ // NT
    NTILES = TPS * NSEQ
    NCH = NT // 128
    NB = NT // 8
    TWO_PI = float(2 * np.pi)

    def din(name, shape, dt=F32):
        return nc.dram_tensor(name, list(shape), dt, kind="ExternalInput").ap()

    x_d = din("x", [NSEQ, SEQ, D])
    mem_d = din("mem", [NSEQ, MEM, D])
    w_d = {
        "w_in": din("w_in", [D, D_IN]),
        "w_a_val": din("w_a_val", [D, D]),
        "w_a_gate": din("w_a_gate", [D, D]),
        "w_b": din("w_b", [2048, D]),
        "w_kv": din("w_kv", [D, 1024]),
        "w_c": din("w_c", [512, D]),
        "w_out": din("w_out", [D, D]),
        "w_up": din("w_up", [D, 2 * D_FF]),
        "w_down": din("w_down", [D_FF, D]),
    }
    pc_d = din("pcols", [128, NPC])
    cm_d = din("cmat", [128, NCM])
    s5p_d = din("s5p", [128, 4, 1024])
    out_d = nc.dram_tensor("out", [NSEQ, SEQ, D], F32, kind="ExternalOutput").ap()

    blocks = _wblocks(cfg)
    NBK = len(blocks)
    wscr = nc.dram_tensor("wscr", [NBK, 128, SLOT], BF16, kind="Internal").ap()

    with es:
        tk = TK(nc, es)

        def sb(name, shape, dt=F32):
            return es.enter_context(nc.sbuf_tensor("S_" + name, list(shape), dt))

        banks = [es.enter_context(nc.psum_tensor("ps%d" % i, [128, 512], F32)) for i in range(8)]
        bank_i = [0]

        def bank():
            b = banks[bank_i[0] % 8]
            bank_i[0] += 1
            return b

        def V(fn, **kw):
            return tk.op("dve", fn, **kw)

        def G(fn, **kw):
            return tk.op("pool", fn, **kw)

        def A(**kw):
            return tk.op("act", "activation", **kw)

        def MM(**kw):
            return tk.op("pe", "matmul", **kw)

        def TR(**kw):
            return tk.op("pe", "transpose", **kw)

        ARENA = 96 * 1024
        arena = sb("arena", [128, ARENA // 2], BF16)
        ar_off = [0]

        def ar_reset(o=0):
            ar_off[0] = o

        def ar(shape, dt=F32):
            esz = 4 if dt in (F32, I32) else 2
            n = int(np.prod(shape))
            o = (ar_off[0] + 63) // 64 * 64
            assert o + n * esz <= ARENA, ("arena overflow", o, n * esz)
            ar_off[0] = o + n * esz
            v = arena[:, o // 2:o // 2 + n * esz // 2]
            if dt != BF16:
                v = v.bitcast(dt)
            if len(shape) == 2:
                return v.rearrange("p (a b) -> p a b", a=shape[0])
            if len(shape) == 3:
                return v.rearrange("p (a b c) -> p a b c", a=shape[0], b=shape[1])
            return v

        pcols = sb("pcols", [128, NPC])
        ident_f = sb("ident_f", [128, 128])
        bm_f = sb("bm_f", [128, 128])
        cb = sb("cb", [128, 4 * 128 + 512 + 8 * 256 + 128], BF16)
        ident_b, ones_b, U_b, perm_b = cb[:, 0:128], cb[:, 128:256], cb[:, 256:384], cb[:, 384:512]
        nm4_b = cb[:, 512:1024]
        Z_b = cb[:, 1024:1024 + 2048].rearrange("p (a c) -> p a c", a=8)
        L2_b = cb[:, 3072:3200]
        tk.dma("sp", pcols[:], pc_d[:, :])
        ar_reset()
        cm = ar([NCM])
        tk.dma("sp", cm, cm_d[:, :])
        V("tensor_copy", out=ident_f[:], in_=cm[:, CM_ID:CM_ID + 128])
        V("tensor_copy", out=bm_f[:], in_=cm[:, CM_BM:CM_BM + 128])
        V("tensor_copy", out=ident_b, in_=cm[:, CM_ID:CM_ID + 128])
        V("memset", ap=ones_b, constant=1.0)
        V("tensor_copy", out=U_b, in_=cm[:, CM_U:CM_U + 128])
        V("tensor_copy", out=perm_b, in_=cm[:, CM_PERM:CM_PERM + 128])
        for i in range(4):
            V("tensor_copy", out=nm4_b[:, i * 128:(i + 1) * 128], in_=cm[:, CM_NM:CM_NM + 128])
        V("tensor_copy", out=cb[:, 1024:1024 + 2048], in_=cm[:, CM_Z:CM_Z + 2048])
        V("tensor_copy", out=L2_b, in_=cm[:, CM_L2:CM_L2 + 128])

        def pc(name, i=0, n=1):
            o = PCO[name] + i
            return pcols[:, o:o + n]

        if cfg.s5:
            rbar = sb("s5_rbar", [128, 64])
            cosT = sb("s5_cos", [128, 64, NB], BF16)
            sinT = sb("s5_sin", [128, 64, NB], BF16)
            carry = sb("s5_carry", [128, 64])
            ar_reset()
            c_re = ar([64, 16]); c_im = ar([64, 16])
            bbr = ar([64, 16]); bbi = ar([64, 16])
            ctab = {}
            for key in ("L", "S1", "S2", "R", "O"):
                for sh in (0, 1):
                    ctab[(key, sh)] = ar([8, 64])
            dtg = ar([64]); lrd = ar([64]); th = ar([64]); t0_ = ar([64]); t1_ = ar([64]); t2_ = ar([64])
            arr = ar([64]); aii = ar([64]); fr = ar([64]); fi = ar([64])
            save = ar_off[0]
            b_re = ar([64, 16]); b_im = ar([64, 16]); tb = ar([64, 16])
            ANG = ar([32, 64]); MAGN = ar([32, 64])
            tmpa = ar([8, 64])
            ki = ar([1024], I32)
            kf_ = ar([1024]); tt_ = ar([1024])
            s5v = s5p_d.rearrange("p a (g k) -> p a g k", g=64)
            tk.dma("sp", b_re, s5v[:, 0])
            tk.dma("sp", b_im, s5v[:, 1])
            tk.dma("sp", c_re, s5v[:, 2])
            tk.dma("sp", c_im, s5v[:, 3])

            def sincos_reduce(dst, src, n):
                kv = ki[:, 0:n]; kf = kf_[:, 0:n]; tt = tt_[:, 0:n]
                V("tensor_scalar", out=kf, in0=src, scalar1=float(1.0 / TWO_PI), scalar2=64.0, op0=ALU.mult, op1=ALU.add)
                V("tensor_copy", out=kv, in_=kf)
                V("tensor_copy", out=kf, in_=kv)
                V("tensor_scalar", out=tt, in0=src, scalar1=float(64 * TWO_PI), scalar2=None, op0=ALU.add)
                V("scalar_tensor_tensor", out=tt, in0=kf, scalar=-TWO_PI, in1=tt, op0=ALU.mult, op1=ALU.add)
                A(out=dst, in_=tt, func=AF.Sin)

            A(out=dtg, in_=pc("ldt", 0, 64), func=AF.Exp)
            V("tensor_tensor", out=lrd, in0=pc("lre", 0, 64), in1=dtg, op=ALU.mult)
            V("tensor_tensor", out=th, in0=pc("lim", 0, 64), in1=dtg, op=ALU.mult)
            A(out=t0_, in_=lrd, func=AF.Exp)
            V("tensor_scalar", out=t1_, in0=th, scalar1=float(np.pi / 2), scalar2=None, op0=ALU.add)
            sincos_reduce(arr, t1_, 64)
            sincos_reduce(aii, th, 64)
            V("tensor_tensor", out=arr, in0=arr, in1=t0_, op=ALU.mult)
            V("tensor_tensor", out=aii, in0=aii, in1=t0_, op=ALU.mult)
            lr, li = pc("lre", 0, 64), pc("lim", 0, 64)
            V("tensor_tensor", out=t1_, in0=lr, in1=lr, op=ALU.mult)
            V("tensor_tensor", out=t2_, in0=li, in1=li, op=ALU.mult)
            V("tensor_tensor", out=t1_, in0=t1_, in1=t2_, op=ALU.add)
            V("reciprocal", out=t1_, in_=t1_)
            V("tensor_scalar", out=t0_, in0=arr, scalar1=-1.0, scalar2=None, op0=ALU.add)
            V("tensor_tensor", out=fr, in0=t0_, in1=lr, op=ALU.mult)
            V("tensor_tensor", out=t2_, in0=aii, in1=li, op=ALU.mult)
            V("tensor_tensor", out=fr, in0=fr, in1=t2_, op=ALU.add)
            V("tensor_tensor", out=fr, in0=fr, in1=t1_, op=ALU.mult)
            V("tensor_tensor", out=fi, in0=aii, in1=lr, op=ALU.mult)
            V("tensor_tensor", out=t2_, in0=t0_, in1=li, op=ALU.mult)
            V("tensor_tensor", out=fi, in0=fi, in1=t2_, op=ALU.subtract)
            V("tensor_tensor", out=fi, in0=fi, in1=t1_, op=ALU.mult)
            frb = fr.unsqueeze(2).to_broadcast([128, 64, 16])
            fib = fi.unsqueeze(2).to_broadcast([128, 64, 16])
            V("tensor_tensor", out=bbr, in0=b_re, in1=frb, op=ALU.mult)
            V("tensor_tensor", out=tb, in0=b_im, in1=fib, op=ALU.mult)
            V("tensor_tensor", out=bbr, in0=bbr, in1=tb, op=ALU.subtract)
            V("tensor_tensor", out=bbi, in0=b_im, in1=frb, op=ALU.mult)
            V("tensor_tensor", out=tb, in0=b_re, in1=fib, op=ALU.mult)
            V("tensor_tensor", out=bbi, in0=bbi, in1=tb, op=ALU.add)
            nvb = pc("nv", 0, 32).unsqueeze(2).to_broadcast([128, 32, 64])
            V("tensor_tensor", out=ANG, in0=th.unsqueeze(1).to_broadcast([128, 32, 64]), in1=nvb, op=ALU.mult)
            V("tensor_tensor", out=MAGN, in0=lrd.unsqueeze(1).to_broadcast([128, 32, 64]), in1=nvb, op=ALU.mult)
            A(out=MAGN, in_=MAGN, func=AF.Exp)
            for key, st, phn in (("L", 0, "ph1"), ("S1", 1, "ph1"), ("S2", 1, "ph2"), ("R", 2, "psi"), ("O", 3, "psi")):
                for sh in (0, 1):
                    dst = ctab[(key, sh)]
                    V("tensor_scalar", out=tmpa, in0=ANG[:, st * 8:(st + 1) * 8, :], scalar1=pc(phn),
                      scalar2=None, op0=ALU.add)
                    V("tensor_scalar", out=tmpa, in0=tmpa, scalar1=float(np.pi / 2 * (1 + sh)), scalar2=None, op0=ALU.add)
                    sincos_reduce(dst.rearrange("p a b -> p (a b)"), tmpa.rearrange("p a b -> p (a b)"), 512)
                    V("tensor_tensor", out=dst, in0=dst, in1=MAGN[:, st * 8:(st + 1) * 8, :], op=ALU.mult)
            A(out=rbar[:], in_=lrd, func=AF.Exp, scale=8.0)
            NBC = 16
            bang = ANG[:, 0:16, :].rearrange("p a b -> p (a b)")
            stmp = MAGN[:, 0:16, :].rearrange("p a b -> p (a b)")
            bang3 = bang.rearrange("p (g b) -> p g b", g=64)
            for b0 in range(0, NB, NBC):
                V("tensor_tensor", out=bang3, in0=th.unsqueeze(2).to_broadcast([128, 64, NBC]),
                  in1=pc("bmul", b0, NBC).unsqueeze(1).to_broadcast([128, 64, NBC]), op=ALU.mult)
                sincos_reduce(stmp, bang, 64 * NBC)
                V("tensor_copy", out=sinT[:, :, b0:b0 + NBC], in_=stmp.rearrange("p (g b) -> p g b", g=64))
                V("tensor_scalar", out=bang, in0=bang, scalar1=float(np.pi / 2), scalar2=None, op0=ALU.add)
                sincos_reduce(stmp, bang, 64 * NBC)
                V("tensor_copy", out=cosT[:, :, b0:b0 + NBC], in_=stmp.rearrange("p (g b) -> p g b", g=64))
            st_bi = [i for i, bl in enumerate(blocks) if bl[4] == "st"]
            to_bi = [i for i, bl in enumerate(blocks) if bl[4] == "to"]
            ar_reset(save)
            stS = ar([16, 2, 128], BF16)
            stT = ar([16, 2, 128], BF16)
            tq = ar([8, 8, 16])
            tq2 = ar([8, 8, 16])
            tm = ar([4, 128])
            tabs = {key: ar([8, 8, 16]) for key in ("L", "S1", "S2", "R", "O")}
            for bt in range(4):
                for sub in range(2):
                    g0 = bt * 16 + sub * 8
                    for key, P_re, P_im in (("L", bbr, bbi), ("S1", bbr, bbi), ("S2", bbr, bbi), ("R", c_re, c_im), ("O", c_re, c_im)):
                        t4 = tabs[key]
                        c0 = ctab[(key, 0)][:, :, g0:g0 + 8].rearrange("p n g -> p g n").unsqueeze(3).to_broadcast([128, 8, 8, 16])
                        c1 = ctab[(key, 1)][:, :, g0:g0 + 8].rearrange("p n g -> p g n").unsqueeze(3).to_broadcast([128, 8, 8, 16])
                        pr = P_re[:, g0:g0 + 8, :].unsqueeze(2).to_broadcast([128, 8, 8, 16])
                        pi_ = P_im[:, g0:g0 + 8, :].unsqueeze(2).to_broadcast([128, 8, 8, 16])
                        EW = G if key in ("S1", "S2", "O") else V
                        tqq = tq2 if key in ("S1", "S2", "O") else tq
                        EW("tensor_tensor", out=t4, in0=pr, in1=c0, op=ALU.mult)
                        EW("tensor_tensor", out=tqq, in0=pi_, in1=c1, op=ALU.mult)
                        EW("tensor_tensor", out=t4, in0=t4, in1=tqq, op=ALU.add)
                    for q in range(2):
                        b = bank()
                        for gg in range(4):
                            gl = q * 4 + gg
                            MM(out=b[:, gg * 128:(gg + 1) * 128], lhsT=tabs["L"][:, gl].rearrange("p a b -> p (a b)"),
                               rhs=tabs["R"][:, gl].rearrange("p a b -> p (a b)"), start=True, stop=True)
                        V("tensor_tensor", out=tm, in0=b[:].rearrange("p (a c) -> p a c", a=4),
                          in1=bm_f[:].unsqueeze(1).to_broadcast([128, 4, 128]), op=ALU.mult)
                        for gg in range(4):
                            gl = q * 4 + gg
                            V("scalar_tensor_tensor", out=stT[:, sub * 8 + gl, 0, :], in0=ident_f[:], scalar=pc("s5d", g0 + gl),
                              in1=tm[:, gg, :], op0=ALU.mult, op1=ALU.add)
                        for si_, key in ((0, "S1"), (1, "S2")):
                            b2 = bank()
                            for gg in range(4):
                                gl = q * 4 + gg
                                TR(out=b2[:, gg * 128:(gg + 1) * 128], in_=tabs[key][:, gl].rearrange("p a b -> p (a b)"),
                                   identity=ident_f[:])
                            A(out=stS[:, sub * 8 + q * 4:sub * 8 + (q + 1) * 4, si_, :],
                              in_=b2[:].rearrange("p (a c) -> p a c", a=4), func=AF.Copy)
                    V("tensor_copy", out=stT[:, sub * 8:(sub + 1) * 8, 1, :], in_=tabs["O"].rearrange("p g a b -> p g (a b)"))
                tk.dma("sp", wscr[st_bi[bt], :, :], stS.rearrange("p a b c -> p (a b c)"))
                tk.dma("sp", wscr[to_bi[bt], :, :], stT.rearrange("p a b c -> p (a b c)"))

        xT = sb("xT", [128, 8, NT])
        hT = sb("hT", [128, 8, NT], BF16)
        mrg = sb("mrg", [128, 8, NT], BF16)
        slots = [sb("wslot%d" % i, [128, SLOT], BF16) for i in range(NSLOT)]
        if cfg.ffn:
            fhalo = sb("fhalo", [128, 44, 2], BF16)
        if cfg.xa:
            KT = sb("KT", [128, 4, NSEQ * MEM], BF16)
            Vm = sb("Vm", [128, NSEQ * 2, 512], BF16)
        if cfg.ssd:
            hs = sb("ssd_hs", [128, 2048])
            hsb = sb("ssd_hsb", [128, 2048], BF16)
            mhalo = sb("mhalo", [128, 24, 3], BF16)
            Aneg = sb("Aneg", [128, 32])
            A(out=Aneg[:], in_=pc("alog", 0, 32), func=AF.Exp)
            V("tensor_scalar", out=Aneg[:], in0=Aneg[:], scalar1=-1.0, scalar2=None, op0=ALU.mult)

        stream = {"next_issue": 0, "next_use": 0}
        total_blocks = NBK * NTILES

        def issue_upto(n):
            while stream["next_issue"] < min(n, total_blocks):
                i = stream["next_issue"]
                bi = i % NBK
                wname, nk, c0, ncb, tag = blocks[bi]
                sl = slots[i % NSLOT][:, 0:nk * ncb]
                if i < NBK and not wname.startswith("s5"):
                    src = w_d[wname][:, c0:c0 + ncb].rearrange("(k p) c -> p k c", p=128)
                    tk.dma("pool", sl.rearrange("p (k c) -> p k c", k=nk), src)
                    if NTILES > 1:
                        tk.dma("sp", wscr[bi, :, 0:nk * ncb], sl)
                else:
                    tk.dma("sp", sl, wscr[bi, :, 0:nk * ncb])
                stream["next_issue"] += 1

        def next_block(tag):
            i = stream["next_use"]
            bi = i % NBK
            wname, nk, c0, ncb, btag = blocks[bi]
            assert btag == tag, (btag, tag)
            issue_upto(i + NSLOT)
            stream["next_use"] += 1
            return slots[i % NSLOT][:, 0:nk * ncb].rearrange("p (k c) -> p k c", k=nk), nk, ncb

        def proj(tag, ncols, rhs_fn, nk, evac, n=NT):
            m = 0
            done = 0
            while done < ncols:
                wv, wnk, ncb = next_block(tag)
                assert wnk == nk
                for mm in range(ncb // 128):
                    b = bank()
                    for k in range(nk):
                        MM(out=b[:, 0:n], lhsT=wv[:, k, mm * 128:(mm + 1) * 128], rhs=rhs_fn(k),
                           start=(k == 0), stop=(k == nk - 1))
                    evac(m, b)
                    m += 1
                done += ncb

        def rms_stats(src_fn, nct, n, sq, denom):
            for ct in range(nct):
                A(out=sq[:, ct, 0:n], in_=src_fn(ct), func=AF.Square)
            b = bank()
            for ct in range(nct):
                MM(out=b[:, 0:n], lhsT=ones_b, rhs=sq[:, ct, 0:n], start=(ct == 0), stop=(ct == nct - 1))
            rs = sq[:, 0:2, :].rearrange("p a b -> p (a b)").bitcast(F32)[:, 0:n]
            A(out=rs, in_=b[:, 0:n], func=AF.Sqrt, scale=1.0 / denom, bias=pc("eps"))
            V("reciprocal", out=rs, in_=rs)
            return rs

        def rmsnorm_to_hT(gname):
            sq = ar([8, NT], BF16)
            rstd = rms_stats(lambda ct: xT[:, ct, :], 8, NT, sq, D)
            for ct in range(8):
                V("scalar_tensor_tensor", out=hT[:, ct, :], in0=xT[:, ct, :], scalar=pc(gname, ct), in1=rstd,
                  op0=ALU.mult, op1=ALU.mult)

        n_merged = [0]

        def merge(gtag, ysrc):
            first = (n_merged[0] == 0)
            n_merged[0] += 1
            sgs = [ar([NT]) for _ in range(2)]
            tmps = [ar([NT], BF16) for _ in range(2)]

            def ev(m, b):
                sg = sgs[m % 2]
                A(out=sg, in_=b[:], func=AF.Sigmoid)
                if first:
                    G("tensor_tensor", out=mrg[:, m, :], in0=sg, in1=ysrc[:, m, :], op=ALU.mult)
                else:
                    tp = tmps[m % 2]
                    G("tensor_tensor", out=tp, in0=sg, in1=ysrc[:, m, :], op=ALU.mult)
                    G("tensor_tensor", out=mrg[:, m, :], in0=mrg[:, m, :], in1=tp, op=ALU.add)
            proj(gtag, 1024, lambda k: hT[:, k, :], 8, ev)

        if cfg.xa:
            ar_reset()
            NM = NSEQ * MEM
            memin = ar([NSEQ * 2, D])
            memT = ar([8, NM])
            memTn = ar([8, NM], BF16)
            sqm = ar([8, NM], BF16)
            wkv = [ar([8, 512], BF16) for _ in range(2)]
            tk.dma("act", memin, mem_d.rearrange("s (t p) d -> p (s t) d", p=128))
            for i in range(2):
                tk.dma("pool", wkv[i], w_d["w_kv"][:, i * 512:(i + 1) * 512].rearrange("(k p) c -> p k c", p=128))
            for ct in range(8):
                b = bank()
                for s in range(NSEQ * 2):
                    TR(out=b[:, s * 128:(s + 1) * 128], in_=memin[:, s, ct * 128:(ct + 1) * 128], identity=ident_f[:])
                A(out=memT[:, ct, :], in_=b[:, 0:NM], func=AF.Copy)
            rstd = rms_stats(lambda ct: memT[:, ct, :], 8, NM, sqm, D)
            for ct in range(8):
                V("scalar_tensor_tensor", out=memTn[:, ct, :], in0=memT[:, ct, :], scalar=pc("gmem", ct),
                  in1=rstd, op0=ALU.mult, op1=ALU.mult)
            for h in range(4):
                b = bank()
                for k in range(8):
                    MM(out=b[:, 0:NM], lhsT=wkv[0][:, k, h * 128:(h + 1) * 128], rhs=memTn[:, k, :],
                       start=(k == 0), stop=(k == 7))
                A(out=KT[:, h, :], in_=b[:, 0:NM], func=AF.Copy)
            for mt in range(NSEQ * 2):
                b = bank()
                for k in range(8):
                    MM(out=b[:], lhsT=memTn[:, k, mt * 128:(mt + 1) * 128], rhs=wkv[1][:, k, :],
                       start=(k == 0), stop=(k == 7))
                A(out=Vm[:, mt, :], in_=b[:], func=AF.Copy)

        xin_next = [None]
        pending_tail = [None]
        for ti in range(NTILES):
            s_i = ti // TPS
            t0 = (ti % TPS) * NT
            first = (ti % TPS == 0)
            n_merged[0] = 0
            ar_reset()
            if ti == 0:
                xin = ar([NCH, D])
                tk.dma("act", xin, x_d[s_i, t0:t0 + NT, :].rearrange("(s p) d -> p s d", p=128))
            else:
                xin = xin_next[0]
            for ct in range(8):
                b = bank()
                for s in range(NCH):
                    TR(out=b[:, s * 128:(s + 1) * 128], in_=xin[:, s, ct * 128:(ct + 1) * 128], identity=ident_f[:])
                A(out=xT[:, ct, :], in_=b[:, 0:NT], func=AF.Copy)
            rmsnorm_to_hT("g1")
            if pending_tail[0] is not None:
                pending_tail[0]()
                pending_tail[0] = None

            if cfg.s5:
                ar_reset()
                uT = ar([8, NT], BF16)
                U2 = ar([8, 8, NB], BF16)
                o_st = ar_off[0]
                St = ar([64, NB])
                o_after = ar_off[0]
                ar_reset(o_st)
                Yg = ar([8, 8, NB], BF16)
                ar_reset(o_after)
                o_wS = ar_off[0]
                wS = ar([64, NB])
                wSb = ar([64, NB], BF16)
                sprev = ar([64, NB + 1], BF16)
                gT = uT
                tA = [ar([8, NB]) for _ in range(2)]
                tB = [ar([8, NB]) for _ in range(2)]
                o_xa = ar_off[0]

                def ev_u(m, b):
                    A(out=uT[:, m, :].rearrange("p (j b) -> p j b", j=8), in_=b[:].rearrange("p (b j) -> p j b", j=8),
                      func=AF.Copy)
                proj("u", 1024, lambda k: hT[:, k, :], 8, ev_u)
                if first:
                    V("memset", ap=carry[:], constant=0.0)
                uTv = uT.rearrange("p c (j b) -> p c b j", j=8)
                for g8 in range(8):
                    b = bank()
                    bv = b[:, 0:8 * NB].rearrange("p (c b) -> p c b", c=8)
                    for j in range(8):
                        MM(out=bv, lhsT=Z_b[:, g8, 128 - 16 * j:256 - 16 * j], rhs=uTv[:, :, :, j],
                           start=(j == 0), stop=(j == 7))
                    V("tensor_copy", out=U2[:, :, g8, :], in_=bv)
                for blk in range(4):
                    wv, _, _ = next_block("st")
                    wv4 = wv.rearrange("p (g s) c -> p g s c", s=2)
                    for half in range(2):
                        bx, by = bank(), bank()
                        for gi in range(8):
                            gl = half * 8 + gi
                            g = blk * 16 + gl
                            rhs = U2[:, g // 8, g % 8, :]
                            MM(out=bx[:, gi * NB:(gi + 1) * NB], lhsT=wv4[:, gl, 0, :], rhs=rhs, start=True, stop=True)
                            MM(out=by[:, gi * NB:(gi + 1) * NB], lhsT=wv4[:, gl, 1, :], rhs=rhs, start=True, stop=True)
                        gs = blk * 16 + half * 8
                        ta, tb_ = tA[half], tB[half]
                        V("tensor_tensor", out=ta, in0=bx[:, 0:8 * NB].rearrange("p (g b) -> p g b", g=8),
                          in1=cosT[:, gs:gs + 8, :], op=ALU.mult)
                        V("tensor_tensor", out=tb_, in0=by[:, 0:8 * NB].rearrange("p (g b) -> p g b", g=8),
                          in1=sinT[:, gs:gs + 8, :], op=ALU.mult)
                        G("tensor_tensor", out=St[:, gs:gs + 8, :], in0=ta, in1=tb_, op=ALU.add)
            if cfg.xa:
                o_x0 = o_xa if cfg.s5 else 0
                ar_reset(o_x0)
                qT = ar([4, NT], BF16)
                PT = [ar([2, NT], BF16) for _ in range(2)]
                oT = ar([4, NT], BF16)
                rden = [ar([NT]) for _ in range(2)]
                o_x3 = ar_off[0]
                ar_reset(o_x0)
                ytmp = ar([8, NT], BF16)
                ar_reset(o_x3)

                def ev_q(m, b):
                    A(out=qT[:, m, :], in_=b[:], func=AF.Copy)
                proj("q", 512, lambda k: hT[:, k, :], 8, ev_q)
                for h in range(4):
                    pt = PT[h % 2]
                    for mt in range(2):
                        b = bank()
                        MM(out=b[:], lhsT=KT[:, h, s_i * MEM + mt * 128:s_i * MEM + (mt + 1) * 128], rhs=qT[:, h, :],
                           start=True, stop=True)
                        A(out=pt[:, mt, :], in_=b[:], func=AF.Exp, scale=float(128 ** -0.5))
                    bd = bank()
                    for mt in range(2):
                        MM(out=bd[:], lhsT=ones_b, rhs=pt[:, mt, :], start=(mt == 0), stop=(mt == 1))
                    bo = bank()
                    for mt in range(2):
                        MM(out=bo[:], lhsT=Vm[:, s_i * 2 + mt, h * 128:(h + 1) * 128], rhs=pt[:, mt, :],
                           start=(mt == 0), stop=(mt == 1))
                    V("reciprocal", out=rden[h % 2], in_=bd[:])
                    V("tensor_tensor", out=oT[:, h, :], in0=bo[:], in1=rden[h % 2], op=ALU.mult)

                def ev_wc(m, b):
                    A(out=ytmp[:, m, :], in_=b[:], func=AF.Copy)
                proj("wc", 1024, lambda k: oT[:, k, :], 4, ev_wc)
                merge("gc", ytmp)

            if cfg.s5:
                for g in range(64):
                    V("tensor_tensor_scan", out=wS[:, g, :], data0=rbar[:, g:g + 1].to_broadcast([128, NB]),
                      data1=St[:, g, :], initial=carry[:, g:g + 1], op0=ALU.mult, op1=ALU.add)
                A(out=wSb, in_=wS, func=AF.Copy)
                A(out=sprev[:, :, 0], in_=carry[:], func=AF.Copy)
                wSbf = wSb.rearrange("p g b -> p (g b)")
                for q in range(8):
                    b = bank()
                    MM(out=b[:, 0:8 * NB], lhsT=perm_b, rhs=wSbf[:, q * 8 * NB:(q + 1) * 8 * NB], start=True, stop=True)
                    ta, tb_ = tA[q % 2], tB[q % 2]
                    V("tensor_tensor", out=ta, in0=b[:, 0:8 * NB].rearrange("p (g b) -> p g b", g=8),
                      in1=sinT[:, q * 8:(q + 1) * 8, :], op=ALU.mult)
                    G("tensor_tensor", out=tb_, in0=wS[:, q * 8:(q + 1) * 8, :], in1=cosT[:, q * 8:(q + 1) * 8, :], op=ALU.mult)
                    V("tensor_tensor", out=sprev[:, q * 8:(q + 1) * 8, 1:NB + 1], in0=tb_, in1=ta, op=ALU.subtract)
                    V("tensor_tensor", out=carry[:, q * 8:(q + 1) * 8], in0=tb_[:, :, NB - 1], in1=ta[:, :, NB - 1], op=ALU.subtract)
                for blk in range(4):
                    wv, _, _ = next_block("to")
                    wv4 = wv.rearrange("p (g s) c -> p g s c", s=2)
                    for half in range(2):
                        b = bank()
                        for gi in range(8):
                            gl = half * 8 + gi
                            g = blk * 16 + gl
                            MM(out=b[:, gi * NB:(gi + 1) * NB], lhsT=wv4[:, gl, 0, :], rhs=U2[:, g // 8, g % 8, :],
                               start=True, stop=False)
                            MM(out=b[:, gi * NB:(gi + 1) * NB], lhsT=wv4[:, gl, 1, :], rhs=sprev[:, g, 0:NB],
                               start=False, stop=True)
                        ct = (blk * 16 + half * 8) // 8
                        A(out=Yg[:, ct, :, :], in_=b[:, 0:8 * NB].rearrange("p (g b) -> p g b", g=8), func=AF.Gelu)
                gTv = gT.rearrange("p c (j b) -> p c b j", j=8)
                for t in range(8):
                    b = bank()
                    bv = b[:, 0:8 * NB].rearrange("p (c b) -> p c b", c=8)
                    for g8 in range(8):
                        MM(out=bv, lhsT=Z_b[:, t, 128 - 16 * g8:256 - 16 * g8], rhs=Yg[:, :, g8, :],
                           start=(g8 == 0), stop=(g8 == 7))
                    A(out=gTv[:, :, :, t], in_=bv, func=AF.Copy)

                ar_reset(o_wS)
                ytmp = ar([8, NT], BF16)

                def ev_av(m, b):
                    A(out=ytmp[:, m, :].rearrange("p (b j) -> p b j", j=8), in_=b[:].rearrange("p (j b) -> p b j", j=8),
                      func=AF.Copy)
                proj("av", 1024, lambda k: gT[:, k, :], 8, ev_av)
                sg2 = [ar([NT], BF16) for _ in range(2)]

                def ev_ag(m, b):
                    A(out=sg2[m % 2].rearrange("p (b j) -> p b j", j=8), in_=b[:].rearrange("p (j b) -> p b j", j=8),
                      func=AF.Sigmoid)
                    G("tensor_tensor", out=ytmp[:, m, :], in0=ytmp[:, m, :], in1=sg2[m % 2], op=ALU.mult)
                proj("ag", 1024, lambda k: gT[:, k, :], 8, ev_ag)
                merge("ga", ytmp)

            if cfg.ssd:
                ar_reset()
                sz = ar([16, NT], BF16)
                xbcs = ar([24, NT], BF16)
                o_ssd = ar_off[0]
                dtt = ar([NCH, 32]); dA = ar([NCH, 32]); dtd = ar([NCH, 32]); cdb2 = ar([2, 32])
                dAb = ar([NCH, 32], BF16); ndAb = ar([NCH, 32], BF16)
                raw = [ar([NT + 3], BF16) for _ in range(3)]
                ctm = [ar([NT]) for _ in range(3)]
                Gs = ar([4, 128])
                E4 = [ar([4, 128]) for _ in range(2)]
                EA4 = [ar([4, 128]) for _ in range(2)]
                Mall = ar([32, 128], BF16)
                Cdall = ar([32, 128], BF16)
                xdt = ar([2048], BF16)
                xdtd = ar([2048], BF16)
                Btok = ar([512], BF16)
                ytm = [ar([4, 128], BF16) for _ in range(2)]
                ytm2 = [ar([4, 128]) for _ in range(2)]
                wv, _, _ = next_block("dt")
                bdt = bank()
                for c in range(NCH):
                    for k in range(8):
                        MM(out=bdt[:, c * 32:(c + 1) * 32], lhsT=hT[:, k, c * 128:(c + 1) * 128], rhs=wv[:, k, :],
                           start=(k == 0), stop=(k == 7))
                V("tensor_tensor", out=dtt, in0=bdt[:, 0:NCH * 32].rearrange("p (c h) -> p c h", c=NCH),
                  in1=pc("dtb", 0, 32).unsqueeze(1).to_broadcast([128, NCH, 32]), op=ALU.add)
                A(out=dtt, in_=dtt, func=AF.Exp)
                A(out=dtt, in_=dtt, func=AF.Ln, bias=1.0)
                V("tensor_tensor", out=dA, in0=dtt, in1=Aneg[:].unsqueeze(1).to_broadcast([128, NCH, 32]), op=ALU.mult)
                V("tensor_copy", out=dAb, in_=dA)
                V("tensor_scalar", out=ndAb, in0=dA, scalar1=-1.0, scalar2=None, op0=ALU.mult)

                def ev_z(m, b):
                    A(out=sz[:, m, :], in_=b[:], func=AF.Silu)
                proj("z", 2048, lambda k: hT[:, k, :], 8, ev_z)
                if first:
                    G("memset", ap=mhalo[:], constant=0.0)

                pend = []

                def ev_xbc(m, b):
                    r = raw[m % 3]
                    tm = ctm[m % 3]
                    G("tensor_copy", out=r[:, 0:3], in_=mhalo[:, m, :])
                    A(out=r[:, 3:NT + 3], in_=b[:], func=AF.Copy)
                    G("tensor_copy", out=mhalo[:, m, :], in_=r[:, NT:NT + 3])
                    G("tensor_scalar", out=tm, in0=r[:, 0:NT], scalar1=pc("m2cw", m), scalar2=pc("m2cb", m),
                      op0=ALU.mult, op1=ALU.add)
                    for kk in range(1, 4):
                        V("scalar_tensor_tensor", out=tm, in0=r[:, kk:NT + kk], scalar=pc("m2cw", 24 * kk + m), in1=tm,
                          op0=ALU.mult, op1=ALU.add)
                    while pend:
                        pend.pop(0)()
                    pend.append(lambda m=m, tm=tm: A(out=xbcs[:, m, :], in_=tm, func=AF.Silu))
                proj("xbc", 3072, lambda k: hT[:, k, :], 8, ev_xbc)
                while pend:
                    pend.pop(0)()

                for c in range(NCH):
                    cs = slice(c * 128, (c + 1) * 128)
                    firstc = first and c == 0
                    cdb = cdb2[:, c % 2, :]
                    bD = bank()
                    MM(out=bD[:, 0:32], lhsT=L2_b, rhs=dAb[:, c, :], start=True, stop=True)
                    MM(out=bD[:, 32:64], lhsT=ones_b, rhs=dAb[:, c, :], start=True, stop=True)
                    A(out=dtd[:, c, :], in_=bD[:, 0:32], func=AF.Exp)
                    A(out=cdb, in_=bD[:, 32:64], func=AF.Exp)
                    V("tensor_tensor", out=dtd[:, c, :], in0=dtd[:, c, :], in1=dtt[:, c, :], op=ALU.mult)
                    bG = bank()
                    for g in range(4):
                        MM(out=bG[:, g * 128:(g + 1) * 128], lhsT=xbcs[:, 16 + g, cs], rhs=xbcs[:, 20 + g, cs],
                           start=True, stop=True)
                    A(out=Gs, in_=bG[:].rearrange("p (g l) -> p g l", g=4), func=AF.Copy)
                    for q in range(4):
                        bT = bank()
                        bTb = bT[:].bitcast(BF16)
                        for i in range(4):
                            TR(out=bTb[:, i * 128:(i + 1) * 128], in_=xbcs[:, q * 4 + i, cs], identity=ident_b)
                        src = bTb[:, 0:512].rearrange("p (h d) -> p h d", h=8)
                        V("tensor_tensor", out=xdt[:, q * 512:(q + 1) * 512].rearrange("p (h d) -> p h d", h=8), in0=src,
                          in1=dtt[:, c, q * 8:(q + 1) * 8].unsqueeze(2).to_broadcast([128, 8, 64]), op=ALU.mult)
                        V("tensor_tensor", out=xdtd[:, q * 512:(q + 1) * 512].rearrange("p (h d) -> p h d", h=8), in0=src,
                          in1=dtd[:, c, q * 8:(q + 1) * 8].unsqueeze(2).to_broadcast([128, 8, 64]), op=ALU.mult)
                    bT = bank()
                    bTb = bT[:].bitcast(BF16)
                    for g in range(4):
                        TR(out=bTb[:, g * 128:(g + 1) * 128], in_=xbcs[:, 16 + g, cs], identity=ident_b)
                    A(out=Btok, in_=bTb[:, 0:512], func=AF.Copy)

                    ybank = {}

                    def emit_y(hq):
                        q = hq // 2
                        if hq % 2 == 0:
                            ybank[q] = bank()
                        bY = ybank[q]
                        for pp in range(2):
                            pi_ = (hq % 2) * 2 + pp
                            pr = q * 4 + pi_
                            for hh in range(2):
                                h = 2 * pr + hh
                                o = bY[hh * 64:(hh + 1) * 64, pi_ * 128:(pi_ + 1) * 128]
                                MM(out=o, lhsT=xdt[:, h * 64:(h + 1) * 64], rhs=Mall[:, h, :], start=True, stop=firstc)
                                if not firstc:
                                    MM(out=o, lhsT=hsb[:, h * 64:(h + 1) * 64], rhs=Cdall[:, h, :], start=False, stop=True)
                        if hq % 2 == 1:
                            t1, t2 = ytm[q % 2], ytm2[q % 2]
                            for p4 in range(4):
                                A(out=t1[:, p4, :], in_=xbcs[:, q * 4 + p4, cs], func=AF.Copy, scale=pc("dcol", q * 4 + p4))
                            V("tensor_tensor", out=t2, in0=bY[:].rearrange("p (a l) -> p a l", a=4), in1=t1, op=ALU.add)
                            G("tensor_tensor", out=xbcs[:, q * 4:(q + 1) * 4, cs], in0=t2, in1=sz[:, q * 4:(q + 1) * 4, cs],
                              op=ALU.mult)

                    for hq in range(8):
                        h0 = hq * 4
                        g = hq // 2
                        bE, bA = bank(), bank()
                        MM(out=bE[:].rearrange("p (h l) -> p h l", h=4), lhsT=U_b,
                           rhs=ndAb[:, c, h0:h0 + 4].unsqueeze(2).to_broadcast([128, 4, 128]), start=True, stop=False)
                        MM(out=bE[:], lhsT=ident_b, rhs=nm4_b, start=False, stop=False)
                        for hh in range(4):
                            MM(out=bE[:, hh * 128:(hh + 1) * 128], lhsT=dAb[:, c, h0 + hh:h0 + hh + 1].to_broadcast([128, 128]),
                               rhs=U_b, start=False, stop=(hh == 3))
                        for hh in range(4):
                            MM(out=bA[:, hh * 128:(hh + 1) * 128], lhsT=dAb[:, c, h0 + hh:h0 + hh + 1].to_broadcast([128, 128]),
                               rhs=U_b, start=True, stop=True)
                        e4, ea4 = E4[hq % 2], EA4[hq % 2]
                        A(out=e4, in_=bE[:].rearrange("p (h l) -> p h l", h=4), func=AF.Exp)
                        A(out=ea4, in_=bA[:].rearrange("p (h l) -> p h l", h=4), func=AF.Exp)
                        V("tensor_tensor", out=Mall[:, h0:h0 + 4, :], in0=e4,
                          in1=Gs[:, g, :].unsqueeze(1).to_broadcast([128, 4, 128]), op=ALU.mult)
                        G("tensor_tensor", out=Cdall[:, h0:h0 + 4, :], in0=ea4,
                          in1=xbcs[:, 20 + g, cs].unsqueeze(1).to_broadcast([128, 4, 128]), op=ALU.mult)
                        if hq >= 3:
                            emit_y(hq - 3)
                    emit_y(5)
                    emit_y(6)
                    emit_y(7)

                    for g in range(4):
                        bS = bank()
                        MM(out=bS[:], lhsT=Btok[:, g * 128:(g + 1) * 128], rhs=xdtd[:, g * 512:(g + 1) * 512], start=True, stop=True)
                        hv = hs[:, g * 512:(g + 1) * 512]
                        if firstc:
                            V("tensor_copy", out=hv, in_=bS[:])
                        else:
                            G("tensor_tensor", out=hv.rearrange("p (h d) -> p h d", h=8), in0=hv.rearrange("p (h d) -> p h d", h=8),
                              in1=cdb[:, g * 8:(g + 1) * 8].unsqueeze(2).to_broadcast([128, 8, 64]), op=ALU.mult)
                            V("tensor_tensor", out=hv, in0=bS[:], in1=hv, op=ALU.add)
                        A(out=hsb[:, g * 512:(g + 1) * 512], in_=hv, func=AF.Copy)
                ar_reset(o_ssd)
                ytmp = ar([8, NT], BF16)
                sgb = ar([8, NT], BF16)
                for ct in range(16):
                    A(out=sz[:, ct, :], in_=xbcs[:, ct, :], func=AF.Square)

                def ev_gb(m, b):
                    A(out=sgb[:, m, :], in_=b[:], func=AF.Sigmoid)
                proj("gb", 1024, lambda k: hT[:, k, :], 8, ev_gb)
                bn = bank()
                for ct in range(16):
                    MM(out=bn[:], lhsT=ones_b, rhs=sz[:, ct, :], start=(ct == 0), stop=(ct == 15))
                rstd = sz[:, 0:2, :].rearrange("p a b -> p (a b)").bitcast(F32)
                A(out=rstd, in_=bn[:], func=AF.Sqrt, scale=1.0 / 2048, bias=pc("eps"))
                V("reciprocal", out=rstd, in_=rstd)
                for ct in range(16):
                    V("scalar_tensor_tensor", out=xbcs[:, ct, :], in0=xbcs[:, ct, :], scalar=pc("m2norm", ct), in1=rstd,
                      op0=ALU.mult, op1=ALU.mult)
                first_m = (n_merged[0] == 0)
                n_merged[0] += 1
                mtmp = [ar([NT], BF16) for _ in range(2)]

                def ev_wb(m, b):
                    A(out=ytmp[:, m, :], in_=b[:], func=AF.Copy)
                    if first_m:
                        G("tensor_tensor", out=mrg[:, m, :], in0=sgb[:, m, :], in1=ytmp[:, m, :], op=ALU.mult)
                    else:
                        G("tensor_tensor", out=mtmp[m % 2], in0=sgb[:, m, :], in1=ytmp[:, m, :], op=ALU.mult)
                        G("tensor_tensor", out=mrg[:, m, :], in0=mrg[:, m, :], in1=mtmp[m % 2], op=ALU.add)
                proj("wb", 1024, lambda k: xbcs[:, k, :], 16, ev_wb)

            if n_merged[0] > 0:
                def ev_wo(m, b):
                    V("tensor_tensor", out=xT[:, m, :], in0=b[:], in1=xT[:, m, :], op=ALU.add)
                proj("wo", 1024, lambda k: mrg[:, k, :], 8, ev_wo)

            if ti + 1 < NTILES:
                ar_reset(40 * 1024)
                xin_next[0] = ar([NCH, D])
                ns_i, nt0 = (ti + 1) // TPS, ((ti + 1) % TPS) * NT
                tk.dma("act", xin_next[0], x_d[ns_i, nt0:nt0 + NT, :].rearrange("(s p) d -> p s d", p=128))
            if cfg.ffn:
                ar_reset()
                rmsnorm_to_hT("g2")
                ar_reset()
                gact = ar([22, NT], BF16)
                raw = [ar([NT + 2], BF16) for _ in range(3)]
                ctm = [ar([NT]) for _ in range(3)]
                if first:
                    G("memset", ap=fhalo[:], constant=0.0)

                pend = []

                def ev_up(m, b):
                    r = raw[m % 3]
                    tm = ctm[m % 3]
                    G("tensor_copy", out=r[:, 0:2], in_=fhalo[:, m, :])
                    A(out=r[:, 2:NT + 2], in_=b[:], func=AF.Copy)
                    G("tensor_copy", out=fhalo[:, m, :], in_=r[:, NT:NT + 2])
                    G("tensor_scalar", out=tm, in0=r[:, 0:NT], scalar1=pc("fcw", m), scalar2=pc("fcb", m),
                      op0=ALU.mult, op1=ALU.add)
                    for kk in range(1, 3):
                        V("scalar_tensor_tensor", out=tm, in0=r[:, kk:NT + kk], scalar=pc("fcw", 44 * kk + m), in1=tm,
                          op0=ALU.mult, op1=ALU.add)
                    while pend:
                        pend.pop(0)()
                    if m < 22:
                        pend.append(lambda m=m, tm=tm: A(out=gact[:, m, :], in_=tm, func=AF.Silu))
                    else:
                        pend.append(lambda m=m, tm=tm: V("tensor_tensor", out=gact[:, m - 22, :], in0=gact[:, m - 22, :],
                                                         in1=tm, op=ALU.mult))
                proj("up", 2 * D_FF, lambda k: hT[:, k, :], 8, ev_up)
                while pend:
                    pend.pop(0)()

                def ev_dn(m, b):
                    V("tensor_tensor", out=xT[:, m, :], in0=b[:], in1=xT[:, m, :], op=ALU.add)
                proj("dn", 1024, lambda k: gact[:, k, :], 22, ev_dn)

            ar_reset(56 * 1024)
            sqf = ar([8, NT], BF16)
            xout = ar([NCH, D])
            xfin = ar([8, NT])
            rstd = rms_stats(lambda ct: xT[:, ct, :], 8, NT, sqf, D)
            for ct in range(8):
                V("scalar_tensor_tensor", out=xfin[:, ct, :], in0=xT[:, ct, :], scalar=pc("gf", ct), in1=rstd,
                  op0=ALU.mult, op1=ALU.mult)

            def final_tail(xout=xout, xfin=xfin, s_i=s_i, t0=t0):
                for s in range(NCH):
                    for half in range(2):
                        b = bank()
                        for c4 in range(4):
                            ct = half * 4 + c4
                            TR(out=b[:, c4 * 128:(c4 + 1) * 128], in_=xfin[:, ct, s * 128:(s + 1) * 128], identity=ident_f[:])
                        A(out=xout[:, s, half * 512:(half + 1) * 512], in_=b[:], func=AF.Copy)
                tk.dma("sp", out_d[s_i, t0:t0 + NT, :].rearrange("(s p) d -> p s d", p=128), xout, final=True)
            pending_tail[0] = final_tail

        pending_tail[0]()
        tk.finish()
    return nc


def _cols(v):
    v = np.asarray(v, np.float32).reshape(-1, 128)
    return np.ascontiguousarray(v.T)


def _const_mats():
    cm = np.zeros((128, NCM), np.float32)
    r = np.arange(128)
    cm[:, CM_ID:CM_ID + 128] = np.eye(128)
    cm[:, CM_U:CM_U + 128] = (r[:, None] <= r[None, :])
    cm[:, CM_NM:CM_NM + 128] = np.where(r[None, :] < r[:, None], -30000.0, 0.0)
    pm = np.zeros((128, 128), np.float32)
    for rp in range(64):
        pm[rp + 64, rp] = 1.0
        pm[rp, rp + 64] = -1.0
    cm[:, CM_PERM:CM_PERM + 128] = pm
    cm[:, CM_BM:CM_BM + 128] = ((r[None, :] // 16) >= (r[:, None] // 16))
    cm[:, CM_L2:CM_L2 + 128] = (r[:, None] > r[None, :])
    for a in range(8):
        z = np.zeros((128, 256), np.float32)
        for k in range(16):
            z[16 * a + k, 128 + k] = 1.0
        cm[:, CM_Z + 256 * a:CM_Z + 256 * (a + 1)] = z
    return cm


def make_inputs(cfg, inp, b0):
    pcv = np.zeros((128, NPC), np.float32)

    def put(name, arr, i=0):
        arr = np.asarray(arr, np.float32)
        pcv[:, PCO[name] + i:PCO[name] + i + arr.shape[1]] = arr

    put("g1", _cols(inp["norm_mix"][0]))
    put("g2", _cols(inp["norm_ffn"][0]))
    put("gf", _cols(inp["norm_final"]))
    put("gmem", _cols(inp["norm_mem"][0]))
    for k in range(3):
        put("fcw", _cols(inp["ffn_conv_w"][0][k]), 44 * k)
    put("fcb", _cols(inp["ffn_conv_b"][0]))
    for k in range(4):
        put("m2cw", _cols(inp["m2_conv_w"][0][k]), 24 * k)
    put("m2cb", _cols(inp["m2_conv_b"][0]))
    put("m2norm", _cols(inp["m2_norm"][0]))
    md = np.asarray(inp["m2_d"][0], np.float32)
    put("dcol", np.repeat(md.reshape(16, 2).T, 64, axis=0))
    put("dtb", np.tile(np.asarray(inp["m2_dt_bias"][0], np.float32)[None, :], (128, 1)))
    put("alog", np.tile(np.asarray(inp["m2_a_log"][0], np.float32)[None, :], (128, 1)))
    pcv[:, PCO["eps"]] = EPS
    put("lre", np.tile(np.asarray(inp["s5_lambda_re"][0], np.float32).T, (2, 1)))
    put("lim", np.tile(np.asarray(inp["s5_lambda_im"][0], np.float32).T, (2, 1)))
    put("ldt", np.tile(np.asarray(inp["s5_log_dt"][0], np.float32)[None, :], (128, 1)))
    sd = np.asarray(inp["s5_d"][0], np.float32).reshape(64, 16)
    put("s5d", np.tile(sd.T, (8, 1)))
    half = (np.arange(128) >= 64)
    pcv[:, PCO["ph1"]] = np.where(half, -np.pi / 2, 0.0)
    pcv[:, PCO["ph2"]] = np.where(half, np.pi, -np.pi / 2)
    pcv[:, PCO["psi"]] = np.where(half, np.pi / 2, 0.0)
    nv = [0, -1, -2, -3, -4, -5, -6, -7] + [7, 6, 5, 4, 3, 2, 1, 0] + [0, 1, 2, 3, 4, 5, 6, 7] + [1, 2, 3, 4, 5, 6, 7, 8]
    put("nv", np.tile(np.asarray(nv, np.float32)[None, :], (128, 1)))
    put("bmul", np.tile((8.0 * (np.arange(64, dtype=np.float32) + 1.0))[None, :], (128, 1)))
    s5p = np.zeros((128, 4, 1024), np.float32)
    bre = np.asarray(inp["s5_b_re"][0], np.float32)
    bim = np.asarray(inp["s5_b_im"][0], np.float32)
    cre = np.asarray(inp["s5_c_re"][0], np.float32)
    cim = np.asarray(inp["s5_c_im"][0], np.float32)
    s5p[:, 0] = np.tile(bre.transpose(1, 0, 2).reshape(64, 1024), (2, 1))
    s5p[:, 1] = np.tile(bim.transpose(1, 0, 2).reshape(64, 1024), (2, 1))
    s5p[:, 2] = np.tile(cre.transpose(2, 0, 1).reshape(64, 1024), (2, 1))
    s5p[:, 3] = np.tile(cim.transpose(2, 0, 1).reshape(64, 1024), (2, 1))
    m = {
        "x": np.ascontiguousarray(inp["x"][b0:b0 + cfg.nseq, :cfg.seq]),
        "mem": np.ascontiguousarray(inp["mem"][b0:b0 + cfg.nseq]),
        "pcols": pcv,
        "cmat": _const_mats(),
        "s5p": s5p,
    }
    for k in ("w_in", "w_a_val", "w_a_gate", "w_b", "w_kv", "w_c", "w_out", "w_up", "w_down"):
        m[k] = np.ascontiguousarray(inp[k][0])
    return m


_NC_CACHE = {}


def kernel(**inputs):
    cfg = Cfg()
    inp = {k: np.asarray(v) for k, v in inputs.items()}
    key = "full"
    if key not in _NC_CACHE:
        _NC_CACHE[key] = build(cfg)
    nc = _NC_CACHE[key]
    in_maps = [make_inputs(cfg, inp, 2 * c) for c in range(8)]
    res = run_bass_kernel_spmd(nc, in_maps, core_ids=list(range(8)))
    out = np.concatenate([r["out"] for r in res.results], axis=0)
    return out.astype(np.float32)
```
